# Optimizing a Trainium2 kernel written in Bass

```python
import math
import jax, jax.numpy as jnp
from jax import lax
import numpy as np

D_MODEL = 1024
BATCH = 4
SEQ = 4096
DEPTH = 2
DEC_BATCH = 16
DEC_SEQ = 2048
PAST_LEN = 128

MIX_WIDTH = D_MODEL
HG_WIDTH = MIX_WIDTH // 2
HG_DK = 128
HG_HEADS = HG_WIDTH // HG_DK
HG_DV = HG_WIDTH // HG_HEADS
HG_CHUNK = 64
DA_WIDTH = MIX_WIDTH - HG_WIDTH
DA_HEADS = 4
DA_HEAD_DIM = DA_WIDTH // DA_HEADS // 2
DA_QBLOCK = 128
ROPE_THETA = 10000.0
FFN_HIDDEN = -(-8 * D_MODEL // (3 * 256)) * 256
NORM_EPS = 1e-6
SUBLN_EPS = 1e-5
LOG_FLOOR = 1e-30
IN_COLS = 5 * HG_WIDTH + 3 * DA_WIDTH
IN_SPLITS = tuple(HG_WIDTH * i for i in range(1, 6)) + tuple(5 * HG_WIDTH + DA_WIDTH * i for i in range(1, 3))

kernel_name = "hybrid_hgrn2_diffattn_encoder"


def _rms_norm(x, w, eps=NORM_EPS):
    xf = x.astype(jnp.float32)
    y = xf * lax.rsqrt(jnp.mean(xf * xf, axis=-1, keepdims=True) + eps)
    return (y * w.astype(jnp.float32)).astype(x.dtype)


def _rotary(x):
    T, dh = x.shape[1], x.shape[-1]
    inv = 1.0 / (ROPE_THETA ** (jnp.arange(0, dh, 2, dtype=jnp.float32) / dh))
    ang = jnp.arange(T, dtype=jnp.float32)[:, None] * inv[None, :]
    cos = jnp.cos(ang)[None, :, None, :]
    sin = jnp.sin(ang)[None, :, None, :]
    xf = x.astype(jnp.float32)
    x1, x2 = xf[..., : dh // 2], xf[..., dh // 2:]
    out = jnp.concatenate([x1 * cos - x2 * sin, x2 * cos + x1 * sin], axis=-1)
    return out.astype(x.dtype)


def _hgrn2_direction(q, k, v, log_f):
    B, T, H, DK = q.shape
    DV = v.shape[-1]
    n = T // HG_CHUNK

    def chunks(a):
        return a.reshape(B, n, HG_CHUNK, H, a.shape[-1]).transpose(1, 0, 3, 2, 4)

    causal = jnp.tril(jnp.ones((HG_CHUNK, HG_CHUNK), dtype=bool))[:, :, None]

    def step(state, inp):
        qc, kc, vc, gc = inp
        b = jnp.cumsum(gc, axis=2)
        diff = b[:, :, :, None, :] - b[:, :, None, :, :]
        decay = jnp.where(causal, jnp.exp(jnp.minimum(diff, 0.0)), 0.0)
        scores = jnp.einsum('bhtk,bhsk,bhtsk->bhts', qc, kc, decay)
        out = (jnp.einsum('bhts,bhsv->bhtv', scores, vc)
               + jnp.einsum('bhtk,bhkv->bhtv', qc * jnp.exp(b), state))
        b_end = b[:, :, -1:, :]
        state = (state * jnp.exp(b_end)[:, :, 0, :, None]
                 + jnp.einsum('bhsk,bhsv->bhkv', kc * jnp.exp(b_end - b), vc))
        return state, out

    s0 = jnp.zeros((B, H, DK, DV), jnp.float32)
    _, out = lax.scan(step, s0, (chunks(q), chunks(k), chunks(v), chunks(log_f)))
    return out.transpose(1, 0, 3, 2, 4).reshape(B, T, H, DV)


def _hgrn2_mixer(hq, hf_fwd, hf_bwd, hi, hg, lb, gnorm_w):
    B, T, _ = hq.shape
    shape_k = (B, T, HG_HEADS, HG_DK)
    q = jax.nn.silu(hq.astype(jnp.float32)).reshape(shape_k)
    v = hi.astype(jnp.float32).reshape(B, T, HG_HEADS, HG_DV)

    def gates(z, lb_dir):
        z = z.astype(jnp.float32).reshape(shape_k)
        lb_dir = lb_dir.reshape(HG_HEADS, HG_DK)
        log_f = jnp.logaddexp(jnp.log(jnp.maximum(lb_dir, LOG_FLOOR)),
                              jnp.log1p(-lb_dir) + jax.nn.log_sigmoid(z))
        k = (1.0 - lb_dir) * jax.nn.sigmoid(-z)
        return k, log_f

    k_f, lf_f = gates(hf_fwd, lb[0])
    k_b, lf_b = gates(hf_bwd, lb[1])
    flip = lambda a: jnp.flip(a, axis=1)
    o_f = _hgrn2_direction(q, k_f, v, lf_f)
    o_b = flip(_hgrn2_direction(flip(q), flip(k_b), flip(v), flip(lf_b)))
    gate = jax.nn.silu(hg.astype(jnp.float32)).reshape(B, T, HG_HEADS, HG_DV)
    o = _rms_norm(o_f + o_b, gnorm_w) * gate
    return o.reshape(B, T, HG_WIDTH).astype(hq.dtype)


def _diff_attention(hq, hk, hv, lq1, lk1, lq2, lk2, subln_w, lambda_init):
    B, T, _ = hq.shape
    q = _rotary(hq.reshape(B, T, 2 * DA_HEADS, DA_HEAD_DIM))
    k = _rotary(hk.reshape(B, T, 2 * DA_HEADS, DA_HEAD_DIM))
    v = hv.reshape(B, T, DA_HEADS, 2 * DA_HEAD_DIM)
    f32 = jnp.float32
    lam = (jnp.exp(jnp.sum(lq1.astype(f32) * lk1.astype(f32)))
           - jnp.exp(jnp.sum(lq2.astype(f32) * lk2.astype(f32))) + lambda_init)
    scale = DA_HEAD_DIM ** -0.5
    nq = T // DA_QBLOCK
    q_blocks = q.reshape(B, nq, DA_QBLOCK, 2 * DA_HEADS, DA_HEAD_DIM).transpose(1, 0, 2, 3, 4)

    def attend(qb):
        s = jnp.einsum('bqhd,bkhd->bhqk', qb, k).astype(f32) * scale
        p = jax.nn.softmax(s, axis=-1).reshape(B, DA_HEADS, 2, DA_QBLOCK, T)
        a = p[:, :, 0] - lam * p[:, :, 1]
        return jnp.einsum('bhqk,bkhv->bqhv', a.astype(v.dtype), v)

    o = lax.map(attend, q_blocks)
    o = o.transpose(1, 0, 2, 3, 4).reshape(B, T, DA_HEADS, 2 * DA_HEAD_DIM)
    o = _rms_norm(o, subln_w, SUBLN_EPS) * (1.0 - lambda_init)
    return o.reshape(B, T, DA_WIDTH)


def _layer(x, c, layer_idx, lb, w_ada, b_ada, norm_pre_mix, norm_post_mix, norm_pre_ffn,
           norm_post_ffn, w_in, hg_gnorm, lq1, lk1, lq2, lk2, da_subln, w_out,
           w_ffn_gate, w_ffn_up, w_ffn_down):
    mod = jax.nn.silu(c) @ w_ada + b_ada
    shift_m, scale_m, gate_m, shift_f, scale_f, gate_f = [m[:, None, :] for m in jnp.split(mod, 6, axis=-1)]
    h = _rms_norm(x, norm_pre_mix) * (1.0 + scale_m) + shift_m
    hq, hff, hfb, hi, hg, dq, dk, dv = jnp.split(h @ w_in, IN_SPLITS, axis=-1)
    o_hg = _hgrn2_mixer(hq, hff, hfb, hi, hg, lb, hg_gnorm)
    lambda_init = 0.8 - 0.6 * math.exp(-0.3 * layer_idx)
    o_da = _diff_attention(dq, dk, dv, lq1, lk1, lq2, lk2, da_subln, lambda_init)
    mix = jnp.concatenate([o_hg, o_da], axis=-1) @ w_out
    x = x + gate_m * _rms_norm(mix, norm_post_mix)
    h = _rms_norm(x, norm_pre_ffn) * (1.0 + scale_f) + shift_f
    ffn = (jax.nn.silu(h @ w_ffn_gate) * (h @ w_ffn_up)) @ w_ffn_down
    x = x + gate_f * _rms_norm(ffn, norm_post_ffn)
    return x


def setup_inputs(seed: int = 0) -> dict:
    key = jax.random.key(seed)
    ks = jax.random.split(key, 24)
    f32 = jnp.float32
    nrm = lambda k, shape, s: s * jax.random.normal(k, shape, f32)
    gain = lambda k, shape: 1.0 + 0.05 * jax.random.normal(k, shape, f32)
    return {
        "x_prompt": nrm(ks[0], (BATCH, SEQ, D_MODEL), 1.0),
        "x_sample": nrm(ks[1], (DEC_BATCH, DEC_SEQ, D_MODEL), 1.0),
        "c_prompt": nrm(ks[2], (BATCH, D_MODEL), 1.0),
        "c_sample": nrm(ks[3], (DEC_BATCH, D_MODEL), 1.0),
        "w_ada": nrm(ks[4], (DEPTH, D_MODEL, 6 * D_MODEL), 0.5 * D_MODEL ** -0.5),
        "b_ada": nrm(ks[5], (DEPTH, 6 * D_MODEL), 0.02),
        "norm_pre_mix": gain(ks[6], (DEPTH, D_MODEL)),
        "norm_post_mix": gain(ks[7], (DEPTH, D_MODEL)),
        "norm_pre_ffn": gain(ks[8], (DEPTH, D_MODEL)),
        "norm_post_ffn": gain(ks[9], (DEPTH, D_MODEL)),
        "w_in": nrm(ks[10], (DEPTH, D_MODEL, IN_COLS), D_MODEL ** -0.5),
        "hg_lower_bounds": 1.0 + 0.1 * jax.random.normal(ks[11], (2, DEPTH, HG_WIDTH), f32),
        "hg_gnorm": gain(ks[12], (DEPTH, HG_DV)),
        "da_lambda_q1": nrm(ks[13], (DEPTH, DA_HEAD_DIM), 0.1),
        "da_lambda_k1": nrm(ks[14], (DEPTH, DA_HEAD_DIM), 0.1),
        "da_lambda_q2": nrm(ks[15], (DEPTH, DA_HEAD_DIM), 0.1),
        "da_lambda_k2": nrm(ks[16], (DEPTH, DA_HEAD_DIM), 0.1),
        "da_subln": gain(ks[17], (DEPTH, 2 * DA_HEAD_DIM)),
        "w_out": nrm(ks[18], (DEPTH, MIX_WIDTH, D_MODEL), MIX_WIDTH ** -0.5),
        "w_ffn_gate": nrm(ks[19], (DEPTH, D_MODEL, FFN_HIDDEN), D_MODEL ** -0.5),
        "w_ffn_up": nrm(ks[20], (DEPTH, D_MODEL, FFN_HIDDEN), D_MODEL ** -0.5),
        "w_ffn_down": nrm(ks[21], (DEPTH, FFN_HIDDEN, D_MODEL), FFN_HIDDEN ** -0.5),
    }


def reference(x_prompt, x_sample, c_prompt, c_sample, w_ada, b_ada, norm_pre_mix, norm_post_mix,
              norm_pre_ffn, norm_post_ffn, w_in, hg_lower_bounds, hg_gnorm, da_lambda_q1,
              da_lambda_k1, da_lambda_q2, da_lambda_k2, da_subln, w_out, w_ffn_gate, w_ffn_up,
              w_ffn_down):
    lb_soft = jax.nn.softmax(hg_lower_bounds.astype(jnp.float32), axis=1)
    lb_all = jnp.cumsum(lb_soft, axis=1) - lb_soft[:, :1]

    def trunk(x, c):
        for l in range(DEPTH):
            x = _layer(x, c, l, lb_all[:, l], w_ada[l], b_ada[l], norm_pre_mix[l], norm_post_mix[l],
                       norm_pre_ffn[l], norm_post_ffn[l], w_in[l], hg_gnorm[l], da_lambda_q1[l],
                       da_lambda_k1[l], da_lambda_q2[l], da_lambda_k2[l], da_subln[l], w_out[l],
                       w_ffn_gate[l], w_ffn_up[l], w_ffn_down[l])
        return x

    y_prompt = trunk(x_prompt, c_prompt)
    y_sample = trunk(x_sample, c_sample)
    return (y_prompt, y_sample)
```

```python
import numpy as np
import ml_dtypes
from contextlib import ExitStack
import concourse.bass as bass
import concourse.mybir as mybir
from concourse.bass_utils import run_bass_kernel_spmd

F32 = mybir.dt.float32
BF16 = mybir.dt.bfloat16
U32 = mybir.dt.uint32
AF = mybir.ActivationFunctionType
ALU = mybir.AluOpType

D = 1024
KC = 8
HID = 2816
JC = 22
DEPTH = 2
NCOL = 5120
NEG_BIG = -30000.0
NORM_EPS = 1e-6
SUBLN_EPS = 1e-5
SEM_LIMIT = 30000
import os
DEBUG = bool(int(os.environ.get("MK_DEBUG", "0")))
NLAYERS = int(os.environ.get("MK_LAYERS", str(DEPTH)))


class TK:
    __slots__ = ("ap", "w", "r", "dsem", "dcnt", "name", "dsem_sw")

    def __init__(self, ap, name):
        self.ap = ap
        self.w = None
        self.r = {}
        self.dsem = None
        self.dcnt = 0
        self.name = name


class KB:
    def __init__(self, nc, needed=None):
        self.nc = nc
        self.needed = needed
        self.rec = {k: set() for k in ("pe", "act", "dve", "pool", "sp")}
        self.idx = {k: 0 for k in ("pe", "act", "dve", "pool", "sp")}
        self.tokmap = {k: {} for k in ("pe", "act", "dve", "pool", "sp")}
        self.emitted = {k: [] for k in ("pe", "act", "dve", "pool", "sp")}
        self.seen_idx = {k: {} for k in ("pe", "act", "dve", "pool", "sp")}
        self.eng = {"pe": nc.tensor, "act": nc.scalar, "dve": nc.vector, "pool": nc.gpsimd, "sp": nc.sync}
        self.sem = {}
        self.cnt = {}
        self.seen = {k: {} for k in self.eng}
        self.nsem = 0
        for k in self.eng:
            self._newsem(k)
        self.uid = 0
        self.dma_toks = {}
        self.free_dsems = []
        self.free_dsems_sw = []
        self.dval = {}
        self.dsem_tks = []

    def _newsem(self, k):
        self.sem[k] = self.nc.alloc_semaphore(name=f"s_{k}_{self.nsem}")
        self.nsem += 1
        self.cnt[k] = 0

    def name(self, base):
        self.uid += 1
        return f"{base}_{self.uid}"

    def sb(self, es, base, shape, dtype):
        t = es.enter_context(self.nc.sbuf_tensor(self.name(base), list(shape), dtype))
        return TK(t.ap(), base)

    def ps(self, es, base, shape, dtype=F32):
        t = es.enter_context(self.nc.psum_tensor(self.name(base), list(shape), dtype))
        return TK(t.ap(), base)

    def dram(self, base, shape, dtype, kind="Internal"):
        t = self.nc.dram_tensor(base, list(shape), dtype, kind=kind)
        return TK(t.ap(), base)

    def _wait(self, e, toks):
        import bisect
        need = {}
        for tok in toks:
            if tok is None:
                continue
            if tok[0] == "E":
                _, src, i = tok
                if src == e and e == "pe":
                    continue
                if self.needed is None:
                    if self.seen_idx[e].get(src, 0) >= i:
                        continue
                    self.seen_idx[e][src] = i
                    self.rec[src].add(i)
                    continue
                lst = self.emitted[src]
                p = bisect.bisect_left(lst, i)
                sem, val = self.tokmap[src][lst[p]]
            else:
                sem, val, src = tok
            key = sem.num
            if self.seen[e].get(key, 0) >= val:
                continue
            if key not in need or need[key][1] < val:
                need[key] = (sem, val)
        for key, (sem, val) in need.items():
            self.eng[e].wait_ge(sem, val)
            self.seen[e][key] = val

    def _wait_all(self, e, toks):
        self._wait(e, toks)

    def _deps(self, reads, writes):
        toks = []
        for t in reads:
            toks.append(t.w)
        for t in writes:
            toks.append(t.w)
            toks.extend(t.r.values())
        return toks

    def op(self, e, fn, reads=(), writes=()):
        self._wait(e, self._deps(reads, writes))
        ins = fn(self.eng[e])
        self.idx[e] += 1
        i = self.idx[e]
        if self.needed is not None and i in self.needed[e]:
            if self.cnt[e] >= SEM_LIMIT:
                self._newsem(e)
            self.cnt[e] += 1
            ins.then_inc(self.sem[e], 1)
            self.tokmap[e][i] = (self.sem[e], self.cnt[e])
            self.emitted[e].append(i)
        tok = ("E", e, i)
        for t in reads:
            t.r[("E", e)] = tok
        for t in writes:
            t.w = tok
            t.r = {}
        return ins

    def dma(self, q, out_tk, out_ap, in_tk, in_ap, sbuf_side=None, **kw):
        self._wait(q, self._deps([in_tk], [out_tk]))
        st = sbuf_side if sbuf_side is not None else out_tk
        if st.dsem is None:
            pool_ = self.free_dsems_sw if q == "pool" else self.free_dsems
            st.dsem_sw = (q == "pool")
            if pool_:
                st.dsem = pool_.pop()
            else:
                st.dsem = self.nc.alloc_semaphore(name=f"d_{self.nsem}")
                self.nsem += 1
                self.dval[st.dsem.num] = 0
            self.dsem_tks.append(st)
        ins = self.eng[q].dma_start(out=out_ap, in_=in_ap, **kw)
        self.dval[st.dsem.num] += 16
        ins.then_inc(st.dsem, 16)
        tok = (st.dsem, self.dval[st.dsem.num], "dma")
        in_tk.r[("D", tok[0].num)] = tok
        out_tk.w = tok
        out_tk.r = {}
        self.dma_toks[tok[0].num] = tok
        return tok

    def barrier(self):
        toks = [("E", k, self.idx[k]) for k in self.eng if self.idx[k] > 0]
        toks += list(self.dma_toks.values())
        for e in self.eng:
            self._wait_all(e, toks)
        self.dma_toks = {}
        for t in self.dsem_tks:
            if self.dval[t.dsem.num] >= SEM_LIMIT:
                t.dsem = None
                continue
            (self.free_dsems_sw if getattr(t, "dsem_sw", False) else self.free_dsems).append(t.dsem)
            t.dsem = None
        self.dsem_tks = []


def build_program(TA, TB):
    nc1, kb1 = _build(TA, TB, None)
    needed = {k: v for k, v in kb1.rec.items()}
    nc2, kb2 = _build(TA, TB, needed)
    return nc2


def _build(TA, TB, needed):
    assert TA == 2 * TB and TB % 512 == 0
    nc = bass.Bass("TRN2", target_bir_lowering=False)
    kb = KB(nc, needed)
    op, dma = kb.op, kb.dma

    def din(name, shape, dt=F32):
        return kb.dram(name, shape, dt, kind="ExternalInput")

    xa = din("xa", [TA, D]); xb = din("xb", [TB, D])
    c3T = din("c3T", [128, KC, 4])
    w_ada = din("w_ada", [DEPTH, D, 6 * D]); b_adaT = din("b_adaT", [128, DEPTH, 6, KC])
    b_adaR = din("b_adaR", [4, DEPTH, 6 * D])
    npreT = din("npreT", [128, DEPTH, 2, KC])
    npostR = din("npostR", [4, DEPTH, 2, D])
    w_in = din("w_in", [DEPTH, D, NCOL])
    hlbT = din("hlbT", [128, 2, DEPTH, 4])
    gnT = din("gnT", [128, DEPTH]); swT = din("swT", [128, DEPTH])
    lamB = din("lamB", [128, 4, DEPTH, 64])
    w_out = din("w_out", [DEPTH, D, D])
    w_g = din("w_g", [DEPTH, D, HID]); w_u = din("w_u", [DEPTH, D, HID]); w_d = din("w_d", [DEPTH, HID, D])
    cosT = din("cosT", [128, TA]); sinT = din("sinT", [128, TA])
    flags = din("flags", [128, 2])
    ident_d = din("ident", [128, 128], BF16)
    onesf_d = din("onesf", [128, 128])
    onesb_d = din("onesb", [128, 128], BF16)
    sel_d = din("sel", [4, 3 * 128])
    rmask_d = din("rmask", [128, 2, 512])
    tmask_d = din("tmask", [128, 2, 128], U32)
    ya = kb.dram("ya", [TA, D], F32, kind="ExternalOutput")
    yb = kb.dram("yb", [TB, D], F32, kind="ExternalOutput")
    dk_ = "ExternalOutput" if DEBUG else "Internal"
    x1a = kb.dram("x1a", [TA, D], F32, kind=dk_); x1b = kb.dram("x1b", [TB, D], F32, kind=dk_)
    mixTa = kb.dram("mixTa", [KC, 128, TA], BF16, kind=dk_); mixTb = kb.dram("mixTb", [KC, 128, TB], BF16, kind=dk_)
    wo_s = kb.dram("wo_s", [128, KC, D], BF16)
    wg_s = kb.dram("wg_s", [JC, 128, KC, 128], BF16); wu_s = kb.dram("wu_s", [JC, 128, KC, 128], BF16)
    wd_s = kb.dram("wd_s", [JC, 128, D], BF16)

    with ExitStack() as g:
        ident = kb.sb(g, "ident", [128, 128], BF16)
        onesf = kb.sb(g, "onesf", [128, 128], F32)
        onesb = kb.sb(g, "onesb", [128, 128], BF16)
        sel = kb.sb(g, "sel", [4, 384], F32)
        rmask = kb.sb(g, "rmask", [128, 2, 512], F32)
        tmask = kb.sb(g, "tmask", [128, 2, 128], U32)
        flg = kb.sb(g, "flg", [128, 2], F32)
        zero1 = kb.sb(g, "zero1", [128, 1], F32)
        sc = kb.sb(g, "sc", [128, KC, 4], F32)
        modA = kb.sb(g, "modA", [128, 2, 3, KC], F32)
        modS = kb.sb(g, "modS", [128, 2, 3, KC], F32)
        Gbc = kb.sb(g, "Gbc", [128, 2, 3, D], F32)
        lbt = kb.sb(g, "lbt", [128, 2, DEPTH, 4], F32)
        omlt = kb.sb(g, "omlt", [128, 2, DEPTH, 4], F32)
        gnt = kb.sb(g, "gnt", [128, DEPTH], F32)
        swt = kb.sb(g, "swt", [128, DEPTH], F32)
        nlam = kb.sb(g, "nlam", [128, DEPTH], F32)
        for (t, d_) in ((ident, ident_d), (onesf, onesf_d), (onesb, onesb_d), (sel, sel_d), (rmask, rmask_d),
                        (tmask, tmask_d), (flg, flags), (lbt, hlbT), (gnt, gnT), (swt, swT)):
            dma("sp", t, t.ap, d_, d_.ap)
        op("dve", lambda e: e.memset(zero1.ap, 0.0), writes=[zero1])

        with ExitStack() as es:
            ct = kb.sb(es, "ct", [128, KC, 4], F32)
            t1 = kb.sb(es, "t1", [128, KC, 4], F32)
            dma("sp", ct, ct.ap, c3T, c3T.ap)
            op("act", lambda e: e.activation(out=t1.ap, in_=ct.ap, func=AF.Exp, scale=-1.0), [ct], [t1])
            op("dve", lambda e: e.tensor_scalar_add(out=t1.ap, in0=t1.ap, scalar1=1.0), [t1], [t1])
            op("dve", lambda e: e.reciprocal(out=t1.ap, in_=t1.ap), [t1], [t1])
            op("dve", lambda e: e.tensor_mul(out=sc.ap, in0=ct.ap, in1=t1.ap), [ct, t1], [sc])
            e0 = kb.sb(es, "e0", [128, 2, DEPTH, 4], F32)
            tot = kb.sb(es, "tot", [128, 2, 4], F32)
            op("act", lambda e: e.activation(out=e0.ap, in_=lbt.ap, func=AF.Exp), [lbt], [e0])
            op("dve", lambda e: e.tensor_add(out=tot.ap, in0=e0.ap[:, :, 0, :], in1=e0.ap[:, :, 1, :]), [e0], [tot])
            op("dve", lambda e: e.reciprocal(out=tot.ap, in_=tot.ap), [tot], [tot])
            for l in range(DEPTH):
                op("dve", lambda e, l=l: e.tensor_mul(out=e0.ap[:, :, l, :], in0=e0.ap[:, :, l, :], in1=tot.ap),
                   [e0, tot], [e0])
            op("dve", lambda e: e.tensor_sub(out=lbt.ap[:, :, 0, :], in0=e0.ap[:, :, 0, :], in1=e0.ap[:, :, 0, :]),
               [e0], [lbt])
            op("dve", lambda e: e.tensor_add(out=tot.ap, in0=e0.ap[:, :, 0, :], in1=e0.ap[:, :, 1, :]), [e0], [tot])
            op("dve", lambda e: e.tensor_sub(out=lbt.ap[:, :, 1, :], in0=tot.ap, in1=e0.ap[:, :, 0, :]),
               [tot, e0, lbt], [lbt])
            op("dve", lambda e: e.tensor_scalar(out=omlt.ap, in0=lbt.ap, scalar1=-1.0, scalar2=1.0,
                                                 op0=ALU.mult, op1=ALU.add), [lbt], [omlt])
            lm = kb.sb(es, "lm", [128, 4, DEPTH, 64], F32)
            pr = kb.sb(es, "pr", [128, 2, DEPTH, 64], F32)
            sm = kb.sb(es, "sm", [128, 2, DEPTH], F32)
            dma("sp", lm, lm.ap, lamB, lamB.ap)
            op("dve", lambda e: e.tensor_mul(out=pr.ap[:, 0], in0=lm.ap[:, 0], in1=lm.ap[:, 1]), [lm], [pr])
            op("dve", lambda e: e.tensor_mul(out=pr.ap[:, 1], in0=lm.ap[:, 2], in1=lm.ap[:, 3]), [lm, pr], [pr])
            op("dve", lambda e: e.reduce_sum(out=sm.ap, in_=pr.ap, axis=mybir.AxisListType.X), [pr], [sm])
            op("act", lambda e: e.activation(out=sm.ap, in_=sm.ap, func=AF.Exp), [sm], [sm])
            for l in range(DEPTH):
                lam_init = 0.8 - 0.6 * float(np.exp(-0.3 * l))
                op("dve", lambda e, l=l: e.tensor_sub(out=nlam.ap[:, l:l + 1], in0=sm.ap[:, 1, l:l + 1],
                                                      in1=sm.ap[:, 0, l:l + 1]), [sm, nlam], [nlam])
                op("dve", lambda e, l=l, li=lam_init: e.tensor_scalar_add(out=nlam.ap[:, l:l + 1],
                                                                          in0=nlam.ap[:, l:l + 1], scalar1=-li),
                   [nlam], [nlam])
                op("dve", lambda e, l=l, li=lam_init: e.tensor_scalar_mul(out=swt.ap[:, l:l + 1],
                                                                          in0=swt.ap[:, l:l + 1], scalar1=1.0 - li),
                   [swt], [swt])
            kb.barrier()

        segs = [(0, TA), (1, TB)]

        def rstd_from_ss(ss, n, eps):
            op("dve", lambda e: e.tensor_scalar(out=ss.ap, in0=ss.ap, scalar1=1.0 / n, scalar2=eps,
                                                 op0=ALU.mult, op1=ALU.add), [ss], [ss])
            op("act", lambda e: e.activation(out=ss.ap, in_=ss.ap, func=AF.Ln), [ss], [ss])
            op("act", lambda e: e.activation(out=ss.ap, in_=ss.ap, func=AF.Exp, scale=-0.5), [ss], [ss])

        for l in range(NLAYERS):
            xin = [xa, xb] if l == 0 else [x1a, x1b]
            xout = [x1a, x1b] if l == 0 else [ya, yb]
            if l == DEPTH - 1:
                xout = [ya, yb]
            mixTs = [mixTa, mixTb]
            with ExitStack() as es:
                wa = [kb.sb(es, "wa", [128, KC, D], F32) for _ in range(2)]
                badT = kb.sb(es, "badT", [128, 6, KC], F32)
                badR = kb.sb(es, "badR", [4, 6 * D], F32)
                npT = kb.sb(es, "npT", [128, 2, KC], F32)
                npR = kb.sb(es, "npR", [4, 2, D], F32)
                grow = kb.sb(es, "grow", [4, D], F32)
                pm = kb.ps(es, "pm", [128, KC, 4])
                pr_ = [kb.ps(es, "prw", [128, 512]) for _ in range(2)]
                dma("sp", badT, badT.ap, b_adaT, b_adaT.ap[:, l])
                dma("sp", badR, badR.ap, b_adaR, b_adaR.ap[:, l])
                dma("sp", npT, npT.ap, npreT, npreT.ap[:, l])
                dma("sp", npR, npR.ap, npostR, npostR.ap[:, l])
                for part in range(6):
                    w = wa[part % 2]
                    src = w_ada.ap[l, :, part * D:(part + 1) * D].rearrange("(k p) n -> p k n", p=128)
                    for k in range(KC):
                        dma("sp", w, w.ap[:, k, :], w_ada, src[:, k, :])
                    which = 0 if part < 3 else 1
                    kind = part % 3
                    if kind < 2:
                        for m in range(KC):
                            for k in range(KC):
                                op("pe", lambda e, m=m, k=k, w=w: e.matmul(
                                    pm.ap[:, m, :], lhsT=w.ap[:, k, m * 128:(m + 1) * 128], rhs=sc.ap[:, k, :],
                                    start=(k == 0), stop=(k == KC - 1)), [w, sc], [pm])
                        for s in range(3):
                            if kind == 0:
                                op("dve", lambda e, s=s, which=which, part=part: e.tensor_add(
                                    out=modS.ap[:, which, s, :], in0=pm.ap[:, :, s], in1=badT.ap[:, part, :]),
                                   [pm, badT, modS], [modS])
                            else:
                                op("dve", lambda e, s=s, which=which, part=part: e.tensor_add(
                                    out=modA.ap[:, which, s, :], in0=pm.ap[:, :, s], in1=badT.ap[:, part, :]),
                                   [pm, badT, modA], [modA])
                                op("dve", lambda e, s=s, which=which: e.tensor_scalar_add(
                                    out=modA.ap[:, which, s, :], in0=modA.ap[:, which, s, :], scalar1=1.0),
                                   [modA], [modA])
                                op("dve", lambda e, s=s, which=which: e.tensor_mul(
                                    out=modA.ap[:, which, s, :], in0=modA.ap[:, which, s, :],
                                    in1=npT.ap[:, which, :]), [modA, npT], [modA])
                    else:
                        for hf in range(2):
                            p_ = pr_[hf]
                            for k in range(KC):
                                op("pe", lambda e, k=k, hf=hf, p_=p_, w=w: e.matmul(
                                    p_.ap[0:4, :], lhsT=sc.ap[:, k, :], rhs=w.ap[:, k, hf * 512:(hf + 1) * 512],
                                    start=(k == 0), stop=(k == KC - 1)), [w, sc], [p_])
                            op("dve", lambda e, hf=hf, p_=p_, part=part: e.tensor_add(
                                out=grow.ap[:, hf * 512:(hf + 1) * 512], in0=p_.ap[0:4, :],
                                in1=badR.ap[:, part * D + hf * 512: part * D + (hf + 1) * 512]),
                               [p_, badR, grow], [grow])
                        op("dve", lambda e, which=which: e.tensor_mul(out=grow.ap, in0=grow.ap,
                                                                      in1=npR.ap[:, which, :]), [grow, npR], [grow])
                        for s in range(3):
                            for hf in range(2):
                                p_ = pr_[hf]
                                op("pe", lambda e, s=s, hf=hf, p_=p_: e.matmul(
                                    p_.ap, lhsT=sel.ap[:, s * 128:(s + 1) * 128],
                                    rhs=grow.ap[:, hf * 512:(hf + 1) * 512], start=True, stop=True),
                                   [sel, grow], [p_])
                                op("dve", lambda e, s=s, hf=hf, p_=p_, which=which: e.tensor_copy(
                                    out=Gbc.ap[:, which, s, hf * 512:(hf + 1) * 512], in_=p_.ap), [p_, Gbc], [Gbc])
                kb.barrier()
            cast_jobs = []
            for m in range(KC):
                cast_jobs.append((w_out, w_out.ap[l, :, m * 128:(m + 1) * 128].rearrange("(k p) n -> p k n", p=128),
                                  wo_s, wo_s.ap[:, :, m * 128:(m + 1) * 128]))
            for j in range(JC):
                cast_jobs.append((w_g, w_g.ap[l, :, j * 128:(j + 1) * 128].rearrange("(k p) n -> p k n", p=128),
                                  wg_s, wg_s.ap[j]))
                cast_jobs.append((w_u, w_u.ap[l, :, j * 128:(j + 1) * 128].rearrange("(k p) n -> p k n", p=128),
                                  wu_s, wu_s.ap[j]))
                cast_jobs.append((w_d, w_d.ap[l, j * 128:(j + 1) * 128, :].rearrange("p (k n) -> p k n", k=KC),
                                  wd_s, wd_s.ap[j].rearrange("p (k n) -> p k n", k=KC)))

            for (si, T) in segs:
                NT = T // 128
                NB = T // 512
                X = xin[si]
                XO = xout[si]
                MT = mixTs[si]

                def subseq(tok):
                    return (tok // TB) if si == 0 else 2

                with ExitStack() as segs_es:
                    hT = kb.sb(segs_es, "hT", [128, KC, T], BF16)
                    hT_blk = [TK(hT.ap, f"hTb{b}") for b in range(NB)]
                    with ExitStack() as es:
                        xt = [kb.sb(es, "xt", [128, D], F32) for _ in range(3)]
                        junk = kb.sb(es, "junk", [128, D], BF16)
                        xn = [kb.sb(es, "xn", [128, D], BF16) for _ in range(2)]
                        ss = [kb.sb(es, "ss", [128, 1], F32) for _ in range(2)]
                        ptr = [kb.ps(es, "ptr", [128, KC, 128], BF16) for _ in range(2)]
                        stg = [kb.sb(es, "stg", [128, KC, 128], F32) for _ in range(4)]
                        stb = [kb.sb(es, "stb", [128, KC, 128], BF16) for _ in range(4)]
                        ncast = [0]

                        def cast_some(k_):
                            for _ in range(k_):
                                if not cast_jobs:
                                    return
                                src_tk, src_ap, dst_tk, dst_ap = cast_jobs.pop(0)
                                i_ = ncast[0] % 4
                                ncast[0] += 1
                                dma("sp", stg[i_], stg[i_].ap, src_tk, src_ap)
                                op("pool", lambda e: e.tensor_copy(out=stb[i_].ap, in_=stg[i_].ap), [stg[i_]],
                                   [stb[i_]])
                                dma("pool", dst_tk, dst_ap, stb[i_], stb[i_].ap, sbuf_side=stb[i_])

                        per_tile = -(-len(cast_jobs) // NT) if si == 0 else 0
                        for it in range(NT):
                            cast_some(per_tile)
                            x_ = xt[it % 3]; s_ = ss[it % 2]; n_ = xn[it % 2]; p_ = ptr[it % 2]
                            hb = hT_blk[it // 4]
                            sq = subseq(it * 128)
                            dma("sp", x_, x_.ap, X, X.ap[it * 128:(it + 1) * 128, :])
                            op("act", lambda e: e.activation(out=junk.ap, in_=x_.ap, func=AF.Square,
                                                             accum_out=s_.ap), [x_], [junk, s_])
                            rstd_from_ss(s_, D, NORM_EPS)
                            op("dve", lambda e: e.tensor_scalar(out=n_.ap, in0=x_.ap, scalar1=s_.ap, scalar2=None,
                                                                 op0=ALU.mult), [x_, s_], [n_])
                            for k in range(KC):
                                op("pe", lambda e, k=k: e.transpose(p_.ap[:, k, :], n_.ap[:, k * 128:(k + 1) * 128],
                                                                    ident.ap), [n_, ident], [p_])
                            for k in range(KC):
                                op("act", lambda e, k=k: e.activation(
                                    out=hT.ap[:, k, it * 128:(it + 1) * 128], in_=p_.ap[:, k, :], func=AF.Identity,
                                    scale=modA.ap[:, 0, sq, k:k + 1], bias=modS.ap[:, 0, sq, k:k + 1]),
                                   [p_, modA, modS], [hb])
                        cast_some(len(cast_jobs))
                        kb.barrier()

                    def load_w(es_w, wst, wbf, col, n_i):
                        i = n_i % len(wst)
                        src = w_in.ap[l, :, col:col + 128].rearrange("(k p) n -> p k n", p=128)
                        dma("sp", wst[i], wst[i].ap, w_in, src)
                        wt = wbf[n_i % len(wbf)]
                        op("pool", lambda e: e.tensor_copy(out=wt.ap, in_=wst[i].ap), [wst[i]], [wt])
                        return wt

                    def proj_fm(pz, wt, b):
                        for k in range(KC):
                            op("pe", lambda e, k=k: e.matmul(pz.ap, lhsT=wt.ap[:, k, :],
                                                             rhs=hT.ap[:, k, b * 512:(b + 1) * 512],
                                                             start=(k == 0), stop=(k == KC - 1)),
                               [wt, hT_blk[b]], [pz])

                    def proj_tm(pz, wt, it):
                        for k in range(KC):
                            op("pe", lambda e, k=k: e.matmul(pz.ap, lhsT=hT.ap[:, k, it * 128:(it + 1) * 128],
                                                             rhs=wt.ap[:, k, :],
                                                             start=(k == 0), stop=(k == KC - 1)),
                               [wt, hT_blk[it // 4]], [pz])

                    with ExitStack() as es:
                        for h in range(4):
                            if h == 0:
                                NCH = T // 64
                                wst = [kb.sb(es, "wst", [128, KC, 128], F32) for _ in range(2)]
                                wbf = [kb.sb(es, "wbf", [128, KC, 128], BF16) for _ in range(5)]
                                qf = kb.sb(es, "qf", [128, T], F32)
                                qf_b = [TK(qf.ap, f"qfb{b}") for b in range(NB)]
                                gate = kb.sb(es, "gate", [128, T], BF16)
                                vtm = kb.sb(es, "vtm", [64, NCH, 128], BF16)
                                vtm_b = [TK(vtm.ap, f"vtb{b}") for b in range(NB)]
                                oacc = kb.sb(es, "oacc", [128, T], F32)
                                oacc_b = [TK(oacc.ap, f"oab{b}") for b in range(NB)]
                                tmp = [[kb.sb(es, "tmp", [128, 512], F32) for _ in range(4)] for _ in range(2)]
                                qt = [[kb.sb(es, "qt", [128, 512], BF16) for _ in range(2)] for _ in range(2)]
                                kt = [[kb.sb(es, "kt", [128, 512], BF16) for _ in range(2)] for _ in range(2)]
                                kh = [[kb.sb(es, "kh", [128, 512], BF16) for _ in range(2)] for _ in range(2)]
                                ktm = [[kb.sb(es, "ktm", [64, 128], BF16) for _ in range(3)] for _ in range(2)]
                                stm = [[kb.sb(es, "stm", [64, 64], BF16) for _ in range(3)] for _ in range(2)]
                                bref = [kb.sb(es, "bref", [128, 8], F32) for _ in range(2)]
                                d1 = [[kb.sb(es, "d1", [128, 8], F32) for _ in range(2)] for _ in range(2)]
                                d2 = [kb.sb(es, "d2", [128, 8], F32) for _ in range(2)]
                                d12 = [[kb.sb(es, "d12", [128, 8], F32) for _ in range(2)] for _ in range(2)]
                                S = [kb.sb(es, "S", [128, 128], F32) for _ in range(2)]
                                Sp = [[kb.sb(es, "Sp", [128, 128], BF16) for _ in range(2)] for _ in range(2)]
                                pz = [kb.ps(es, "pz", [128, 512]) for _ in range(2)]
                                pv = kb.ps(es, "pv", [128, 512])
                                pobank = [kb.ps(es, "pob", [128, 512]) for _ in range(2)]
                                po = [[TK(pobank[d_].ap[:, i * 64:(i + 1) * 64], f"po{d_}{i}") for i in range(8)]
                                      for d_ in range(2)]

                                def poslot(n_, dr):
                                    g_ = (n_ // 4) % 2
                                    k_ = (n_ % 4) if dr == 0 else 3 - (n_ % 4)
                                    return g_ * 4 + k_
                                msbank = [kb.ps(es, "msb", [128, 512]) for _ in range(2)]
                                pst = [[TK(msbank[d_].ap[0:64, i * 64:(i + 1) * 64], f"pst{d_}{i}") for i in range(2)]
                                       for d_ in range(2)]
                                pkv = [[TK(msbank[d_].ap[:, 128 + i * 128:256 + i * 128], f"pkv{d_}{i}")
                                        for i in range(2)] for d_ in range(2)]
                                pktb = kb.ps(es, "pktb", [128, 4, 128], BF16)
                                pkt = [[TK(pktb.ap[0:64, d_ * 2 + i, :], f"pkt{d_}{i}") for i in range(2)]
                                       for d_ in range(2)]
                            for d_ in range(2):
                                for t_ in stm[d_]:
                                    op("pool", lambda e, t_=t_: e.memset(t_.ap, 0.0), writes=[t_])
                                op("pool", lambda e, d_=d_: e.memset(S[d_].ap, 0.0), writes=[S[d_]])
                            op("pool", lambda e: e.memset(oacc.ap, 0.0), writes=[oacc] + oacc_b)
                            cols = [h * 128, 512 + h * 128, 1024 + h * 128, 1536 + h * 128, 2048 + h * 128]
                            wq = load_w(es, wst, wbf, cols[0], 0)
                            wff = load_w(es, wst, wbf, cols[1], 1)
                            wfb = load_w(es, wst, wbf, cols[2], 2)
                            wv = load_w(es, wst, wbf, cols[3], 3)
                            wg_ = load_w(es, wst, wbf, cols[4], 4)
                            cnt = [0]

                            def silu_ps(out_ap, out_tk, p_, tm):
                                a, b_, c_ = tm[0], tm[1], tm[2]
                                op("act", lambda e: e.activation(out=a.ap, in_=p_.ap, func=AF.Exp, scale=-1.0),
                                   [p_], [a])
                                op("act", lambda e: e.activation(out=b_.ap, in_=a.ap, func=AF.Ln, bias=1.0),
                                   [a], [b_])
                                op("act", lambda e: e.activation(out=c_.ap, in_=b_.ap, func=AF.Exp, scale=-1.0),
                                   [b_], [c_])
                                op("dve", lambda e: e.tensor_mul(out=out_ap, in0=p_.ap, in1=c_.ap), [p_, c_],
                                   [out_tk])

                            def gates(b, dr, wt, qi):
                                p_ = pz[cnt[0] % 2]; cnt[0] += 1
                                A, B, C, Dd = tmp[dr]
                                lb_ap = lbt.ap[:, dr, l, h:h + 1]
                                oml_ap = omlt.ap[:, dr, l, h:h + 1]
                                q_, k_, kh_ = qt[dr][qi], kt[dr][qi], kh[dr][qi]
                                D1, D12, D2, br = d1[dr][qi], d12[dr][qi], d2[dr], bref[dr]
                                proj_fm(p_, wt, b)
                                op("act", lambda e: e.activation(out=A.ap, in_=p_.ap, func=AF.Exp, scale=-1.0),
                                   [p_], [A])
                                op("act", lambda e: e.activation(out=B.ap, in_=A.ap, func=AF.Ln, scale=lb_ap,
                                                                 bias=1.0), [A, lbt], [B])
                                op("act", lambda e: e.activation(out=C.ap, in_=A.ap, func=AF.Ln, bias=1.0),
                                   [A], [C])
                                op("act", lambda e: e.activation(out=Dd.ap, in_=C.ap, func=AF.Exp, scale=-1.0),
                                   [C], [Dd])
                                yield
                                op("pool", lambda e: e.tensor_sub(out=B.ap, in0=B.ap, in1=C.ap), [B, C], [B])
                                op("dve", lambda e: e.scalar_tensor_tensor(out=A.ap, in0=A.ap, scalar=oml_ap,
                                                                           in1=Dd.ap, op0=ALU.mult, op1=ALU.mult),
                                   [A, Dd, omlt], [A])
                                if dr == 0:
                                    op("dve", lambda e: e.tensor_tensor_scan(out=C.ap, data0=rmask.ap[:, 0, :],
                                                                             data1=B.ap, initial=0.0,
                                                                             op0=ALU.mult, op1=ALU.add),
                                       [rmask, B], [C])
                                    ri, ei = 31, 63
                                else:
                                    op("dve", lambda e: e.tensor_tensor_scan(out=C.ap[:, ::-1],
                                                                             data0=rmask.ap[:, 1, ::-1],
                                                                             data1=B.ap[:, ::-1], initial=0.0,
                                                                             op0=ALU.mult, op1=ALU.add),
                                       [rmask, B], [C])
                                    ri, ei = 32, 0
                                yield
                                c3 = C.ap.rearrange("p (c t) -> p c t", t=64)
                                op("dve", lambda e: e.tensor_copy(out=br.ap, in_=c3[:, :, ri]), [C], [br])
                                op("dve", lambda e: e.tensor_sub(out=D2.ap, in0=c3[:, :, ei], in1=br.ap),
                                   [C, br], [D2])
                                op("act", lambda e: e.activation(out=D12.ap, in_=c3[:, :, ei], func=AF.Exp),
                                   [C], [D12])
                                op("act", lambda e: e.activation(out=D1.ap, in_=br.ap, func=AF.Exp), [br], [D1])
                                op("act", lambda e: e.activation(out=D2.ap, in_=D2.ap, func=AF.Exp), [D2], [D2])
                                yield
                                op("dve", lambda e: e.tensor_sub(out=c3, in0=c3,
                                                                 in1=br.ap.unsqueeze(2).to_broadcast([128, 8, 64])),
                                   [C, br], [C])
                                op("act", lambda e: e.activation(out=Dd.ap, in_=C.ap, func=AF.Exp), [C], [Dd])
                                op("act", lambda e: e.activation(out=B.ap, in_=C.ap, func=AF.Exp, scale=-1.0),
                                   [C], [B])
                                yield
                                op("dve", lambda e: e.tensor_mul(out=q_.ap, in0=qf.ap[:, b * 512:(b + 1) * 512],
                                                                 in1=Dd.ap), [qf_b[b], Dd], [q_])
                                op("pool", lambda e: e.tensor_mul(out=k_.ap, in0=A.ap, in1=B.ap), [A, B], [k_])
                                op("pool", lambda e: e.tensor_mul(
                                    out=kh_.ap.rearrange("p (c t) -> p c t", t=64),
                                    in0=k_.ap.rearrange("p (c t) -> p c t", t=64),
                                    in1=D2.ap.unsqueeze(2).to_broadcast([128, 8, 64])), [k_, D2], [kh_])

                            def cparams(i, j, dr):
                                b = i if dr == 0 else NB - 1 - i
                                ch = j if dr == 0 else 7 - j
                                n_ = i * 8 + j
                                return b, ch, n_, i % 2

                            def stageA(i, j, dr):
                                b, ch, n_, qi = cparams(i, j, dr)
                                o0 = ch * 64
                                kh_ = kh[dr][qi]
                                pk_ = pkt[dr][n_ % 2]; km = ktm[dr][n_ % 3]
                                op("pe", lambda e: e.transpose(pk_.ap, kh_.ap[:, o0:o0 + 64], ident.ap),
                                   [kh_, ident], [pk_])
                                op("act", lambda e: e.copy(out=km.ap, in_=pk_.ap), [pk_], [km])

                            def stageB(i, j, dr):
                                b, ch, n_, qi = cparams(i, j, dr)
                                gch = b * 8 + ch
                                o0 = ch * 64
                                q_, k_ = qt[dr][qi], kt[dr][qi]
                                km = ktm[dr][n_ % 3]; sm_ = stm[dr][n_ % 3]
                                ps_ = pst[dr][n_ % 2]; pkv_ = pkv[dr][n_ % 2]
                                vt_ = vtm_b[b]
                                op("pe", lambda e: e.matmul(pkv_.ap, lhsT=km.ap, rhs=vtm.ap[:, gch, :], start=True,
                                                            stop=True), [km, vt_], [pkv_])
                                if dr == 0:
                                    ra = (slice(0, 64), slice(32, 64)); rb = (slice(0, 32), slice(0, 32))
                                else:
                                    ra = (slice(0, 64), slice(0, 32)); rb = (slice(32, 64), slice(32, 64))
                                for (rs_, cs_) in (ra, rb):
                                    op("pe", lambda e, rs_=rs_, cs_=cs_: e.matmul(
                                        ps_.ap[rs_, cs_], lhsT=k_.ap[:, o0 + rs_.start:o0 + rs_.stop],
                                        rhs=q_.ap[:, o0 + cs_.start:o0 + cs_.stop], start=True, stop=True),
                                       [k_, q_], [ps_])
                                op("dve", lambda e: e.copy_predicated(out=sm_.ap, mask=tmask.ap[0:64, dr, 0:64],
                                                                      data=ps_.ap), [ps_, tmask, sm_], [sm_])

                            def stageC1(i, j, dr):
                                b, ch, n_, qi = cparams(i, j, dr)
                                gch = b * 8 + ch
                                sp_ = Sp[dr][n_ % 2]; S_ = S[dr]
                                if si == 0 and ((dr == 0 and gch == (TB // 64)) or
                                                (dr == 1 and gch == (TB // 64) - 1)):
                                    op("dve", lambda e: e.tensor_scalar(out=S_.ap, in0=S_.ap,
                                                                         scalar1=flg.ap[:, 1:2], scalar2=None,
                                                                         op0=ALU.mult), [S_, flg], [S_])
                                op("act", lambda e: e.activation(out=sp_.ap, in_=S_.ap, func=AF.Copy,
                                                                 scale=d1[dr][qi].ap[:, ch:ch + 1]),
                                   [S_, d1[dr][qi]], [sp_])

                            def stageC2(i, j, dr):
                                b, ch, n_, qi = cparams(i, j, dr)
                                gch = b * 8 + ch
                                o0 = ch * 64
                                q_ = qt[dr][qi]
                                sm_ = stm[dr][n_ % 3]; po_ = po[dr][poslot(n_, dr)]; vt_ = vtm_b[b]
                                op("pe", lambda e: e.matmul(po_.ap, lhsT=vtm.ap[:, gch, :], rhs=sm_.ap, start=True,
                                                            stop=False), [vt_, sm_], [po_])

                            def stageC3(i, j, dr):
                                b, ch, n_, qi = cparams(i, j, dr)
                                gch = b * 8 + ch
                                o0 = ch * 64
                                q_ = qt[dr][qi]
                                po_ = po[dr][poslot(n_, dr)]; sp_ = Sp[dr][n_ % 2]; S_ = S[dr]; pkv_ = pkv[dr][n_ % 2]
                                op("pe", lambda e: e.matmul(po_.ap, lhsT=sp_.ap, rhs=q_.ap[:, o0:o0 + 64],
                                                            start=False, stop=True), [sp_, q_], [po_])
                                op("dve", lambda e: e.scalar_tensor_tensor(out=S_.ap, in0=S_.ap,
                                                                           scalar=d12[dr][qi].ap[:, ch:ch + 1],
                                                                           in1=pkv_.ap, op0=ALU.mult, op1=ALU.add),
                                   [S_, d12[dr][qi], pkv_], [S_])
                                if n_ % 4 == 3:
                                    g_ = (n_ // 4) % 2
                                    t0 = (gch - 3) * 64 if dr == 0 else gch * 64
                                    ob = oacc_b[b]
                                    grp = po[dr][g_ * 4:g_ * 4 + 4]
                                    op("dve", lambda e: e.tensor_add(
                                        out=oacc.ap[:, t0:t0 + 256], in0=pobank[dr].ap[:, g_ * 256:(g_ + 1) * 256],
                                        in1=oacc.ap[:, t0:t0 + 256]), grp + [ob], [ob])

                            for b in range(NB):
                                p_ = pz[cnt[0] % 2]; cnt[0] += 1
                                proj_fm(p_, wq, b)
                                silu_ps(qf.ap[:, b * 512:(b + 1) * 512], qf_b[b], p_, tmp[0])
                                p_ = pz[cnt[0] % 2]; cnt[0] += 1
                                proj_fm(p_, wg_, b)
                                silu_ps(gate.ap[:, b * 512:(b + 1) * 512], gate, p_, tmp[1])
                                for c4 in range(2):
                                    for cq in range(4):
                                        cch = b * 8 + c4 * 4 + cq
                                        for k in range(KC):
                                            op("pe", lambda e, k=k: e.matmul(
                                                pv.ap[0:64, cq * 128:(cq + 1) * 128],
                                                lhsT=hT.ap[:, k, cch * 64:(cch + 1) * 64], rhs=wv.ap[:, k, :],
                                                start=(k == 0), stop=(k == KC - 1)), [wv, hT_blk[b]], [pv])
                                    c0_ = b * 8 + c4 * 4
                                    op("act", lambda e: e.copy(
                                        out=vtm.ap[:, c0_:c0_ + 4, :],
                                        in_=pv.ap[0:64, :].rearrange("p (c n) -> p c n", n=128)), [pv], [vtm_b[b]])
                            for d_ in range(2):
                                for t_ in pst[d_]:
                                    op("dve", lambda e, t_=t_: e.memset(t_.ap, 0.0), writes=[t_])
                            steps = [(i, j) for i in range(NB) for j in range(8)]
                            NS = len(steps)
                            for n in range(NS + 2):
                                if n == 0:
                                    for _ in gates(0, 0, wff, 0):
                                        pass
                                    for _ in gates(NB - 1, 1, wfb, 0):
                                        pass
                                    ggen = []
                                if n < NS and steps[n][1] == 1 and steps[n][0] + 1 < NB:
                                    i = steps[n][0] + 1
                                    ggen = [gates(i, 0, wff, i % 2), gates(NB - 1 - i, 1, wfb, i % 2)]
                                if n < NS and 1 <= steps[n][1] <= 5:
                                    for gg in ggen:
                                        next(gg, None)
                                if n < NS and steps[n][1] == 6:
                                    for gg in ggen:
                                        for _ in gg:
                                            pass
                                    ggen = []
                                if n - 2 >= 0:
                                    for dr in range(2):
                                        stageC1(*steps[n - 2], dr)
                                if n - 1 >= 0 and n - 1 < NS:
                                    for dr in range(2):
                                        stageB(*steps[n - 1], dr)
                                if n - 2 >= 0:
                                    for dr in range(2):
                                        stageC2(*steps[n - 2], dr)
                                if n < NS:
                                    for dr in range(2):
                                        stageA(*steps[n], dr)
                                if n - 2 >= 0:
                                    for dr in range(2):
                                        stageC3(*steps[n - 2], dr)
                            with ExitStack() as es2:
                                sqt = [tmp[0][0], tmp[1][0]]
                                fin = [qt[0][0], qt[0][1]]
                                for b in range(NB):
                                    ob = oacc_b[b]
                                    sq_ = sqt[b % 2]
                                    osl = oacc.ap[:, b * 512:(b + 1) * 512]
                                    op("act", lambda e: e.activation(out=sq_.ap, in_=osl, func=AF.Square), [ob],
                                       [sq_])
                                    p_ = pz[cnt[0] % 2]; cnt[0] += 1
                                    op("pe", lambda e: e.matmul(p_.ap, lhsT=onesf.ap, rhs=sq_.ap, start=True,
                                                                stop=True), [onesf, sq_], [p_])
                                    op("act", lambda e: e.activation(out=sq_.ap, in_=p_.ap, func=AF.Ln,
                                                                     scale=1.0 / 128, bias=NORM_EPS), [p_], [sq_])
                                    op("act", lambda e: e.activation(out=sq_.ap, in_=sq_.ap, func=AF.Exp,
                                                                     scale=-0.5), [sq_], [sq_])
                                    op("dve", lambda e: e.scalar_tensor_tensor(out=sq_.ap, in0=osl,
                                                                               scalar=gnt.ap[:, l:l + 1], in1=sq_.ap,
                                                                               op0=ALU.mult, op1=ALU.mult),
                                       [ob, gnt, sq_], [sq_])
                                    f_ = fin[b % 2]
                                    op("pool", lambda e: e.tensor_mul(out=f_.ap, in0=sq_.ap,
                                                                      in1=gate.ap[:, b * 512:(b + 1) * 512]),
                                       [sq_, gate], [f_])
                                    dma("pool", MT, MT.ap[h, :, b * 512:(b + 1) * 512], f_, f_.ap, sbuf_side=f_)

                        kb.barrier()
                    with ExitStack() as es:
                        wst = [kb.sb(es, "wst", [128, KC, 128], F32) for _ in range(2)]
                        wbf = [kb.sb(es, "wbf", [128, KC, 128], BF16) for _ in range(5)]
                        QT0 = kb.sb(es, "QT0", [128, T], BF16)
                        QT1 = kb.sb(es, "QT1", [128, T], BF16)
                        QTs = [QT0, QT1]
                        KT = kb.sb(es, "KT", [128, T], BF16)
                        op("pool", lambda e: e.memset(QT0.ap, 0.0), writes=[QT0])
                        op("pool", lambda e: e.memset(QT1.ap, 0.0), writes=[QT1])
                        Vt = kb.sb(es, "Vt", [128, NT, 128], BF16)
                        cs = kb.sb(es, "cs", [128, T], F32)
                        sn = kb.sb(es, "sn", [128, T], F32)
                        dma("sp", cs, cs.ap, cosT, cosT.ap[:, 0:T])
                        dma("sp", sn, sn.ap, sinT, sinT.ap[:, 0:T])

                        def load_d(hh):
                            return (load_w(es, wst, wbf, 2560 + hh * 128, 0), load_w(es, wst, wbf, 4096 + hh * 128, 1),
                                    load_w(es, wst, wbf, 3072 + hh * 128, 2), load_w(es, wst, wbf, 4608 + hh * 128, 3),
                                    load_w(es, wst, wbf, 3584 + hh * 128, 4))

                        Wd = load_d(0)
                        for h in range(4):
                            wq_, wqs, wk_, wks, wv_ = Wd
                            with ExitStack() as es2:
                                pz = [kb.ps(es2, "pz", [128, 512]) for _ in range(4)]
                                pv = kb.ps(es2, "pv", [128, 128])
                                ta = [kb.sb(es2, "ta", [128, 512], F32) for _ in range(2)]
                                tb = [kb.sb(es2, "tb", [128, 512], F32) for _ in range(2)]
                                n2 = 0
                                for b in range(NB):
                                    for (wa_, ws_, dst) in ((wq_, wqs, None), (wk_, wks, KT)):
                                        pa = pz[(2 * n2) % 4]; pb = pz[(2 * n2 + 1) % 4]
                                        t1_ = ta[n2 % 2]; t2_ = tb[n2 % 2]; n2 += 1
                                        proj_fm(pa, wa_, b)
                                        proj_fm(pb, ws_, b)
                                        op("dve", lambda e: e.tensor_mul(out=t1_.ap, in0=pa.ap,
                                                                         in1=cs.ap[:, b * 512:(b + 1) * 512]),
                                           [pa, cs], [t1_])
                                        op("dve", lambda e: e.tensor_mul(out=t2_.ap, in0=pb.ap,
                                                                         in1=sn.ap[:, b * 512:(b + 1) * 512]),
                                           [pb, sn], [t2_])
                                        if dst is None:
                                            for c_ in range(2):
                                                rs_ = slice(c_ * 64, (c_ + 1) * 64)
                                                op("pool", lambda e, c_=c_, rs_=rs_: e.tensor_add(
                                                    out=QTs[c_].ap[rs_, b * 512:(b + 1) * 512],
                                                    in0=t1_.ap[rs_, :], in1=t2_.ap[rs_, :]), [t1_, t2_], [QTs[c_]])
                                        else:
                                            op("pool", lambda e: e.tensor_add(
                                                out=dst.ap[:, b * 512:(b + 1) * 512],
                                                in0=t1_.ap, in1=t2_.ap), [t1_, t2_], [dst])
                                    for it in range(b * 4, b * 4 + 4):
                                        proj_tm(pv, wv_, it)
                                        op("act", lambda e, it=it: e.copy(out=Vt.ap[:, it, :], in_=pv.ap), [pv],
                                           [Vt])
                                kb.barrier()
                            if h + 1 < 4:
                                Wd = load_d(h + 1)
                            with ExitStack() as es2:
                                pS = [kb.ps(es2, "pS", [128, 512]) for _ in range(3)]
                                pO = [kb.ps(es2, "pO", [128, 512]) for _ in range(2)]
                                pZ = [kb.ps(es2, "pZ", [128, 512]) for _ in range(2)]
                                pN = kb.ps(es2, "pN", [128, 512])
                                Pm = [kb.sb(es2, "Pm", [128, 512], BF16) for _ in range(5)]
                                r1 = kb.sb(es2, "r1", [128, 512], F32)
                                r2 = kb.sb(es2, "r2", [128, 512], F32)
                                oo = [kb.sb(es2, "oo", [128, 512], F32) for _ in range(2)]
                                sq2 = [kb.sb(es2, "sq2", [128, 512], F32) for _ in range(2)]
                                fin = [kb.sb(es2, "fin", [128, 512], BF16) for _ in range(2)]
                                LAG = 2
                                tiles = [(qb, kt_, c) for qb in range(NB) for kt_ in range(NT) for c in range(2)]

                                def stage1(n):
                                    qb, kt_, c = tiles[n]
                                    qsl = slice(qb * 512, (qb + 1) * 512)
                                    ksl = slice(kt_ * 128, (kt_ + 1) * 128)
                                    cross = (si == 0) and ((qb * 512) // TB != (kt_ * 128) // TB)
                                    bias_ap = flg.ap[:, 0:1] if cross else zero1.ap
                                    ps_ = pS[n % 3]; pm_ = Pm[n % 5]
                                    rs = slice(c * 64, (c + 1) * 64)
                                    op("pe", lambda e: e.matmul(ps_.ap, lhsT=KT.ap[:, ksl], rhs=QTs[c].ap[:, qsl],
                                                                start=True, stop=True), [KT, QTs[c]], [ps_])
                                    op("act", lambda e: e.activation(out=pm_.ap, in_=ps_.ap, func=AF.Exp,
                                                                     scale=0.125, bias=bias_ap),
                                       [ps_, flg, zero1], [pm_])

                                def stage2(n):
                                    qb, kt_, c = tiles[n]
                                    pm_ = Pm[n % 5]
                                    op("pe", lambda e: e.matmul(pO[c].ap, lhsT=Vt.ap[:, kt_, :], rhs=pm_.ap,
                                                                start=(kt_ == 0), stop=(kt_ == NT - 1)),
                                       [Vt, pm_], [pO[c]])
                                    op("pe", lambda e: e.matmul(pZ[c].ap, lhsT=onesb.ap, rhs=pm_.ap,
                                                                start=(kt_ == 0), stop=(kt_ == NT - 1)),
                                       [onesb, pm_], [pZ[c]])

                                def fin1(qb):
                                    o_ = oo[qb % 2]; s2_ = sq2[qb % 2]
                                    op("dve", lambda e: e.reciprocal(out=r1.ap, in_=pZ[0].ap), [pZ[0]], [r1])
                                    op("dve", lambda e: e.reciprocal(out=r2.ap, in_=pZ[1].ap), [pZ[1]], [r2])
                                    op("dve", lambda e: e.tensor_mul(out=r1.ap, in0=pO[0].ap, in1=r1.ap),
                                       [pO[0], r1], [r1])
                                    op("dve", lambda e: e.tensor_mul(out=r2.ap, in0=pO[1].ap, in1=r2.ap),
                                       [pO[1], r2], [r2])
                                    op("dve", lambda e: e.scalar_tensor_tensor(out=o_.ap, in0=r2.ap,
                                                                               scalar=nlam.ap[:, l:l + 1], in1=r1.ap,
                                                                               op0=ALU.mult, op1=ALU.add),
                                       [r1, r2, nlam], [o_])
                                    op("act", lambda e: e.activation(out=s2_.ap, in_=o_.ap, func=AF.Square), [o_],
                                       [s2_])

                                def fin2(qb):
                                    o_ = oo[qb % 2]; s2_ = sq2[qb % 2]
                                    qsl = slice(qb * 512, (qb + 1) * 512)
                                    op("pe", lambda e: e.matmul(pN.ap, lhsT=onesf.ap, rhs=s2_.ap, start=True,
                                                                stop=True), [onesf, s2_], [pN])
                                    op("act", lambda e: e.activation(out=s2_.ap, in_=pN.ap, func=AF.Ln,
                                                                     scale=1.0 / 128, bias=SUBLN_EPS), [pN], [s2_])
                                    op("act", lambda e: e.activation(out=s2_.ap, in_=s2_.ap, func=AF.Exp,
                                                                     scale=-0.5), [s2_], [s2_])
                                    f_ = fin[qb % 2]
                                    op("dve", lambda e: e.scalar_tensor_tensor(out=f_.ap, in0=o_.ap,
                                                                               scalar=swt.ap[:, l:l + 1],
                                                                               in1=s2_.ap, op0=ALU.mult,
                                                                               op1=ALU.mult), [o_, swt, s2_], [f_])
                                    dma("pool", MT, MT.ap[4 + h, :, qsl], f_, f_.ap, sbuf_side=f_)

                                NTI = len(tiles)
                                per_qb = NT * 2
                                pend = {}
                                for n in range(NTI + LAG):
                                    if n < NTI:
                                        stage1(n)
                                    m = n - LAG
                                    if m >= 0:
                                        stage2(m)
                                        if (m + 1) % per_qb == 0:
                                            qb_done = m // per_qb
                                            fin1(qb_done)
                                            pend[n + 4] = qb_done
                                    if n in pend:
                                        fin2(pend.pop(n))
                                for k_ in sorted(pend):
                                    fin2(pend[k_])
                                kb.barrier()
                    kb.barrier()

                with ExitStack() as es:
                    wo = kb.sb(es, "wo", [128, KC, D], BF16)
                    wd = kb.sb(es, "wd", [128, JC, D], BF16)
                    wgr = [kb.sb(es, "wgr", [128, KC, 128], BF16) for _ in range(3)]
                    wur = [kb.sb(es, "wur", [128, KC, 128], BF16) for _ in range(3)]
                    xt = [kb.sb(es, "xt", [128, D], F32) for _ in range(2)]
                    xm = [kb.sb(es, "xm", [128, D], F32) for _ in range(8)]
                    mxb = [kb.sb(es, "mxb", [128, KC, 512], BF16) for _ in range(1)]
                    h2T = [kb.sb(es, "h2T", [128, KC, 512], BF16) for _ in range(2)]
                    hid = kb.sb(es, "hid", [128, JC, 512], BF16)
                    hid_j = [TK(hid.ap, f"hid{j}") for j in range(JC)]
                    tf = kb.sb(es, "tf", [128, D], F32)
                    xn = [kb.sb(es, "xn", [128, D], BF16) for _ in range(2)]
                    ss = [kb.sb(es, "ss", [128, 1], F32) for _ in range(6)]
                    ea = [kb.sb(es, "ea", [128, 512], F32) for _ in range(2)]
                    eb = [kb.sb(es, "eb", [128, 512], F32) for _ in range(2)]
                    ptok = [kb.ps(es, "ptok", [128, D]) for _ in range(2)]
                    ptr = kb.ps(es, "ptr", [128, KC, 128], BF16)
                    pgu = [kb.ps(es, "pgu", [128, 512]) for _ in range(3)]
                    dma("sp", wo, wo.ap, wo_s, wo_s.ap)
                    st_ = {"nss": 0, "ngu": 0, "nt": 0, "nx": 0}

                    def pre_load(b):
                        mb = mxb[0]
                        for k in range(KC):
                            dma("sp", mb, mb.ap[:, k, :], MT, MT.ap[k, :, b * 512:(b + 1) * 512])

                    def pre_tile(b, tt):
                        mb = mxb[0]; h2 = h2T[b % 2]
                        sq = subseq(b * 512)
                        it = b * 4 + tt
                        x_ = xt[st_["nx"] % 2]; n_ = xn[st_["nx"] % 2]; st_["nx"] += 1
                        pt = ptok[st_["nt"] % 2]; st_["nt"] += 1
                        s1 = ss[st_["nss"] % 6]; s2 = ss[(st_["nss"] + 1) % 6]; st_["nss"] += 2
                        xm_ = xm[(b % 2) * 4 + tt]
                        dma("sp", x_, x_.ap, X, X.ap[it * 128:(it + 1) * 128, :])
                        for hf in range(2):
                            for k in range(KC):
                                op("pe", lambda e, k=k, hf=hf: e.matmul(
                                    pt.ap[:, hf * 512:(hf + 1) * 512], lhsT=mb.ap[:, k, tt * 128:(tt + 1) * 128],
                                    rhs=wo.ap[:, k, hf * 512:(hf + 1) * 512], start=(k == 0),
                                    stop=(k == KC - 1)), [mb, wo], [pt])
                        op("act", lambda e: e.activation(out=n_.ap, in_=pt.ap, func=AF.Square,
                                                         accum_out=s1.ap), [pt], [n_, s1])
                        rstd_from_ss(s1, D, NORM_EPS)
                        op("dve", lambda e: e.scalar_tensor_tensor(out=xm_.ap, in0=pt.ap, scalar=s1.ap,
                                                                   in1=Gbc.ap[:, 0, sq, :], op0=ALU.mult,
                                                                   op1=ALU.mult), [pt, s1, Gbc], [xm_])
                        op("pool", lambda e: e.tensor_add(out=xm_.ap, in0=xm_.ap, in1=x_.ap), [xm_, x_], [xm_])
                        op("act", lambda e: e.activation(out=n_.ap, in_=xm_.ap, func=AF.Square,
                                                         accum_out=s2.ap), [xm_], [n_, s2])
                        rstd_from_ss(s2, D, NORM_EPS)
                        op("dve", lambda e: e.tensor_scalar(out=n_.ap, in0=xm_.ap, scalar1=s2.ap, scalar2=None,
                                                             op0=ALU.mult), [xm_, s2], [n_])
                        return n_

                    def pre_tile2(b, tt, n_):
                        h2 = h2T[b % 2]
                        sq = subseq(b * 512)
                        for k in range(KC):
                            op("pe", lambda e, k=k: e.transpose(ptr.ap[:, k, :], n_.ap[:, k * 128:(k + 1) * 128],
                                                                ident.ap), [n_, ident], [ptr])
                        for k in range(KC):
                            op("act", lambda e, k=k: e.activation(
                                out=h2.ap[:, k, tt * 128:(tt + 1) * 128], in_=ptr.ap[:, k, :], func=AF.Identity,
                                scale=modA.ap[:, 1, sq, k:k + 1], bias=modS.ap[:, 1, sq, k:k + 1]),
                               [ptr, modA, modS], [h2])

                    def ffn_j(b, j):
                        h2 = h2T[b % 2]
                        wg_t = wgr[j % 3]; wu_t = wur[j % 3]
                        dma("sp", wg_t, wg_t.ap, wg_s, wg_s.ap[j])
                        dma("sp", wu_t, wu_t.ap, wu_s, wu_s.ap[j])
                        pg = pgu[st_["ngu"] % 3]; pu = pgu[(st_["ngu"] + 1) % 3]; st_["ngu"] += 2
                        a_ = ea[j % 2]; b_ = eb[j % 2]
                        for k in range(KC):
                            op("pe", lambda e, k=k: e.matmul(pg.ap, lhsT=wg_t.ap[:, k, :], rhs=h2.ap[:, k, :],
                                                             start=(k == 0), stop=(k == KC - 1)), [wg_t, h2], [pg])
                        for k in range(KC):
                            op("pe", lambda e, k=k: e.matmul(pu.ap, lhsT=wu_t.ap[:, k, :], rhs=h2.ap[:, k, :],
                                                             start=(k == 0), stop=(k == KC - 1)), [wu_t, h2], [pu])
                        op("act", lambda e: e.activation(out=a_.ap, in_=pg.ap, func=AF.Exp, scale=-1.0), [pg], [a_])
                        op("act", lambda e: e.activation(out=b_.ap, in_=a_.ap, func=AF.Ln, bias=1.0), [a_], [b_])
                        op("act", lambda e: e.activation(out=a_.ap, in_=b_.ap, func=AF.Exp, scale=-1.0), [b_], [a_])
                        op("dve", lambda e: e.tensor_mul(out=b_.ap, in0=pg.ap, in1=a_.ap), [pg, a_], [b_])
                        op("dve", lambda e: e.tensor_mul(out=hid.ap[:, j, :], in0=pu.ap, in1=b_.ap),
                           [pu, b_], [hid_j[j]])

                    def down_tile(b, tt):
                        sq = subseq(b * 512)
                        it = b * 4 + tt
                        pt = ptok[st_["nt"] % 2]; st_["nt"] += 1
                        s1 = ss[st_["nss"] % 6]; st_["nss"] += 1
                        xm_ = xm[(b % 2) * 4 + tt]
                        for hf in range(2):
                            for j in range(JC):
                                op("pe", lambda e, j=j, hf=hf: e.matmul(
                                    pt.ap[:, hf * 512:(hf + 1) * 512], lhsT=hid.ap[:, j, tt * 128:(tt + 1) * 128],
                                    rhs=wd.ap[:, j, hf * 512:(hf + 1) * 512], start=(j == 0),
                                    stop=(j == JC - 1)), [hid_j[j], wd], [pt])
                        op("act", lambda e: e.activation(out=tf.ap, in_=pt.ap, func=AF.Square,
                                                         accum_out=s1.ap), [pt], [tf, s1])
                        rstd_from_ss(s1, D, NORM_EPS)
                        op("dve", lambda e: e.scalar_tensor_tensor(out=tf.ap, in0=pt.ap, scalar=s1.ap,
                                                                   in1=Gbc.ap[:, 1, sq, :], op0=ALU.mult,
                                                                   op1=ALU.mult), [pt, s1, Gbc], [tf])
                        op("pool", lambda e: e.tensor_add(out=xm_.ap, in0=tf.ap, in1=xm_.ap), [tf, xm_], [xm_])
                        dma("pool", XO, XO.ap[it * 128:(it + 1) * 128, :], xm_, xm_.ap, sbuf_side=xm_)

                    pre_load(0)
                    prev_n = None
                    for tt in range(4):
                        nn = pre_tile(0, tt)
                        if prev_n is not None:
                            pre_tile2(0, tt - 1, prev_n)
                        prev_n = nn
                    pre_tile2(0, 3, prev_n)
                    for j in range(JC):
                        dma("sp", wd, wd.ap[:, j, :], wd_s, wd_s.ap[j])
                    P1 = (1, 6, 11, 16); P2 = (5, 10, 15, 20)
                    for b in range(NB):
                        if b + 1 < NB:
                            pre_load(b + 1)
                        held = {}
                        for j in range(JC):
                            ffn_j(b, j)
                            if b + 1 < NB and j in P1:
                                held[P1.index(j)] = pre_tile(b + 1, P1.index(j))
                            if b + 1 < NB and j in P2:
                                pre_tile2(b + 1, P2.index(j), held[P2.index(j)])
                        for tt in range(4):
                            down_tile(b, tt)
                    kb.barrier()
        kb.barrier()
    return nc, kb


_PROG = {}


def _rot_tables(TA, restart):
    pos = np.arange(TA, dtype=np.float32)
    if restart:
        pos = np.where(np.arange(TA) >= TA // 2, pos - np.float32(TA // 2), pos).astype(np.float32)
    inv = (1.0 / (np.float32(10000.0) ** (np.arange(0, 64, 2, dtype=np.float32) / np.float32(64)))).astype(np.float32)
    ang = (pos[None, :] * inv[:, None]).astype(np.float32)
    c = np.cos(ang).astype(np.float32)
    s = np.sin(ang).astype(np.float32)
    cosT = np.zeros((128, TA), np.float32)
    sinT = np.zeros((128, TA), np.float32)
    for p in range(128):
        j = p % 64
        i = j % 32
        cosT[p] = c[i]
        sinT[p] = -s[i] if j < 32 else s[i]
    return cosT, sinT


def kernel(x_prompt, x_sample, c_prompt, c_sample, w_ada, b_ada, norm_pre_mix, norm_post_mix,
           norm_pre_ffn, norm_post_ffn, w_in, hg_lower_bounds, hg_gnorm, da_lambda_q1,
           da_lambda_k1, da_lambda_q2, da_lambda_k2, da_subln, w_out, w_ffn_gate, w_ffn_up,
           w_ffn_down):
    f32 = np.float32
    x_prompt = np.asarray(x_prompt, f32); x_sample = np.asarray(x_sample, f32)
    c_prompt = np.asarray(c_prompt, f32); c_sample = np.asarray(c_sample, f32)
    TA = x_prompt.shape[1]; TB = x_sample.shape[1]
    assert x_prompt.shape[0] == 4 and x_sample.shape[0] == 16
    key = (TA, TB)
    if key not in _PROG:
        _PROG[key] = build_program(TA, TB)
    nc = _PROG[key]
    w_in = np.asarray(w_in, f32)
    perm = np.arange(512).reshape(8, 2, 32)[:, ::-1, :].reshape(512)
    w_in_ext = np.ascontiguousarray(np.concatenate(
        [w_in, w_in[:, :, 2560 + perm], w_in[:, :, 3072 + perm]], axis=2))
    b_ada = np.asarray(b_ada, f32)
    b_adaT = np.ascontiguousarray(b_ada.reshape(DEPTH, 6, KC, 128).transpose(3, 0, 1, 2))
    b_adaR = np.ascontiguousarray(np.broadcast_to(b_ada[None], (4, DEPTH, 6 * D)))
    npre = np.stack([np.asarray(norm_pre_mix, f32), np.asarray(norm_pre_ffn, f32)], axis=1)
    npreT = np.ascontiguousarray(npre.reshape(DEPTH, 2, KC, 128).transpose(3, 0, 1, 2))
    npost = np.stack([np.asarray(norm_post_mix, f32), np.asarray(norm_post_ffn, f32)], axis=1)
    npostR = np.ascontiguousarray(np.broadcast_to(npost[None], (4, DEPTH, 2, D)))
    hlb = np.asarray(hg_lower_bounds, f32)
    hlbT = np.ascontiguousarray(hlb.reshape(2, DEPTH, 4, 128).transpose(3, 0, 1, 2))
    gnT = np.ascontiguousarray(np.asarray(hg_gnorm, f32).T)
    swT = np.ascontiguousarray(np.asarray(da_subln, f32).T)
    lam = np.stack([np.asarray(a, f32) for a in (da_lambda_q1, da_lambda_k1, da_lambda_q2, da_lambda_k2)], axis=0)
    lamB = np.ascontiguousarray(np.broadcast_to(lam[None], (128, 4, DEPTH, 64)))
    ident = np.eye(128, dtype=f32).astype(ml_dtypes.bfloat16)
    onesf = np.ones((128, 128), f32)
    onesb = np.ones((128, 128), f32).astype(ml_dtypes.bfloat16)
    sel = np.zeros((4, 384), f32)
    for s in range(3):
        sel[s, s * 128:(s + 1) * 128] = 1.0
    t = np.arange(512)
    rmask = np.ones((128, 2, 512), f32)
    rmask[:, 0, t % 64 == 0] = 0.0
    rmask[:, 1, t % 64 == 63] = 0.0
    s_ = np.arange(128)[:, None]; t_ = np.arange(128)[None, :]
    same = (s_ // 64) == (t_ // 64)
    tmask = np.zeros((128, 2, 128), np.uint32)
    tmask[:, 0, :] = (same & (s_ <= t_)).astype(np.uint32)
    tmask[:, 1, :] = (same & (s_ >= t_)).astype(np.uint32)
    tabs = [_rot_tables(TA, False), _rot_tables(TA, True)]
    common = dict(w_ada=np.asarray(w_ada, f32), b_adaT=b_adaT, b_adaR=b_adaR, npreT=npreT, npostR=npostR,
                  w_in=w_in_ext, hlbT=hlbT, gnT=gnT, swT=swT, lamB=lamB, w_out=np.asarray(w_out, f32),
                  w_g=np.asarray(w_ffn_gate, f32), w_u=np.asarray(w_ffn_up, f32), w_d=np.asarray(w_ffn_down, f32),
                  ident=ident, onesf=onesf, onesb=onesb, sel=sel, rmask=rmask, tmask=tmask)
    in_maps = []
    for c in range(8):
        if c < 4:
            xa = x_prompt[c]; xb = x_sample[c]
            cs = [c_prompt[c], c_prompt[c], c_sample[c]]
            fl = (0.0, 1.0); tb = tabs[0]
        else:
            k = 4 + 3 * (c - 4)
            xa = np.concatenate([x_sample[k], x_sample[k + 1]], axis=0); xb = x_sample[k + 2]
            cs = [c_sample[k], c_sample[k + 1], c_sample[k + 2]]
            fl = (NEG_BIG, 0.0); tb = tabs[1]
        c3 = np.zeros((4, D), f32)
        c3[:3] = np.stack(cs)
        c3T = np.ascontiguousarray(c3.reshape(4, KC, 128).transpose(2, 1, 0))
        flags = np.zeros((128, 2), f32); flags[:, 0] = fl[0]; flags[:, 1] = fl[1]
        m = dict(common)
        m.update(xa=np.ascontiguousarray(xa), xb=np.ascontiguousarray(xb), c3T=c3T, cosT=tb[0], sinT=tb[1],
                 flags=flags)
        in_maps.append(m)
    res = run_bass_kernel_spmd(nc, in_maps, core_ids=list(range(8)))
    if DEBUG:
        global _DBG
        _DBG = res.results
    y_prompt = np.zeros_like(x_prompt); y_sample = np.zeros_like(x_sample)
    for c in range(8):
        r = res.results[c]
        if c < 4:
            y_prompt[c] = r["ya"]; y_sample[c] = r["yb"]
        else:
            k = 4 + 3 * (c - 4)
            y_sample[k] = r["ya"][:TB]; y_sample[k + 1] = r["ya"][TB:]; y_sample[k + 2] = r["yb"]
    return (y_prompt, y_sample)
```

```python
import numpy as np
import ml_dtypes
from contextlib import ExitStack
import concourse.bass as bass
import concourse.mybir as mybir
from concourse.bass_utils import run_bass_kernel_spmd

F32 = mybir.dt.float32
BF16 = mybir.dt.bfloat16
U32 = mybir.dt.uint32
AF = mybir.ActivationFunctionType
ALU = mybir.AluOpType

D = 1024
KC = 8
HID = 2816
JC = 22
DEPTH = 2
NCOL = 5120
NEG_BIG = -30000.0
NORM_EPS = 1e-6
SUBLN_EPS = 1e-5
SEM_LIMIT = 30000
import os
DEBUG = bool(int(os.environ.get("MK_DEBUG", "0")))
NLAYERS = int(os.environ.get("MK_LAYERS", str(DEPTH)))


class TK:
    __slots__ = ("ap", "w", "r", "dsem", "dcnt", "name", "dsem_sw")

    def __init__(self, ap, name):
        self.ap = ap
        self.w = None
        self.r = {}
        self.dsem = None
        self.dcnt = 0
        self.name = name


class KB:
    def __init__(self, nc, needed=None):
        self.nc = nc
        self.needed = needed
        self.rec = {k: set() for k in ("pe", "act", "dve", "pool", "sp")}
        self.idx = {k: 0 for k in ("pe", "act", "dve", "pool", "sp")}
        self.tokmap = {k: {} for k in ("pe", "act", "dve", "pool", "sp")}
        self.emitted = {k: [] for k in ("pe", "act", "dve", "pool", "sp")}
        self.seen_idx = {k: {} for k in ("pe", "act", "dve", "pool", "sp")}
        self.eng = {"pe": nc.tensor, "act": nc.scalar, "dve": nc.vector, "pool": nc.gpsimd, "sp": nc.sync}
        self.sem = {}
        self.cnt = {}
        self.seen = {k: {} for k in self.eng}
        self.nsem = 0
        for k in self.eng:
            self._newsem(k)
        self.uid = 0
        self.dma_toks = {}
        self.free_dsems = []
        self.free_dsems_sw = []
        self.dval = {}
        self.dsem_tks = []

    def _newsem(self, k):
        self.sem[k] = self.nc.alloc_semaphore(name=f"s_{k}_{self.nsem}")
        self.nsem += 1
        self.cnt[k] = 0

    def name(self, base):
        self.uid += 1
        return f"{base}_{self.uid}"

    def sb(self, es, base, shape, dtype):
        t = es.enter_context(self.nc.sbuf_tensor(self.name(base), list(shape), dtype))
        return TK(t.ap(), base)

    def ps(self, es, base, shape, dtype=F32):
        t = es.enter_context(self.nc.psum_tensor(self.name(base), list(shape), dtype))
        return TK(t.ap(), base)

    def dram(self, base, shape, dtype, kind="Internal"):
        t = self.nc.dram_tensor(base, list(shape), dtype, kind=kind)
        return TK(t.ap(), base)

    def _wait(self, e, toks):
        import bisect
        need = {}
        for tok in toks:
            if tok is None:
                continue
            if tok[0] == "E":
                _, src, i = tok
                if src == e and e == "pe":
                    continue
                if self.needed is None:
                    if self.seen_idx[e].get(src, 0) >= i:
                        continue
                    self.seen_idx[e][src] = i
                    self.rec[src].add(i)
                    continue
                lst = self.emitted[src]
                p = bisect.bisect_left(lst, i)
                sem, val = self.tokmap[src][lst[p]]
            else:
                sem, val, src = tok
            key = sem.num
            if self.seen[e].get(key, 0) >= val:
                continue
            if key not in need or need[key][1] < val:
                need[key] = (sem, val)
        for key, (sem, val) in need.items():
            self.eng[e].wait_ge(sem, val)
            self.seen[e][key] = val

    def _wait_all(self, e, toks):
        self._wait(e, toks)

    def _deps(self, reads, writes):
        toks = []
        for t in reads:
            toks.append(t.w)
        for t in writes:
            toks.append(t.w)
            toks.extend(t.r.values())
        return toks

    def op(self, e, fn, reads=(), writes=()):
        self._wait(e, self._deps(reads, writes))
        ins = fn(self.eng[e])
        self.idx[e] += 1
        i = self.idx[e]
        if self.needed is not None and i in self.needed[e]:
            if self.cnt[e] >= SEM_LIMIT:
                self._newsem(e)
            self.cnt[e] += 1
            ins.then_inc(self.sem[e], 1)
            self.tokmap[e][i] = (self.sem[e], self.cnt[e])
            self.emitted[e].append(i)
        tok = ("E", e, i)
        for t in reads:
            t.r[("E", e)] = tok
        for t in writes:
            t.w = tok
            t.r = {}
        return ins

    def dma(self, q, out_tk, out_ap, in_tk, in_ap, sbuf_side=None, **kw):
        self._wait(q, self._deps([in_tk], [out_tk]))
        st = sbuf_side if sbuf_side is not None else out_tk
        if st.dsem is None:
            pool_ = self.free_dsems_sw if q == "pool" else self.free_dsems
            st.dsem_sw = (q == "pool")
            if pool_:
                st.dsem = pool_.pop()
            else:
                st.dsem = self.nc.alloc_semaphore(name=f"d_{self.nsem}")
                self.nsem += 1
                self.dval[st.dsem.num] = 0
            self.dsem_tks.append(st)
        ins = self.eng[q].dma_start(out=out_ap, in_=in_ap, **kw)
        self.dval[st.dsem.num] += 16
        ins.then_inc(st.dsem, 16)
        tok = (st.dsem, self.dval[st.dsem.num], "dma")
        in_tk.r[("D", tok[0].num)] = tok
        out_tk.w = tok
        out_tk.r = {}
        self.dma_toks[tok[0].num] = tok
        return tok

    def barrier(self):
        toks = [("E", k, self.idx[k]) for k in self.eng if self.idx[k] > 0]
        toks += list(self.dma_toks.values())
        for e in self.eng:
            self._wait_all(e, toks)
        self.dma_toks = {}
        for t in self.dsem_tks:
            if self.dval[t.dsem.num] >= SEM_LIMIT:
                t.dsem = None
                continue
            (self.free_dsems_sw if getattr(t, "dsem_sw", False) else self.free_dsems).append(t.dsem)
            t.dsem = None
        self.dsem_tks = []


def build_program(TA, TB):
    nc1, kb1 = _build(TA, TB, None)
    needed = {k: v for k, v in kb1.rec.items()}
    nc2, kb2 = _build(TA, TB, needed)
    return nc2


def _build(TA, TB, needed):
    assert TA == 2 * TB and TB % 512 == 0
    nc = bass.Bass("TRN2", target_bir_lowering=False)
    kb = KB(nc, needed)
    op, dma = kb.op, kb.dma

    def din(name, shape, dt=F32):
        return kb.dram(name, shape, dt, kind="ExternalInput")

    xa = din("xa", [TA, D]); xb = din("xb", [TB, D])
    c3T = din("c3T", [128, KC, 4])
    w_ada = din("w_ada", [DEPTH, D, 6 * D]); b_adaT = din("b_adaT", [128, DEPTH, 6, KC])
    b_adaR = din("b_adaR", [4, DEPTH, 6 * D])
    npreT = din("npreT", [128, DEPTH, 2, KC])
    npostR = din("npostR", [4, DEPTH, 2, D])
    w_in = din("w_in", [DEPTH, D, NCOL])
    hlbT = din("hlbT", [128, 2, DEPTH, 4])
    gnT = din("gnT", [128, DEPTH]); swT = din("swT", [128, DEPTH])
    lamB = din("lamB", [128, 4, DEPTH, 64])
    w_out = din("w_out", [DEPTH, D, D])
    w_g = din("w_g", [DEPTH, D, HID]); w_u = din("w_u", [DEPTH, D, HID]); w_d = din("w_d", [DEPTH, HID, D])
    cosT = din("cosT", [128, TA]); sinT = din("sinT", [128, TA])
    flags = din("flags", [128, 2])
    ident_d = din("ident", [128, 128], BF16)
    onesf_d = din("onesf", [128, 128])
    onesb_d = din("onesb", [128, 128], BF16)
    sel_d = din("sel", [4, 3 * 128])
    rmask_d = din("rmask", [128, 2, 512])
    tmask_d = din("tmask", [128, 2, 128], U32)
    ya = kb.dram("ya", [TA, D], F32, kind="ExternalOutput")
    yb = kb.dram("yb", [TB, D], F32, kind="ExternalOutput")
    dk_ = "ExternalOutput" if DEBUG else "Internal"
    x1a = kb.dram("x1a", [TA, D], F32, kind=dk_); x1b = kb.dram("x1b", [TB, D], F32, kind=dk_)
    mixTa = kb.dram("mixTa", [KC, 128, TA], BF16, kind=dk_); mixTb = kb.dram("mixTb", [KC, 128, TB], BF16, kind=dk_)
    wo_s = kb.dram("wo_s", [128, KC, D], BF16)
    wg_s = kb.dram("wg_s", [JC, 128, KC, 128], BF16); wu_s = kb.dram("wu_s", [JC, 128, KC, 128], BF16)
    wd_s = kb.dram("wd_s", [JC, 128, D], BF16)

    with ExitStack() as g:
        ident = kb.sb(g, "ident", [128, 128], BF16)
        onesf = kb.sb(g, "onesf", [128, 128], F32)
        onesb = kb.sb(g, "onesb", [128, 128], BF16)
        sel = kb.sb(g, "sel", [4, 384], F32)
        rmask = kb.sb(g, "rmask", [128, 2, 512], F32)
        tmask = kb.sb(g, "tmask", [128, 2, 128], U32)
        flg = kb.sb(g, "flg", [128, 2], F32)
        zero1 = kb.sb(g, "zero1", [128, 1], F32)
        sc = kb.sb(g, "sc", [128, KC, 4], F32)
        modA = kb.sb(g, "modA", [128, 2, 3, KC], F32)
        modS = kb.sb(g, "modS", [128, 2, 3, KC], F32)
        Gbc = kb.sb(g, "Gbc", [128, 2, 3, D], F32)
        lbt = kb.sb(g, "lbt", [128, 2, DEPTH, 4], F32)
        omlt = kb.sb(g, "omlt", [128, 2, DEPTH, 4], F32)
        gnt = kb.sb(g, "gnt", [128, DEPTH], F32)
        swt = kb.sb(g, "swt", [128, DEPTH], F32)
        nlam = kb.sb(g, "nlam", [128, DEPTH], F32)
        for (t, d_) in ((ident, ident_d), (onesf, onesf_d), (onesb, onesb_d), (sel, sel_d), (rmask, rmask_d),
                        (tmask, tmask_d), (flg, flags), (lbt, hlbT), (gnt, gnT), (swt, swT)):
            dma("sp", t, t.ap, d_, d_.ap)
        op("dve", lambda e: e.memset(zero1.ap, 0.0), writes=[zero1])

        with ExitStack() as es:
            ct = kb.sb(es, "ct", [128, KC, 4], F32)
            t1 = kb.sb(es, "t1", [128, KC, 4], F32)
            dma("sp", ct, ct.ap, c3T, c3T.ap)
            op("act", lambda e: e.activation(out=t1.ap, in_=ct.ap, func=AF.Exp, scale=-1.0), [ct], [t1])
            op("dve", lambda e: e.tensor_scalar_add(out=t1.ap, in0=t1.ap, scalar1=1.0), [t1], [t1])
            op("dve", lambda e: e.reciprocal(out=t1.ap, in_=t1.ap), [t1], [t1])
            op("dve", lambda e: e.tensor_mul(out=sc.ap, in0=ct.ap, in1=t1.ap), [ct, t1], [sc])
            e0 = kb.sb(es, "e0", [128, 2, DEPTH, 4], F32)
            tot = kb.sb(es, "tot", [128, 2, 4], F32)
            op("act", lambda e: e.activation(out=e0.ap, in_=lbt.ap, func=AF.Exp), [lbt], [e0])
            op("dve", lambda e: e.tensor_add(out=tot.ap, in0=e0.ap[:, :, 0, :], in1=e0.ap[:, :, 1, :]), [e0], [tot])
            op("dve", lambda e: e.reciprocal(out=tot.ap, in_=tot.ap), [tot], [tot])
            for l in range(DEPTH):
                op("dve", lambda e, l=l: e.tensor_mul(out=e0.ap[:, :, l, :], in0=e0.ap[:, :, l, :], in1=tot.ap),
                   [e0, tot], [e0])
            op("dve", lambda e: e.tensor_sub(out=lbt.ap[:, :, 0, :], in0=e0.ap[:, :, 0, :], in1=e0.ap[:, :, 0, :]),
               [e0], [lbt])
            op("dve", lambda e: e.tensor_add(out=tot.ap, in0=e0.ap[:, :, 0, :], in1=e0.ap[:, :, 1, :]), [e0], [tot])
            op("dve", lambda e: e.tensor_sub(out=lbt.ap[:, :, 1, :], in0=tot.ap, in1=e0.ap[:, :, 0, :]),
               [tot, e0, lbt], [lbt])
            op("dve", lambda e: e.tensor_scalar(out=omlt.ap, in0=lbt.ap, scalar1=-1.0, scalar2=1.0,
                                                 op0=ALU.mult, op1=ALU.add), [lbt], [omlt])
            lm = kb.sb(es, "lm", [128, 4, DEPTH, 64], F32)
            pr = kb.sb(es, "pr", [128, 2, DEPTH, 64], F32)
            sm = kb.sb(es, "sm", [128, 2, DEPTH], F32)
            dma("sp", lm, lm.ap, lamB, lamB.ap)
            op("dve", lambda e: e.tensor_mul(out=pr.ap[:, 0], in0=lm.ap[:, 0], in1=lm.ap[:, 1]), [lm], [pr])
            op("dve", lambda e: e.tensor_mul(out=pr.ap[:, 1], in0=lm.ap[:, 2], in1=lm.ap[:, 3]), [lm, pr], [pr])
            op("dve", lambda e: e.reduce_sum(out=sm.ap, in_=pr.ap, axis=mybir.AxisListType.X), [pr], [sm])
            op("act", lambda e: e.activation(out=sm.ap, in_=sm.ap, func=AF.Exp), [sm], [sm])
            for l in range(DEPTH):
                lam_init = 0.8 - 0.6 * float(np.exp(-0.3 * l))
                op("dve", lambda e, l=l: e.tensor_sub(out=nlam.ap[:, l:l + 1], in0=sm.ap[:, 1, l:l + 1],
                                                      in1=sm.ap[:, 0, l:l + 1]), [sm, nlam], [nlam])
                op("dve", lambda e, l=l, li=lam_init: e.tensor_scalar_add(out=nlam.ap[:, l:l + 1],
                                                                          in0=nlam.ap[:, l:l + 1], scalar1=-li),
                   [nlam], [nlam])
                op("dve", lambda e, l=l, li=lam_init: e.tensor_scalar_mul(out=swt.ap[:, l:l + 1],
                                                                          in0=swt.ap[:, l:l + 1], scalar1=1.0 - li),
                   [swt], [swt])
            kb.barrier()

        segs = [(0, TA), (1, TB)]

        def rstd_from_ss(ss, n, eps):
            op("dve", lambda e: e.tensor_scalar(out=ss.ap, in0=ss.ap, scalar1=1.0 / n, scalar2=eps,
                                                 op0=ALU.mult, op1=ALU.add), [ss], [ss])
            op("act", lambda e: e.activation(out=ss.ap, in_=ss.ap, func=AF.Ln), [ss], [ss])
            op("act", lambda e: e.activation(out=ss.ap, in_=ss.ap, func=AF.Exp, scale=-0.5), [ss], [ss])

        for l in range(NLAYERS):
            xin = [xa, xb] if l == 0 else [x1a, x1b]
            xout = [x1a, x1b] if l == 0 else [ya, yb]
            if l == DEPTH - 1:
                xout = [ya, yb]
            mixTs = [mixTa, mixTb]
            with ExitStack() as es:
                wa = [kb.sb(es, "wa", [128, KC, D], F32) for _ in range(2)]
                badT = kb.sb(es, "badT", [128, 6, KC], F32)
                badR = kb.sb(es, "badR", [4, 6 * D], F32)
                npT = kb.sb(es, "npT", [128, 2, KC], F32)
                npR = kb.sb(es, "npR", [4, 2, D], F32)
                grow = kb.sb(es, "grow", [4, D], F32)
                pm = kb.ps(es, "pm", [128, KC, 4])
                pr_ = [kb.ps(es, "prw", [128, 512]) for _ in range(2)]
                dma("sp", badT, badT.ap, b_adaT, b_adaT.ap[:, l])
                dma("sp", badR, badR.ap, b_adaR, b_adaR.ap[:, l])
                dma("sp", npT, npT.ap, npreT, npreT.ap[:, l])
                dma("sp", npR, npR.ap, npostR, npostR.ap[:, l])
                for part in range(6):
                    w = wa[part % 2]
                    src = w_ada.ap[l, :, part * D:(part + 1) * D].rearrange("(k p) n -> p k n", p=128)
                    for k in range(KC):
                        dma("sp", w, w.ap[:, k, :], w_ada, src[:, k, :])
                    which = 0 if part < 3 else 1
                    kind = part % 3
                    if kind < 2:
                        for m in range(KC):
                            for k in range(KC):
                                op("pe", lambda e, m=m, k=k, w=w: e.matmul(
                                    pm.ap[:, m, :], lhsT=w.ap[:, k, m * 128:(m + 1) * 128], rhs=sc.ap[:, k, :],
                                    start=(k == 0), stop=(k == KC - 1)), [w, sc], [pm])
                        for s in range(3):
                            if kind == 0:
                                op("dve", lambda e, s=s, which=which, part=part: e.tensor_add(
                                    out=modS.ap[:, which, s, :], in0=pm.ap[:, :, s], in1=badT.ap[:, part, :]),
                                   [pm, badT, modS], [modS])
                            else:
                                op("dve", lambda e, s=s, which=which, part=part: e.tensor_add(
                                    out=modA.ap[:, which, s, :], in0=pm.ap[:, :, s], in1=badT.ap[:, part, :]),
                                   [pm, badT, modA], [modA])
                                op("dve", lambda e, s=s, which=which: e.tensor_scalar_add(
                                    out=modA.ap[:, which, s, :], in0=modA.ap[:, which, s, :], scalar1=1.0),
                                   [modA], [modA])
                                op("dve", lambda e, s=s, which=which: e.tensor_mul(
                                    out=modA.ap[:, which, s, :], in0=modA.ap[:, which, s, :],
                                    in1=npT.ap[:, which, :]), [modA, npT], [modA])
                    else:
                        for hf in range(2):
                            p_ = pr_[hf]
                            for k in range(KC):
                                op("pe", lambda e, k=k, hf=hf, p_=p_, w=w: e.matmul(
                                    p_.ap[0:4, :], lhsT=sc.ap[:, k, :], rhs=w.ap[:, k, hf * 512:(hf + 1) * 512],
                                    start=(k == 0), stop=(k == KC - 1)), [w, sc], [p_])
                            op("dve", lambda e, hf=hf, p_=p_, part=part: e.tensor_add(
                                out=grow.ap[:, hf * 512:(hf + 1) * 512], in0=p_.ap[0:4, :],
                                in1=badR.ap[:, part * D + hf * 512: part * D + (hf + 1) * 512]),
                               [p_, badR, grow], [grow])
                        op("dve", lambda e, which=which: e.tensor_mul(out=grow.ap, in0=grow.ap,
                                                                      in1=npR.ap[:, which, :]), [grow, npR], [grow])
                        for s in range(3):
                            for hf in range(2):
                                p_ = pr_[hf]
                                op("pe", lambda e, s=s, hf=hf, p_=p_: e.matmul(
                                    p_.ap, lhsT=sel.ap[:, s * 128:(s + 1) * 128],
                                    rhs=grow.ap[:, hf * 512:(hf + 1) * 512], start=True, stop=True),
                                   [sel, grow], [p_])
                                op("dve", lambda e, s=s, hf=hf, p_=p_, which=which: e.tensor_copy(
                                    out=Gbc.ap[:, which, s, hf * 512:(hf + 1) * 512], in_=p_.ap), [p_, Gbc], [Gbc])
                kb.barrier()
            cast_jobs = []
            for m in range(KC):
                cast_jobs.append((w_out, w_out.ap[l, :, m * 128:(m + 1) * 128].rearrange("(k p) n -> p k n", p=128),
                                  wo_s, wo_s.ap[:, :, m * 128:(m + 1) * 128]))
            for j in range(JC):
                cast_jobs.append((w_g, w_g.ap[l, :, j * 128:(j + 1) * 128].rearrange("(k p) n -> p k n", p=128),
                                  wg_s, wg_s.ap[j]))
                cast_jobs.append((w_u, w_u.ap[l, :, j * 128:(j + 1) * 128].rearrange("(k p) n -> p k n", p=128),
                                  wu_s, wu_s.ap[j]))
                cast_jobs.append((w_d, w_d.ap[l, j * 128:(j + 1) * 128, :].rearrange("p (k n) -> p k n", k=KC),
                                  wd_s, wd_s.ap[j].rearrange("p (k n) -> p k n", k=KC)))

            for (si, T) in segs:
                NT = T // 128
                NB = T // 512
                X = xin[si]
                XO = xout[si]
                MT = mixTs[si]

                def subseq(tok):
                    return (tok // TB) if si == 0 else 2

                with ExitStack() as segs_es:
                    hT = kb.sb(segs_es, "hT", [128, KC, T], BF16)
                    hT_blk = [TK(hT.ap, f"hTb{b}") for b in range(NB)]
                    with ExitStack() as es:
                        xt = [kb.sb(es, "xt", [128, D], F32) for _ in range(3)]
                        junk = kb.sb(es, "junk", [128, D], BF16)
                        xn = [kb.sb(es, "xn", [128, D], BF16) for _ in range(2)]
                        ss = [kb.sb(es, "ss", [128, 1], F32) for _ in range(2)]
                        ptr = [kb.ps(es, "ptr", [128, KC, 128], BF16) for _ in range(2)]
                        stg = [kb.sb(es, "stg", [128, KC, 128], F32) for _ in range(4)]
                        stb = [kb.sb(es, "stb", [128, KC, 128], BF16) for _ in range(4)]
                        ncast = [0]

                        def cast_some(k_):
                            for _ in range(k_):
                                if not cast_jobs:
                                    return
                                src_tk, src_ap, dst_tk, dst_ap = cast_jobs.pop(0)
                                i_ = ncast[0] % 4
                                ncast[0] += 1
                                dma("sp", stg[i_], stg[i_].ap, src_tk, src_ap)
                                op("pool", lambda e: e.tensor_copy(out=stb[i_].ap, in_=stg[i_].ap), [stg[i_]],
                                   [stb[i_]])
                                dma("pool", dst_tk, dst_ap, stb[i_], stb[i_].ap, sbuf_side=stb[i_])

                        per_tile = -(-len(cast_jobs) // NT) if si == 0 else 0
                        for it in range(NT):
                            cast_some(per_tile)
                            x_ = xt[it % 3]; s_ = ss[it % 2]; n_ = xn[it % 2]; p_ = ptr[it % 2]
                            hb = hT_blk[it // 4]
                            sq = subseq(it * 128)
                            dma("sp", x_, x_.ap, X, X.ap[it * 128:(it + 1) * 128, :])
                            op("act", lambda e: e.activation(out=junk.ap, in_=x_.ap, func=AF.Square,
                                                             accum_out=s_.ap), [x_], [junk, s_])
                            rstd_from_ss(s_, D, NORM_EPS)
                            op("dve", lambda e: e.tensor_scalar(out=n_.ap, in0=x_.ap, scalar1=s_.ap, scalar2=None,
                                                                 op0=ALU.mult), [x_, s_], [n_])
                            for k in range(KC):
                                op("pe", lambda e, k=k: e.transpose(p_.ap[:, k, :], n_.ap[:, k * 128:(k + 1) * 128],
                                                                    ident.ap), [n_, ident], [p_])
                            for k in range(KC):
                                op("act", lambda e, k=k: e.activation(
                                    out=hT.ap[:, k, it * 128:(it + 1) * 128], in_=p_.ap[:, k, :], func=AF.Identity,
                                    scale=modA.ap[:, 0, sq, k:k + 1], bias=modS.ap[:, 0, sq, k:k + 1]),
                                   [p_, modA, modS], [hb])
                        cast_some(len(cast_jobs))
                        kb.barrier()

                    def load_w(es_w, wst, wbf, col, n_i):
                        i = n_i % len(wst)
                        src = w_in.ap[l, :, col:col + 128].rearrange("(k p) n -> p k n", p=128)
                        dma("sp", wst[i], wst[i].ap, w_in, src)
                        wt = wbf[n_i % len(wbf)]
                        op("pool", lambda e: e.tensor_copy(out=wt.ap, in_=wst[i].ap), [wst[i]], [wt])
                        return wt

                    def proj_fm(pz, wt, b):
                        for k in range(KC):
                            op("pe", lambda e, k=k: e.matmul(pz.ap, lhsT=wt.ap[:, k, :],
                                                             rhs=hT.ap[:, k, b * 512:(b + 1) * 512],
                                                             start=(k == 0), stop=(k == KC - 1)),
                               [wt, hT_blk[b]], [pz])

                    def proj_tm(pz, wt, it):
                        for k in range(KC):
                            op("pe", lambda e, k=k: e.matmul(pz.ap, lhsT=hT.ap[:, k, it * 128:(it + 1) * 128],
                                                             rhs=wt.ap[:, k, :],
                                                             start=(k == 0), stop=(k == KC - 1)),
                               [wt, hT_blk[it // 4]], [pz])

                    with ExitStack() as es:
                        for h in range(4):
                            if h == 0:
                                NCH = T // 64
                                wst = [kb.sb(es, "wst", [128, KC, 128], F32) for _ in range(2)]
                                wbf = [kb.sb(es, "wbf", [128, KC, 128], BF16) for _ in range(5)]
                                qf = kb.sb(es, "qf", [128, T], F32)
                                qf_b = [TK(qf.ap, f"qfb{b}") for b in range(NB)]
                                gate = kb.sb(es, "gate", [128, T], BF16)
                                vtm = kb.sb(es, "vtm", [64, NCH, 128], BF16)
                                vtm_b = [TK(vtm.ap, f"vtb{b}") for b in range(NB)]
                                oacc = kb.sb(es, "oacc", [128, T], F32)
                                oacc_b = [TK(oacc.ap, f"oab{b}") for b in range(NB)]
                                tmp = [[kb.sb(es, "tmp", [128, 512], F32) for _ in range(4)] for _ in range(2)]
                                qt = [[kb.sb(es, "qt", [128, 512], BF16) for _ in range(2)] for _ in range(2)]
                                kt = [[kb.sb(es, "kt", [128, 512], BF16) for _ in range(2)] for _ in range(2)]
                                kh = [[kb.sb(es, "kh", [128, 512], BF16) for _ in range(2)] for _ in range(2)]
                                ktm = [[kb.sb(es, "ktm", [64, 128], BF16) for _ in range(3)] for _ in range(2)]
                                stm = [[kb.sb(es, "stm", [64, 64], BF16) for _ in range(3)] for _ in range(2)]
                                bref = [kb.sb(es, "bref", [128, 8], F32) for _ in range(2)]
                                d1 = [[kb.sb(es, "d1", [128, 8], F32) for _ in range(2)] for _ in range(2)]
                                d2 = [kb.sb(es, "d2", [128, 8], F32) for _ in range(2)]
                                d12 = [[kb.sb(es, "d12", [128, 8], F32) for _ in range(2)] for _ in range(2)]
                                S = [kb.sb(es, "S", [128, 128], F32) for _ in range(2)]
                                Sp = [[kb.sb(es, "Sp", [128, 128], BF16) for _ in range(2)] for _ in range(2)]
                                pz = [kb.ps(es, "pz", [128, 512]) for _ in range(2)]
                                pv = kb.ps(es, "pv", [128, 512])
                                pobank = [kb.ps(es, "pob", [128, 512]) for _ in range(2)]
                                po = [[TK(pobank[d_].ap[:, i * 64:(i + 1) * 64], f"po{d_}{i}") for i in range(8)]
                                      for d_ in range(2)]

                                def poslot(n_, dr):
                                    g_ = (n_ // 4) % 2
                                    k_ = (n_ % 4) if dr == 0 else 3 - (n_ % 4)
                                    return g_ * 4 + k_
                                msbank = [kb.ps(es, "msb", [128, 512]) for _ in range(2)]
                                pst = [[TK(msbank[d_].ap[0:64, i * 64:(i + 1) * 64], f"pst{d_}{i}") for i in range(2)]
                                       for d_ in range(2)]
                                pkv = [[TK(msbank[d_].ap[:, 128 + i * 128:256 + i * 128], f"pkv{d_}{i}")
                                        for i in range(2)] for d_ in range(2)]
                                pktb = kb.ps(es, "pktb", [128, 4, 128], BF16)
                                pkt = [[TK(pktb.ap[0:64, d_ * 2 + i, :], f"pkt{d_}{i}") for i in range(2)]
                                       for d_ in range(2)]
                            for d_ in range(2):
                                for t_ in stm[d_]:
                                    op("pool", lambda e, t_=t_: e.memset(t_.ap, 0.0), writes=[t_])
                                op("pool", lambda e, d_=d_: e.memset(S[d_].ap, 0.0), writes=[S[d_]])
                            op("pool", lambda e: e.memset(oacc.ap, 0.0), writes=[oacc] + oacc_b)
                            cols = [h * 128, 512 + h * 128, 1024 + h * 128, 1536 + h * 128, 2048 + h * 128]
                            wq = load_w(es, wst, wbf, cols[0], 0)
                            wff = load_w(es, wst, wbf, cols[1], 1)
                            wfb = load_w(es, wst, wbf, cols[2], 2)
                            wv = load_w(es, wst, wbf, cols[3], 3)
                            wg_ = load_w(es, wst, wbf, cols[4], 4)
                            cnt = [0]

                            def silu_ps(out_ap, out_tk, p_, tm):
                                a, b_, c_ = tm[0], tm[1], tm[2]
                                op("act", lambda e: e.activation(out=a.ap, in_=p_.ap, func=AF.Exp, scale=-1.0),
                                   [p_], [a])
                                op("act", lambda e: e.activation(out=b_.ap, in_=a.ap, func=AF.Ln, bias=1.0),
                                   [a], [b_])
                                op("act", lambda e: e.activation(out=c_.ap, in_=b_.ap, func=AF.Exp, scale=-1.0),
                                   [b_], [c_])
                                op("dve", lambda e: e.tensor_mul(out=out_ap, in0=p_.ap, in1=c_.ap), [p_, c_],
                                   [out_tk])

                            def gates(b, dr, wt, qi):
                                p_ = pz[cnt[0] % 2]; cnt[0] += 1
                                A, B, C, Dd = tmp[dr]
                                lb_ap = lbt.ap[:, dr, l, h:h + 1]
                                oml_ap = omlt.ap[:, dr, l, h:h + 1]
                                q_, k_, kh_ = qt[dr][qi], kt[dr][qi], kh[dr][qi]
                                D1, D12, D2, br = d1[dr][qi], d12[dr][qi], d2[dr], bref[dr]
                                proj_fm(p_, wt, b)
                                op("act", lambda e: e.activation(out=A.ap, in_=p_.ap, func=AF.Exp, scale=-1.0),
                                   [p_], [A])
                                op("act", lambda e: e.activation(out=B.ap, in_=A.ap, func=AF.Ln, scale=lb_ap,
                                                                 bias=1.0), [A, lbt], [B])
                                op("act", lambda e: e.activation(out=C.ap, in_=A.ap, func=AF.Ln, bias=1.0),
                                   [A], [C])
                                op("act", lambda e: e.activation(out=Dd.ap, in_=C.ap, func=AF.Exp, scale=-1.0),
                                   [C], [Dd])
                                yield
                                op("pool", lambda e: e.tensor_sub(out=B.ap, in0=B.ap, in1=C.ap), [B, C], [B])
                                op("dve", lambda e: e.scalar_tensor_tensor(out=A.ap, in0=A.ap, scalar=oml_ap,
                                                                           in1=Dd.ap, op0=ALU.mult, op1=ALU.mult),
                                   [A, Dd, omlt], [A])
                                if dr == 0:
                                    op("dve", lambda e: e.tensor_tensor_scan(out=C.ap, data0=rmask.ap[:, 0, :],
                                                                             data1=B.ap, initial=0.0,
                                                                             op0=ALU.mult, op1=ALU.add),
                                       [rmask, B], [C])
                                    ri, ei = 31, 63
                                else:
                                    op("dve", lambda e: e.tensor_tensor_scan(out=C.ap[:, ::-1],
                                                                             data0=rmask.ap[:, 1, ::-1],
                                                                             data1=B.ap[:, ::-1], initial=0.0,
                                                                             op0=ALU.mult, op1=ALU.add),
                                       [rmask, B], [C])
                                    ri, ei = 32, 0
                                yield
                                c3 = C.ap.rearrange("p (c t) -> p c t", t=64)
                                op("dve", lambda e: e.tensor_copy(out=br.ap, in_=c3[:, :, ri]), [C], [br])
                                op("dve", lambda e: e.tensor_sub(out=D2.ap, in0=c3[:, :, ei], in1=br.ap),
                                   [C, br], [D2])
                                op("act", lambda e: e.activation(out=D12.ap, in_=c3[:, :, ei], func=AF.Exp),
                                   [C], [D12])
                                op("act", lambda e: e.activation(out=D1.ap, in_=br.ap, func=AF.Exp), [br], [D1])
                                op("act", lambda e: e.activation(out=D2.ap, in_=D2.ap, func=AF.Exp), [D2], [D2])
                                yield
                                op("dve", lambda e: e.tensor_sub(out=c3, in0=c3,
                                                                 in1=br.ap.unsqueeze(2).to_broadcast([128, 8, 64])),
                                   [C, br], [C])
                                op("act", lambda e: e.activation(out=Dd.ap, in_=C.ap, func=AF.Exp), [C], [Dd])
                                op("act", lambda e: e.activation(out=B.ap, in_=C.ap, func=AF.Exp, scale=-1.0),
                                   [C], [B])
                                yield
                                op("dve", lambda e: e.tensor_mul(out=q_.ap, in0=qf.ap[:, b * 512:(b + 1) * 512],
                                                                 in1=Dd.ap), [qf_b[b], Dd], [q_])
                                op("pool", lambda e: e.tensor_mul(out=k_.ap, in0=A.ap, in1=B.ap), [A, B], [k_])
                                op("pool", lambda e: e.tensor_mul(
                                    out=kh_.ap.rearrange("p (c t) -> p c t", t=64),
                                    in0=k_.ap.rearrange("p (c t) -> p c t", t=64),
                                    in1=D2.ap.unsqueeze(2).to_broadcast([128, 8, 64])), [k_, D2], [kh_])

                            def cparams(i, j, dr):
                                b = i if dr == 0 else NB - 1 - i
                                ch = j if dr == 0 else 7 - j
                                n_ = i * 8 + j
                                return b, ch, n_, i % 2

                            def stageA(i, j, dr):
                                b, ch, n_, qi = cparams(i, j, dr)
                                o0 = ch * 64
                                kh_ = kh[dr][qi]
                                pk_ = pkt[dr][n_ % 2]; km = ktm[dr][n_ % 3]
                                op("pe", lambda e: e.transpose(pk_.ap, kh_.ap[:, o0:o0 + 64], ident.ap),
                                   [kh_, ident], [pk_])
                                op("act", lambda e: e.copy(out=km.ap, in_=pk_.ap), [pk_], [km])

                            def stageB(i, j, dr):
                                b, ch, n_, qi = cparams(i, j, dr)
                                gch = b * 8 + ch
                                o0 = ch * 64
                                q_, k_ = qt[dr][qi], kt[dr][qi]
                                km = ktm[dr][n_ % 3]; sm_ = stm[dr][n_ % 3]
                                ps_ = pst[dr][n_ % 2]; pkv_ = pkv[dr][n_ % 2]
                                vt_ = vtm_b[b]
                                op("pe", lambda e: e.matmul(pkv_.ap, lhsT=km.ap, rhs=vtm.ap[:, gch, :], start=True,
                                                            stop=True), [km, vt_], [pkv_])
                                if dr == 0:
                                    ra = (slice(0, 64), slice(32, 64)); rb = (slice(0, 32), slice(0, 32))
                                else:
                                    ra = (slice(0, 64), slice(0, 32)); rb = (slice(32, 64), slice(32, 64))
                                for (rs_, cs_) in (ra, rb):
                                    op("pe", lambda e, rs_=rs_, cs_=cs_: e.matmul(
                                        ps_.ap[rs_, cs_], lhsT=k_.ap[:, o0 + rs_.start:o0 + rs_.stop],
                                        rhs=q_.ap[:, o0 + cs_.start:o0 + cs_.stop], start=True, stop=True),
                                       [k_, q_], [ps_])
                                op("dve", lambda e: e.copy_predicated(out=sm_.ap, mask=tmask.ap[0:64, dr, 0:64],
                                                                      data=ps_.ap), [ps_, tmask, sm_], [sm_])

                            def stageC1(i, j, dr):
                                b, ch, n_, qi = cparams(i, j, dr)
                                gch = b * 8 + ch
                                sp_ = Sp[dr][n_ % 2]; S_ = S[dr]
                                if si == 0 and ((dr == 0 and gch == (TB // 64)) or
                                                (dr == 1 and gch == (TB // 64) - 1)):
                                    op("dve", lambda e: e.tensor_scalar(out=S_.ap, in0=S_.ap,
                                                                         scalar1=flg.ap[:, 1:2], scalar2=None,
                                                                         op0=ALU.mult), [S_, flg], [S_])
                                op("act", lambda e: e.activation(out=sp_.ap, in_=S_.ap, func=AF.Copy,
                                                                 scale=d1[dr][qi].ap[:, ch:ch + 1]),
                                   [S_, d1[dr][qi]], [sp_])

                            def stageC2(i, j, dr):
                                b, ch, n_, qi = cparams(i, j, dr)
                                gch = b * 8 + ch
                                o0 = ch * 64
                                q_ = qt[dr][qi]
                                sm_ = stm[dr][n_ % 3]; po_ = po[dr][poslot(n_, dr)]; vt_ = vtm_b[b]
                                op("pe", lambda e: e.matmul(po_.ap, lhsT=vtm.ap[:, gch, :], rhs=sm_.ap, start=True,
                                                            stop=False), [vt_, sm_], [po_])

                            def stageC3(i, j, dr):
                                b, ch, n_, qi = cparams(i, j, dr)
                                gch = b * 8 + ch
                                o0 = ch * 64
                                q_ = qt[dr][qi]
                                po_ = po[dr][poslot(n_, dr)]; sp_ = Sp[dr][n_ % 2]; S_ = S[dr]; pkv_ = pkv[dr][n_ % 2]
                                op("pe", lambda e: e.matmul(po_.ap, lhsT=sp_.ap, rhs=q_.ap[:, o0:o0 + 64],
                                                            start=False, stop=True), [sp_, q_], [po_])
                                op("dve", lambda e: e.scalar_tensor_tensor(out=S_.ap, in0=S_.ap,
                                                                           scalar=d12[dr][qi].ap[:, ch:ch + 1],
                                                                           in1=pkv_.ap, op0=ALU.mult, op1=ALU.add),
                                   [S_, d12[dr][qi], pkv_], [S_])
                                if n_ % 4 == 3:
                                    g_ = (n_ // 4) % 2
                                    t0 = (gch - 3) * 64 if dr == 0 else gch * 64
                                    ob = oacc_b[b]
                                    grp = po[dr][g_ * 4:g_ * 4 + 4]
                                    op("dve", lambda e: e.tensor_add(
                                        out=oacc.ap[:, t0:t0 + 256], in0=pobank[dr].ap[:, g_ * 256:(g_ + 1) * 256],
                                        in1=oacc.ap[:, t0:t0 + 256]), grp + [ob], [ob])

                            for b in range(NB):
                                p_ = pz[cnt[0] % 2]; cnt[0] += 1
                                proj_fm(p_, wq, b)
                                silu_ps(qf.ap[:, b * 512:(b + 1) * 512], qf_b[b], p_, tmp[0])
                                p_ = pz[cnt[0] % 2]; cnt[0] += 1
                                proj_fm(p_, wg_, b)
                                silu_ps(gate.ap[:, b * 512:(b + 1) * 512], gate, p_, tmp[1])
                                for c4 in range(2):
                                    for cq in range(4):
                                        cch = b * 8 + c4 * 4 + cq
                                        for k in range(KC):
                                            op("pe", lambda e, k=k: e.matmul(
                                                pv.ap[0:64, cq * 128:(cq + 1) * 128],
                                                lhsT=hT.ap[:, k, cch * 64:(cch + 1) * 64], rhs=wv.ap[:, k, :],
                                                start=(k == 0), stop=(k == KC - 1)), [wv, hT_blk[b]], [pv])
                                    c0_ = b * 8 + c4 * 4
                                    op("act", lambda e: e.copy(
                                        out=vtm.ap[:, c0_:c0_ + 4, :],
                                        in_=pv.ap[0:64, :].rearrange("p (c n) -> p c n", n=128)), [pv], [vtm_b[b]])
                            for d_ in range(2):
                                for t_ in pst[d_]:
                                    op("dve", lambda e, t_=t_: e.memset(t_.ap, 0.0), writes=[t_])
                            steps = [(i, j) for i in range(NB) for j in range(8)]
                            NS = len(steps)
                            for n in range(NS + 2):
                                if n == 0:
                                    for _ in gates(0, 0, wff, 0):
                                        pass
                                    for _ in gates(NB - 1, 1, wfb, 0):
                                        pass
                                    ggen = []
                                if n < NS and steps[n][1] == 1 and steps[n][0] + 1 < NB:
                                    i = steps[n][0] + 1
                                    ggen = [gates(i, 0, wff, i % 2), gates(NB - 1 - i, 1, wfb, i % 2)]
                                if n < NS and 1 <= steps[n][1] <= 5:
                                    for gg in ggen:
                                        next(gg, None)
                                if n < NS and steps[n][1] == 6:
                                    for gg in ggen:
                                        for _ in gg:
                                            pass
                                    ggen = []
                                if n - 2 >= 0:
                                    for dr in range(2):
                                        stageC1(*steps[n - 2], dr)
                                if n - 1 >= 0 and n - 1 < NS:
                                    for dr in range(2):
                                        stageB(*steps[n - 1], dr)
                                if n - 2 >= 0:
                                    for dr in range(2):
                                        stageC2(*steps[n - 2], dr)
                                if n < NS:
                                    for dr in range(2):
                                        stageA(*steps[n], dr)
                                if n - 2 >= 0:
                                    for dr in range(2):
                                        stageC3(*steps[n - 2], dr)
                            with ExitStack() as es2:
                                sqt = [tmp[0][0], tmp[1][0]]
                                fin = [qt[0][0], qt[0][1]]
                                for b in range(NB):
                                    ob = oacc_b[b]
                                    sq_ = sqt[b % 2]
                                    osl = oacc.ap[:, b * 512:(b + 1) * 512]
                                    op("act", lambda e: e.activation(out=sq_.ap, in_=osl, func=AF.Square), [ob],
                                       [sq_])
                                    p_ = pz[cnt[0] % 2]; cnt[0] += 1
                                    op("pe", lambda e: e.matmul(p_.ap, lhsT=onesf.ap, rhs=sq_.ap, start=True,
                                                                stop=True), [onesf, sq_], [p_])
                                    op("act", lambda e: e.activation(out=sq_.ap, in_=p_.ap, func=AF.Ln,
                                                                     scale=1.0 / 128, bias=NORM_EPS), [p_], [sq_])
                                    op("act", lambda e: e.activation(out=sq_.ap, in_=sq_.ap, func=AF.Exp,
                                                                     scale=-0.5), [sq_], [sq_])
                                    op("dve", lambda e: e.scalar_tensor_tensor(out=sq_.ap, in0=osl,
                                                                               scalar=gnt.ap[:, l:l + 1], in1=sq_.ap,
                                                                               op0=ALU.mult, op1=ALU.mult),
                                       [ob, gnt, sq_], [sq_])
                                    f_ = fin[b % 2]
                                    op("pool", lambda e: e.tensor_mul(out=f_.ap, in0=sq_.ap,
                                                                      in1=gate.ap[:, b * 512:(b + 1) * 512]),
                                       [sq_, gate], [f_])
                                    dma("pool", MT, MT.ap[h, :, b * 512:(b + 1) * 512], f_, f_.ap, sbuf_side=f_)

                        kb.barrier()
                    with ExitStack() as es:
                        wst = [kb.sb(es, "wst", [128, KC, 128], F32) for _ in range(2)]
                        wbf = [kb.sb(es, "wbf", [128, KC, 128], BF16) for _ in range(5)]
                        QT0 = kb.sb(es, "QT0", [128, T], BF16)
                        QT1 = kb.sb(es, "QT1", [128, T], BF16)
                        QTs = [QT0, QT1]
                        KT = kb.sb(es, "KT", [128, T], BF16)
                        op("pool", lambda e: e.memset(QT0.ap, 0.0), writes=[QT0])
                        op("pool", lambda e: e.memset(QT1.ap, 0.0), writes=[QT1])
                        Vt = kb.sb(es, "Vt", [128, NT, 128], BF16)
                        cs = kb.sb(es, "cs", [128, T], F32)
                        sn = kb.sb(es, "sn", [128, T], F32)
                        dma("sp", cs, cs.ap, cosT, cosT.ap[:, 0:T])
                        dma("sp", sn, sn.ap, sinT, sinT.ap[:, 0:T])

                        def load_d(hh):
                            return (load_w(es, wst, wbf, 2560 + hh * 128, 0), load_w(es, wst, wbf, 4096 + hh * 128, 1),
                                    load_w(es, wst, wbf, 3072 + hh * 128, 2), load_w(es, wst, wbf, 4608 + hh * 128, 3),
                                    load_w(es, wst, wbf, 3584 + hh * 128, 4))

                        Wd = load_d(0)
                        for h in range(4):
                            wq_, wqs, wk_, wks, wv_ = Wd
                            with ExitStack() as es2:
                                pz = [kb.ps(es2, "pz", [128, 512]) for _ in range(4)]
                                pv = kb.ps(es2, "pv", [128, 128])
                                ta = [kb.sb(es2, "ta", [128, 512], F32) for _ in range(2)]
                                tb = [kb.sb(es2, "tb", [128, 512], F32) for _ in range(2)]
                                n2 = 0
                                for b in range(NB):
                                    for (wa_, ws_, dst) in ((wq_, wqs, None), (wk_, wks, KT)):
                                        pa = pz[(2 * n2) % 4]; pb = pz[(2 * n2 + 1) % 4]
                                        t1_ = ta[n2 % 2]; t2_ = tb[n2 % 2]; n2 += 1
                                        proj_fm(pa, wa_, b)
                                        proj_fm(pb, ws_, b)
                                        op("dve", lambda e: e.tensor_mul(out=t1_.ap, in0=pa.ap,
                                                                         in1=cs.ap[:, b * 512:(b + 1) * 512]),
                                           [pa, cs], [t1_])
                                        op("dve", lambda e: e.tensor_mul(out=t2_.ap, in0=pb.ap,
                                                                         in1=sn.ap[:, b * 512:(b + 1) * 512]),
                                           [pb, sn], [t2_])
                                        if dst is None:
                                            for c_ in range(2):
                                                rs_ = slice(c_ * 64, (c_ + 1) * 64)
                                                op("pool", lambda e, c_=c_, rs_=rs_: e.tensor_add(
                                                    out=QTs[c_].ap[rs_, b * 512:(b + 1) * 512],
                                                    in0=t1_.ap[rs_, :], in1=t2_.ap[rs_, :]), [t1_, t2_], [QTs[c_]])
                                        else:
                                            op("pool", lambda e: e.tensor_add(
                                                out=dst.ap[:, b * 512:(b + 1) * 512],
                                                in0=t1_.ap, in1=t2_.ap), [t1_, t2_], [dst])
                                    for it in range(b * 4, b * 4 + 4):
                                        proj_tm(pv, wv_, it)
                                        op("act", lambda e, it=it: e.copy(out=Vt.ap[:, it, :], in_=pv.ap), [pv],
                                           [Vt])
                                kb.barrier()
                            if h + 1 < 4:
                                Wd = load_d(h + 1)
                            with ExitStack() as es2:
                                pS = [kb.ps(es2, "pS", [128, 512]) for _ in range(3)]
                                pO = [kb.ps(es2, "pO", [128, 512]) for _ in range(2)]
                                pZ = [kb.ps(es2, "pZ", [128, 512]) for _ in range(2)]
                                pN = kb.ps(es2, "pN", [128, 512])
                                Pm = [kb.sb(es2, "Pm", [128, 512], BF16) for _ in range(5)]
                                r1 = kb.sb(es2, "r1", [128, 512], F32)
                                r2 = kb.sb(es2, "r2", [128, 512], F32)
                                oo = [kb.sb(es2, "oo", [128, 512], F32) for _ in range(2)]
                                sq2 = [kb.sb(es2, "sq2", [128, 512], F32) for _ in range(2)]
                                fin = [kb.sb(es2, "fin", [128, 512], BF16) for _ in range(2)]
                                LAG = 2
                                tiles = [(qb, kt_, c) for qb in range(NB) for kt_ in range(NT) for c in range(2)]

                                def stage1(n):
                                    qb, kt_, c = tiles[n]
                                    qsl = slice(qb * 512, (qb + 1) * 512)
                                    ksl = slice(kt_ * 128, (kt_ + 1) * 128)
                                    cross = (si == 0) and ((qb * 512) // TB != (kt_ * 128) // TB)
                                    bias_ap = flg.ap[:, 0:1] if cross else zero1.ap
                                    ps_ = pS[n % 3]; pm_ = Pm[n % 5]
                                    rs = slice(c * 64, (c + 1) * 64)
                                    op("pe", lambda e: e.matmul(ps_.ap, lhsT=KT.ap[:, ksl], rhs=QTs[c].ap[:, qsl],
                                                                start=True, stop=True), [KT, QTs[c]], [ps_])
                                    op("act", lambda e: e.activation(out=pm_.ap, in_=ps_.ap, func=AF.Exp,
                                                                     scale=0.125, bias=bias_ap),
                                       [ps_, flg, zero1], [pm_])

                                def stage2(n):
                                    qb, kt_, c = tiles[n]
                                    pm_ = Pm[n % 5]
                                    op("pe", lambda e: e.matmul(pO[c].ap, lhsT=Vt.ap[:, kt_, :], rhs=pm_.ap,
                                                                start=(kt_ == 0), stop=(kt_ == NT - 1)),
                                       [Vt, pm_], [pO[c]])
                                    op("pe", lambda e: e.matmul(pZ[c].ap, lhsT=onesb.ap, rhs=pm_.ap,
                                                                start=(kt_ == 0), stop=(kt_ == NT - 1)),
                                       [onesb, pm_], [pZ[c]])

                                def fin1(qb):
                                    o_ = oo[qb % 2]; s2_ = sq2[qb % 2]
                                    op("dve", lambda e: e.reciprocal(out=r1.ap, in_=pZ[0].ap), [pZ[0]], [r1])
                                    op("dve", lambda e: e.reciprocal(out=r2.ap, in_=pZ[1].ap), [pZ[1]], [r2])
                                    op("dve", lambda e: e.tensor_mul(out=r1.ap, in0=pO[0].ap, in1=r1.ap),
                                       [pO[0], r1], [r1])
                                    op("dve", lambda e: e.tensor_mul(out=r2.ap, in0=pO[1].ap, in1=r2.ap),
                                       [pO[1], r2], [r2])
                                    op("dve", lambda e: e.scalar_tensor_tensor(out=o_.ap, in0=r2.ap,
                                                                               scalar=nlam.ap[:, l:l + 1], in1=r1.ap,
                                                                               op0=ALU.mult, op1=ALU.add),
                                       [r1, r2, nlam], [o_])
                                    op("act", lambda e: e.activation(out=s2_.ap, in_=o_.ap, func=AF.Square), [o_],
                                       [s2_])

                                def fin2(qb):
                                    o_ = oo[qb % 2]; s2_ = sq2[qb % 2]
                                    qsl = slice(qb * 512, (qb + 1) * 512)
                                    op("pe", lambda e: e.matmul(pN.ap, lhsT=onesf.ap, rhs=s2_.ap, start=True,
                                                                stop=True), [onesf, s2_], [pN])
                                    op("act", lambda e: e.activation(out=s2_.ap, in_=pN.ap, func=AF.Ln,
                                                                     scale=1.0 / 128, bias=SUBLN_EPS), [pN], [s2_])
                                    op("act", lambda e: e.activation(out=s2_.ap, in_=s2_.ap, func=AF.Exp,
                                                                     scale=-0.5), [s2_], [s2_])
                                    f_ = fin[qb % 2]
                                    op("dve", lambda e: e.scalar_tensor_tensor(out=f_.ap, in0=o_.ap,
                                                                               scalar=swt.ap[:, l:l + 1],
                                                                               in1=s2_.ap, op0=ALU.mult,
                                                                               op1=ALU.mult), [o_, swt, s2_], [f_])
                                    dma("pool", MT, MT.ap[4 + h, :, qsl], f_, f_.ap, sbuf_side=f_)

                                NTI = len(tiles)
                                per_qb = NT * 2
                                pend = {}
                                for n in range(NTI + LAG):
                                    if n < NTI:
                                        stage1(n)
                                    m = n - LAG
                                    if m >= 0:
                                        stage2(m)
                                        if (m + 1) % per_qb == 0:
                                            qb_done = m // per_qb
                                            fin1(qb_done)
                                            pend[n + 4] = qb_done
                                    if n in pend:
                                        fin2(pend.pop(n))
                                for k_ in sorted(pend):
                                    fin2(pend[k_])
                                kb.barrier()
                    kb.barrier()

                with ExitStack() as es:
                    wo = kb.sb(es, "wo", [128, KC, D], BF16)
                    wd = kb.sb(es, "wd", [128, JC, D], BF16)
                    wgr = [kb.sb(es, "wgr", [128, KC, 128], BF16) for _ in range(3)]
                    wur = [kb.sb(es, "wur", [128, KC, 128], BF16) for _ in range(3)]
                    xt = [kb.sb(es, "xt", [128, D], F32) for _ in range(2)]
                    xm = [kb.sb(es, "xm", [128, D], F32) for _ in range(8)]
                    mxb = [kb.sb(es, "mxb", [128, KC, 512], BF16) for _ in range(1)]
                    h2T = [kb.sb(es, "h2T", [128, KC, 512], BF16) for _ in range(2)]
                    hid = kb.sb(es, "hid", [128, JC, 512], BF16)
                    hid_j = [TK(hid.ap, f"hid{j}") for j in range(JC)]
                    tf = kb.sb(es, "tf", [128, D], F32)
                    xn = [kb.sb(es, "xn", [128, D], BF16) for _ in range(2)]
                    ss = [kb.sb(es, "ss", [128, 1], F32) for _ in range(6)]
                    ea = [kb.sb(es, "ea", [128, 512], F32) for _ in range(2)]
                    eb = [kb.sb(es, "eb", [128, 512], F32) for _ in range(2)]
                    ptok = [kb.ps(es, "ptok", [128, D]) for _ in range(2)]
                    ptr = kb.ps(es, "ptr", [128, KC, 128], BF16)
                    pgu = [kb.ps(es, "pgu", [128, 512]) for _ in range(3)]
                    dma("sp", wo, wo.ap, wo_s, wo_s.ap)
                    st_ = {"nss": 0, "ngu": 0, "nt": 0, "nx": 0}

                    def pre_load(b):
                        mb = mxb[0]
                        for k in range(KC):
                            dma("sp", mb, mb.ap[:, k, :], MT, MT.ap[k, :, b * 512:(b + 1) * 512])

                    def pre_tile(b, tt):
                        mb = mxb[0]; h2 = h2T[b % 2]
                        sq = subseq(b * 512)
                        it = b * 4 + tt
                        x_ = xt[st_["nx"] % 2]; n_ = xn[st_["nx"] % 2]; st_["nx"] += 1
                        pt = ptok[st_["nt"] % 2]; st_["nt"] += 1
                        s1 = ss[st_["nss"] % 6]; s2 = ss[(st_["nss"] + 1) % 6]; st_["nss"] += 2
                        xm_ = xm[(b % 2) * 4 + tt]
                        dma("sp", x_, x_.ap, X, X.ap[it * 128:(it + 1) * 128, :])
                        for hf in range(2):
                            for k in range(KC):
                                op("pe", lambda e, k=k, hf=hf: e.matmul(
                                    pt.ap[:, hf * 512:(hf + 1) * 512], lhsT=mb.ap[:, k, tt * 128:(tt + 1) * 128],
                                    rhs=wo.ap[:, k, hf * 512:(hf + 1) * 512], start=(k == 0),
                                    stop=(k == KC - 1)), [mb, wo], [pt])
                        yield
                        op("act", lambda e: e.activation(out=n_.ap, in_=pt.ap, func=AF.Square,
                                                         accum_out=s1.ap), [pt], [n_, s1])
                        rstd_from_ss(s1, D, NORM_EPS)
                        yield
                        op("dve", lambda e: e.scalar_tensor_tensor(out=xm_.ap, in0=pt.ap, scalar=s1.ap,
                                                                   in1=Gbc.ap[:, 0, sq, :], op0=ALU.mult,
                                                                   op1=ALU.mult), [pt, s1, Gbc], [xm_])
                        op("pool", lambda e: e.tensor_add(out=xm_.ap, in0=xm_.ap, in1=x_.ap), [xm_, x_], [xm_])
                        yield
                        op("act", lambda e: e.activation(out=n_.ap, in_=xm_.ap, func=AF.Square,
                                                         accum_out=s2.ap), [xm_], [n_, s2])
                        rstd_from_ss(s2, D, NORM_EPS)
                        yield
                        op("dve", lambda e: e.tensor_scalar(out=n_.ap, in0=xm_.ap, scalar1=s2.ap, scalar2=None,
                                                             op0=ALU.mult), [xm_, s2], [n_])
                        yield
                        pre_tile2(b, tt, n_)

                    def pre_tile2(b, tt, n_):
                        h2 = h2T[b % 2]
                        sq = subseq(b * 512)
                        for k in range(KC):
                            op("pe", lambda e, k=k: e.transpose(ptr.ap[:, k, :], n_.ap[:, k * 128:(k + 1) * 128],
                                                                ident.ap), [n_, ident], [ptr])
                        for k in range(KC):
                            op("act", lambda e, k=k: e.activation(
                                out=h2.ap[:, k, tt * 128:(tt + 1) * 128], in_=ptr.ap[:, k, :], func=AF.Identity,
                                scale=modA.ap[:, 1, sq, k:k + 1], bias=modS.ap[:, 1, sq, k:k + 1]),
                               [ptr, modA, modS], [h2])

                    def ffn_j(b, j):
                        h2 = h2T[b % 2]
                        wg_t = wgr[j % 3]; wu_t = wur[j % 3]
                        dma("sp", wg_t, wg_t.ap, wg_s, wg_s.ap[j])
                        dma("sp", wu_t, wu_t.ap, wu_s, wu_s.ap[j])
                        pg = pgu[st_["ngu"] % 3]; pu = pgu[(st_["ngu"] + 1) % 3]; st_["ngu"] += 2
                        a_ = ea[j % 2]; b_ = eb[j % 2]
                        for k in range(KC):
                            op("pe", lambda e, k=k: e.matmul(pg.ap, lhsT=wg_t.ap[:, k, :], rhs=h2.ap[:, k, :],
                                                             start=(k == 0), stop=(k == KC - 1)), [wg_t, h2], [pg])
                        for k in range(KC):
                            op("pe", lambda e, k=k: e.matmul(pu.ap, lhsT=wu_t.ap[:, k, :], rhs=h2.ap[:, k, :],
                                                             start=(k == 0), stop=(k == KC - 1)), [wu_t, h2], [pu])
                        op("act", lambda e: e.activation(out=a_.ap, in_=pg.ap, func=AF.Exp, scale=-1.0), [pg], [a_])
                        op("act", lambda e: e.activation(out=b_.ap, in_=a_.ap, func=AF.Ln, bias=1.0), [a_], [b_])
                        op("act", lambda e: e.activation(out=a_.ap, in_=b_.ap, func=AF.Exp, scale=-1.0), [b_], [a_])
                        op("dve", lambda e: e.tensor_mul(out=b_.ap, in0=pg.ap, in1=a_.ap), [pg, a_], [b_])
                        op("dve", lambda e: e.tensor_mul(out=hid.ap[:, j, :], in0=pu.ap, in1=b_.ap),
                           [pu, b_], [hid_j[j]])

                    def down_tile(b, tt):
                        sq = subseq(b * 512)
                        it = b * 4 + tt
                        pt = ptok[st_["nt"] % 2]; st_["nt"] += 1
                        s1 = ss[st_["nss"] % 6]; st_["nss"] += 1
                        xm_ = xm[(b % 2) * 4 + tt]
                        for hf in range(2):
                            for j in range(JC):
                                op("pe", lambda e, j=j, hf=hf: e.matmul(
                                    pt.ap[:, hf * 512:(hf + 1) * 512], lhsT=hid.ap[:, j, tt * 128:(tt + 1) * 128],
                                    rhs=wd.ap[:, j, hf * 512:(hf + 1) * 512], start=(j == 0),
                                    stop=(j == JC - 1)), [hid_j[j], wd], [pt])
                        op("act", lambda e: e.activation(out=tf.ap, in_=pt.ap, func=AF.Square,
                                                         accum_out=s1.ap), [pt], [tf, s1])
                        rstd_from_ss(s1, D, NORM_EPS)
                        op("dve", lambda e: e.scalar_tensor_tensor(out=tf.ap, in0=pt.ap, scalar=s1.ap,
                                                                   in1=Gbc.ap[:, 1, sq, :], op0=ALU.mult,
                                                                   op1=ALU.mult), [pt, s1, Gbc], [tf])
                        op("pool", lambda e: e.tensor_add(out=xm_.ap, in0=tf.ap, in1=xm_.ap), [tf, xm_], [xm_])
                        dma("pool", XO, XO.ap[it * 128:(it + 1) * 128, :], xm_, xm_.ap, sbuf_side=xm_)

                    pre_load(0)
                    for tt in range(4):
                        for _ in pre_tile(0, tt):
                            pass
                    for j in range(JC):
                        dma("sp", wd, wd.ap[:, j, :], wd_s, wd_s.ap[j])
                    for b in range(NB):
                        if b + 1 < NB:
                            pre_load(b + 1)
                        gens = []
                        for j in range(JC):
                            ffn_j(b, j)
                            if b + 1 < NB and j % 5 == 0 and j // 5 < 4:
                                gens.append(pre_tile(b + 1, j // 5))
                            for gg in gens:
                                next(gg, None)
                        for gg in gens:
                            for _ in gg:
                                pass
                        for tt in range(4):
                            down_tile(b, tt)
                    kb.barrier()
        kb.barrier()
    return nc, kb


_PROG = {}


def _rot_tables(TA, restart):
    pos = np.arange(TA, dtype=np.float32)
    if restart:
        pos = np.where(np.arange(TA) >= TA // 2, pos - np.float32(TA // 2), pos).astype(np.float32)
    inv = (1.0 / (np.float32(10000.0) ** (np.arange(0, 64, 2, dtype=np.float32) / np.float32(64)))).astype(np.float32)
    ang = (pos[None, :] * inv[:, None]).astype(np.float32)
    c = np.cos(ang).astype(np.float32)
    s = np.sin(ang).astype(np.float32)
    cosT = np.zeros((128, TA), np.float32)
    sinT = np.zeros((128, TA), np.float32)
    for p in range(128):
        j = p % 64
        i = j % 32
        cosT[p] = c[i]
        sinT[p] = -s[i] if j < 32 else s[i]
    return cosT, sinT


def kernel(x_prompt, x_sample, c_prompt, c_sample, w_ada, b_ada, norm_pre_mix, norm_post_mix,
           norm_pre_ffn, norm_post_ffn, w_in, hg_lower_bounds, hg_gnorm, da_lambda_q1,
           da_lambda_k1, da_lambda_q2, da_lambda_k2, da_subln, w_out, w_ffn_gate, w_ffn_up,
           w_ffn_down):
    f32 = np.float32
    x_prompt = np.asarray(x_prompt, f32); x_sample = np.asarray(x_sample, f32)
    c_prompt = np.asarray(c_prompt, f32); c_sample = np.asarray(c_sample, f32)
    TA = x_prompt.shape[1]; TB = x_sample.shape[1]
    assert x_prompt.shape[0] == 4 and x_sample.shape[0] == 16
    key = (TA, TB)
    if key not in _PROG:
        _PROG[key] = build_program(TA, TB)
    nc = _PROG[key]
    w_in = np.asarray(w_in, f32)
    perm = np.arange(512).reshape(8, 2, 32)[:, ::-1, :].reshape(512)
    w_in_ext = np.ascontiguousarray(np.concatenate(
        [w_in, w_in[:, :, 2560 + perm], w_in[:, :, 3072 + perm]], axis=2))
    b_ada = np.asarray(b_ada, f32)
    b_adaT = np.ascontiguousarray(b_ada.reshape(DEPTH, 6, KC, 128).transpose(3, 0, 1, 2))
    b_adaR = np.ascontiguousarray(np.broadcast_to(b_ada[None], (4, DEPTH, 6 * D)))
    npre = np.stack([np.asarray(norm_pre_mix, f32), np.asarray(norm_pre_ffn, f32)], axis=1)
    npreT = np.ascontiguousarray(npre.reshape(DEPTH, 2, KC, 128).transpose(3, 0, 1, 2))
    npost = np.stack([np.asarray(norm_post_mix, f32), np.asarray(norm_post_ffn, f32)], axis=1)
    npostR = np.ascontiguousarray(np.broadcast_to(npost[None], (4, DEPTH, 2, D)))
    hlb = np.asarray(hg_lower_bounds, f32)
    hlbT = np.ascontiguousarray(hlb.reshape(2, DEPTH, 4, 128).transpose(3, 0, 1, 2))
    gnT = np.ascontiguousarray(np.asarray(hg_gnorm, f32).T)
    swT = np.ascontiguousarray(np.asarray(da_subln, f32).T)
    lam = np.stack([np.asarray(a, f32) for a in (da_lambda_q1, da_lambda_k1, da_lambda_q2, da_lambda_k2)], axis=0)
    lamB = np.ascontiguousarray(np.broadcast_to(lam[None], (128, 4, DEPTH, 64)))
    ident = np.eye(128, dtype=f32).astype(ml_dtypes.bfloat16)
    onesf = np.ones((128, 128), f32)
    onesb = np.ones((128, 128), f32).astype(ml_dtypes.bfloat16)
    sel = np.zeros((4, 384), f32)
    for s in range(3):
        sel[s, s * 128:(s + 1) * 128] = 1.0
    t = np.arange(512)
    rmask = np.ones((128, 2, 512), f32)
    rmask[:, 0, t % 64 == 0] = 0.0
    rmask[:, 1, t % 64 == 63] = 0.0
    s_ = np.arange(128)[:, None]; t_ = np.arange(128)[None, :]
    same = (s_ // 64) == (t_ // 64)
    tmask = np.zeros((128, 2, 128), np.uint32)
    tmask[:, 0, :] = (same & (s_ <= t_)).astype(np.uint32)
    tmask[:, 1, :] = (same & (s_ >= t_)).astype(np.uint32)
    tabs = [_rot_tables(TA, False), _rot_tables(TA, True)]
    common = dict(w_ada=np.asarray(w_ada, f32), b_adaT=b_adaT, b_adaR=b_adaR, npreT=npreT, npostR=npostR,
                  w_in=w_in_ext, hlbT=hlbT, gnT=gnT, swT=swT, lamB=lamB, w_out=np.asarray(w_out, f32),
                  w_g=np.asarray(w_ffn_gate, f32), w_u=np.asarray(w_ffn_up, f32), w_d=np.asarray(w_ffn_down, f32),
                  ident=ident, onesf=onesf, onesb=onesb, sel=sel, rmask=rmask, tmask=tmask)
    in_maps = []
    for c in range(8):
        if c < 4:
            xa = x_prompt[c]; xb = x_sample[c]
            cs = [c_prompt[c], c_prompt[c], c_sample[c]]
            fl = (0.0, 1.0); tb = tabs[0]
        else:
            k = 4 + 3 * (c - 4)
            xa = np.concatenate([x_sample[k], x_sample[k + 1]], axis=0); xb = x_sample[k + 2]
            cs = [c_sample[k], c_sample[k + 1], c_sample[k + 2]]
            fl = (NEG_BIG, 0.0); tb = tabs[1]
        c3 = np.zeros((4, D), f32)
        c3[:3] = np.stack(cs)
        c3T = np.ascontiguousarray(c3.reshape(4, KC, 128).transpose(2, 1, 0))
        flags = np.zeros((128, 2), f32); flags[:, 0] = fl[0]; flags[:, 1] = fl[1]
        m = dict(common)
        m.update(xa=np.ascontiguousarray(xa), xb=np.ascontiguousarray(xb), c3T=c3T, cosT=tb[0], sinT=tb[1],
                 flags=flags)
        in_maps.append(m)
    res = run_bass_kernel_spmd(nc, in_maps, core_ids=list(range(8)))
    if DEBUG:
        global _DBG
        _DBG = res.results
    y_prompt = np.zeros_like(x_prompt); y_sample = np.zeros_like(x_sample)
    for c in range(8):
        r = res.results[c]
        if c < 4:
            y_prompt[c] = r["ya"]; y_sample[c] = r["yb"]
        else:
            k = 4 + 3 * (c - 4)
            y_sample[k] = r["ya"][:TB]; y_sample[k + 1] = r["ya"][TB:]; y_sample[k + 2] = r["yb"]
    return (y_prompt, y_sample)
```

```python
import numpy as np
import ml_dtypes
from contextlib import ExitStack
import concourse.bass as bass
import concourse.mybir as mybir
from concourse.bass_utils import run_bass_kernel_spmd

F32 = mybir.dt.float32
BF16 = mybir.dt.bfloat16
U32 = mybir.dt.uint32
AF = mybir.ActivationFunctionType
ALU = mybir.AluOpType

D = 1024
KC = 8
HID = 2816
JC = 22
DEPTH = 2
NCOL = 5120
NEG_BIG = -30000.0
NORM_EPS = 1e-6
SUBLN_EPS = 1e-5
SEM_LIMIT = 30000
import os
DEBUG = bool(int(os.environ.get("MK_DEBUG", "0")))
NLAYERS = int(os.environ.get("MK_LAYERS", str(DEPTH)))


class TK:
    __slots__ = ("ap", "w", "r", "dsem", "dcnt", "name", "dsem_sw")

    def __init__(self, ap, name):
        self.ap = ap
        self.w = None
        self.r = {}
        self.dsem = None
        self.dcnt = 0
        self.name = name


class KB:
    def __init__(self, nc, needed=None):
        self.nc = nc
        self.needed = needed
        self.rec = {k: set() for k in ("pe", "act", "dve", "pool", "sp")}
        self.idx = {k: 0 for k in ("pe", "act", "dve", "pool", "sp")}
        self.tokmap = {k: {} for k in ("pe", "act", "dve", "pool", "sp")}
        self.emitted = {k: [] for k in ("pe", "act", "dve", "pool", "sp")}
        self.seen_idx = {k: {} for k in ("pe", "act", "dve", "pool", "sp")}
        self.eng = {"pe": nc.tensor, "act": nc.scalar, "dve": nc.vector, "pool": nc.gpsimd, "sp": nc.sync}
        self.sem = {}
        self.cnt = {}
        self.seen = {k: {} for k in self.eng}
        self.nsem = 0
        for k in self.eng:
            self._newsem(k)
        self.uid = 0
        self.dma_toks = {}
        self.free_dsems = []
        self.free_dsems_sw = []
        self.dval = {}
        self.dsem_tks = []

    def _newsem(self, k):
        self.sem[k] = self.nc.alloc_semaphore(name=f"s_{k}_{self.nsem}")
        self.nsem += 1
        self.cnt[k] = 0

    def name(self, base):
        self.uid += 1
        return f"{base}_{self.uid}"

    def sb(self, es, base, shape, dtype):
        t = es.enter_context(self.nc.sbuf_tensor(self.name(base), list(shape), dtype))
        return TK(t.ap(), base)

    def ps(self, es, base, shape, dtype=F32):
        t = es.enter_context(self.nc.psum_tensor(self.name(base), list(shape), dtype))
        return TK(t.ap(), base)

    def dram(self, base, shape, dtype, kind="Internal"):
        t = self.nc.dram_tensor(base, list(shape), dtype, kind=kind)
        return TK(t.ap(), base)

    def _wait(self, e, toks):
        import bisect
        need = {}
        for tok in toks:
            if tok is None:
                continue
            if tok[0] == "E":
                _, src, i = tok
                if src == e and e == "pe":
                    continue
                if self.needed is None:
                    if self.seen_idx[e].get(src, 0) >= i:
                        continue
                    self.seen_idx[e][src] = i
                    self.rec[src].add(i)
                    continue
                lst = self.emitted[src]
                p = bisect.bisect_left(lst, i)
                sem, val = self.tokmap[src][lst[p]]
            else:
                sem, val, src = tok
            key = sem.num
            if self.seen[e].get(key, 0) >= val:
                continue
            if key not in need or need[key][1] < val:
                need[key] = (sem, val)
        for key, (sem, val) in need.items():
            self.eng[e].wait_ge(sem, val)
            self.seen[e][key] = val

    def _wait_all(self, e, toks):
        self._wait(e, toks)

    def _deps(self, reads, writes):
        toks = []
        for t in reads:
            toks.append(t.w)
        for t in writes:
            toks.append(t.w)
            toks.extend(t.r.values())
        return toks

    def op(self, e, fn, reads=(), writes=()):
        self._wait(e, self._deps(reads, writes))
        ins = fn(self.eng[e])
        self.idx[e] += 1
        i = self.idx[e]
        if self.needed is not None and i in self.needed[e]:
            if self.cnt[e] >= SEM_LIMIT:
                self._newsem(e)
            self.cnt[e] += 1
            ins.then_inc(self.sem[e], 1)
            self.tokmap[e][i] = (self.sem[e], self.cnt[e])
            self.emitted[e].append(i)
        tok = ("E", e, i)
        for t in reads:
            t.r[("E", e)] = tok
        for t in writes:
            t.w = tok
            t.r = {}
        return ins

    def dma(self, q, out_tk, out_ap, in_tk, in_ap, sbuf_side=None, **kw):
        self._wait(q, self._deps([in_tk], [out_tk]))
        st = sbuf_side if sbuf_side is not None else out_tk
        if st.dsem is None:
            pool_ = self.free_dsems_sw if q == "pool" else self.free_dsems
            st.dsem_sw = (q == "pool")
            if pool_:
                st.dsem = pool_.pop()
            else:
                st.dsem = self.nc.alloc_semaphore(name=f"d_{self.nsem}")
                self.nsem += 1
                self.dval[st.dsem.num] = 0
            self.dsem_tks.append(st)
        ins = self.eng[q].dma_start(out=out_ap, in_=in_ap, **kw)
        self.dval[st.dsem.num] += 16
        ins.then_inc(st.dsem, 16)
        tok = (st.dsem, self.dval[st.dsem.num], "dma")
        in_tk.r[("D", tok[0].num)] = tok
        out_tk.w = tok
        out_tk.r = {}
        self.dma_toks[tok[0].num] = tok
        return tok

    def barrier(self):
        toks = [("E", k, self.idx[k]) for k in self.eng if self.idx[k] > 0]
        toks += list(self.dma_toks.values())
        for e in self.eng:
            self._wait_all(e, toks)
        self.dma_toks = {}
        for t in self.dsem_tks:
            if self.dval[t.dsem.num] >= SEM_LIMIT:
                t.dsem = None
                continue
            (self.free_dsems_sw if getattr(t, "dsem_sw", False) else self.free_dsems).append(t.dsem)
            t.dsem = None
        self.dsem_tks = []


def build_program(TA, TB):
    nc1, kb1 = _build(TA, TB, None)
    needed = {k: v for k, v in kb1.rec.items()}
    nc2, kb2 = _build(TA, TB, needed)
    return nc2


def _build(TA, TB, needed):
    assert TA == 2 * TB and TB % 512 == 0
    nc = bass.Bass("TRN2", target_bir_lowering=False)
    kb = KB(nc, needed)
    op, dma = kb.op, kb.dma

    def din(name, shape, dt=F32):
        return kb.dram(name, shape, dt, kind="ExternalInput")

    xa = din("xa", [TA, D]); xb = din("xb", [TB, D])
    c3T = din("c3T", [128, KC, 4])
    w_ada = din("w_ada", [DEPTH, D, 6 * D]); b_adaT = din("b_adaT", [128, DEPTH, 6, KC])
    b_adaR = din("b_adaR", [4, DEPTH, 6 * D])
    npreT = din("npreT", [128, DEPTH, 2, KC])
    npostR = din("npostR", [4, DEPTH, 2, D])
    w_in = din("w_in", [DEPTH, D, NCOL])
    hlbT = din("hlbT", [128, 2, DEPTH, 4])
    gnT = din("gnT", [128, DEPTH]); swT = din("swT", [128, DEPTH])
    lamB = din("lamB", [128, 4, DEPTH, 64])
    w_out = din("w_out", [DEPTH, D, D])
    w_g = din("w_g", [DEPTH, D, HID]); w_u = din("w_u", [DEPTH, D, HID]); w_d = din("w_d", [DEPTH, HID, D])
    cosT = din("cosT", [128, TA]); sinT = din("sinT", [128, TA])
    flags = din("flags", [128, 2])
    ident_d = din("ident", [128, 128], BF16)
    onesf_d = din("onesf", [128, 128])
    onesb_d = din("onesb", [128, 128], BF16)
    sel_d = din("sel", [4, 3 * 128])
    rmask_d = din("rmask", [128, 2, 512])
    tmask_d = din("tmask", [128, 2, 128], U32)
    ya = kb.dram("ya", [TA, D], F32, kind="ExternalOutput")
    yb = kb.dram("yb", [TB, D], F32, kind="ExternalOutput")
    dk_ = "ExternalOutput" if DEBUG else "Internal"
    x1a = kb.dram("x1a", [TA, D], F32, kind=dk_); x1b = kb.dram("x1b", [TB, D], F32, kind=dk_)
    mixTa = kb.dram("mixTa", [KC, 128, TA], BF16, kind=dk_); mixTb = kb.dram("mixTb", [KC, 128, TB], BF16, kind=dk_)
    wo_s = kb.dram("wo_s", [128, KC, D], BF16)
    wg_s = kb.dram("wg_s", [JC, 128, KC, 128], BF16); wu_s = kb.dram("wu_s", [JC, 128, KC, 128], BF16)
    wd_s = kb.dram("wd_s", [JC, 128, D], BF16)

    with ExitStack() as g:
        ident = kb.sb(g, "ident", [128, 128], BF16)
        onesf = kb.sb(g, "onesf", [128, 128], F32)
        onesb = kb.sb(g, "onesb", [128, 128], BF16)
        sel = kb.sb(g, "sel", [4, 384], F32)
        rmask = kb.sb(g, "rmask", [128, 2, 512], F32)
        tmask = kb.sb(g, "tmask", [128, 2, 128], U32)
        flg = kb.sb(g, "flg", [128, 2], F32)
        zero1 = kb.sb(g, "zero1", [128, 1], F32)
        sc = kb.sb(g, "sc", [128, KC, 4], F32)
        modA = kb.sb(g, "modA", [128, 2, 3, KC], F32)
        modS = kb.sb(g, "modS", [128, 2, 3, KC], F32)
        Gbc = kb.sb(g, "Gbc", [128, 2, 3, D], F32)
        lbt = kb.sb(g, "lbt", [128, 2, DEPTH, 4], F32)
        omlt = kb.sb(g, "omlt", [128, 2, DEPTH, 4], F32)
        gnt = kb.sb(g, "gnt", [128, DEPTH], F32)
        swt = kb.sb(g, "swt", [128, DEPTH], F32)
        nlam = kb.sb(g, "nlam", [128, DEPTH], F32)
        for (t, d_) in ((ident, ident_d), (onesf, onesf_d), (onesb, onesb_d), (sel, sel_d), (rmask, rmask_d),
                        (tmask, tmask_d), (flg, flags), (lbt, hlbT), (gnt, gnT), (swt, swT)):
            dma("sp", t, t.ap, d_, d_.ap)
        op("dve", lambda e: e.memset(zero1.ap, 0.0), writes=[zero1])

        with ExitStack() as es:
            ct = kb.sb(es, "ct", [128, KC, 4], F32)
            t1 = kb.sb(es, "t1", [128, KC, 4], F32)
            dma("sp", ct, ct.ap, c3T, c3T.ap)
            op("act", lambda e: e.activation(out=t1.ap, in_=ct.ap, func=AF.Exp, scale=-1.0), [ct], [t1])
            op("dve", lambda e: e.tensor_scalar_add(out=t1.ap, in0=t1.ap, scalar1=1.0), [t1], [t1])
            op("dve", lambda e: e.reciprocal(out=t1.ap, in_=t1.ap), [t1], [t1])
            op("dve", lambda e: e.tensor_mul(out=sc.ap, in0=ct.ap, in1=t1.ap), [ct, t1], [sc])
            e0 = kb.sb(es, "e0", [128, 2, DEPTH, 4], F32)
            tot = kb.sb(es, "tot", [128, 2, 4], F32)
            op("act", lambda e: e.activation(out=e0.ap, in_=lbt.ap, func=AF.Exp), [lbt], [e0])
            op("dve", lambda e: e.tensor_add(out=tot.ap, in0=e0.ap[:, :, 0, :], in1=e0.ap[:, :, 1, :]), [e0], [tot])
            op("dve", lambda e: e.reciprocal(out=tot.ap, in_=tot.ap), [tot], [tot])
            for l in range(DEPTH):
                op("dve", lambda e, l=l: e.tensor_mul(out=e0.ap[:, :, l, :], in0=e0.ap[:, :, l, :], in1=tot.ap),
                   [e0, tot], [e0])
            op("dve", lambda e: e.tensor_sub(out=lbt.ap[:, :, 0, :], in0=e0.ap[:, :, 0, :], in1=e0.ap[:, :, 0, :]),
               [e0], [lbt])
            op("dve", lambda e: e.tensor_add(out=tot.ap, in0=e0.ap[:, :, 0, :], in1=e0.ap[:, :, 1, :]), [e0], [tot])
            op("dve", lambda e: e.tensor_sub(out=lbt.ap[:, :, 1, :], in0=tot.ap, in1=e0.ap[:, :, 0, :]),
               [tot, e0, lbt], [lbt])
            op("dve", lambda e: e.tensor_scalar(out=omlt.ap, in0=lbt.ap, scalar1=-1.0, scalar2=1.0,
                                                 op0=ALU.mult, op1=ALU.add), [lbt], [omlt])
            lm = kb.sb(es, "lm", [128, 4, DEPTH, 64], F32)
            pr = kb.sb(es, "pr", [128, 2, DEPTH, 64], F32)
            sm = kb.sb(es, "sm", [128, 2, DEPTH], F32)
            dma("sp", lm, lm.ap, lamB, lamB.ap)
            op("dve", lambda e: e.tensor_mul(out=pr.ap[:, 0], in0=lm.ap[:, 0], in1=lm.ap[:, 1]), [lm], [pr])
            op("dve", lambda e: e.tensor_mul(out=pr.ap[:, 1], in0=lm.ap[:, 2], in1=lm.ap[:, 3]), [lm, pr], [pr])
            op("dve", lambda e: e.reduce_sum(out=sm.ap, in_=pr.ap, axis=mybir.AxisListType.X), [pr], [sm])
            op("act", lambda e: e.activation(out=sm.ap, in_=sm.ap, func=AF.Exp), [sm], [sm])
            for l in range(DEPTH):
                lam_init = 0.8 - 0.6 * float(np.exp(-0.3 * l))
                op("dve", lambda e, l=l: e.tensor_sub(out=nlam.ap[:, l:l + 1], in0=sm.ap[:, 1, l:l + 1],
                                                      in1=sm.ap[:, 0, l:l + 1]), [sm, nlam], [nlam])
                op("dve", lambda e, l=l, li=lam_init: e.tensor_scalar_add(out=nlam.ap[:, l:l + 1],
                                                                          in0=nlam.ap[:, l:l + 1], scalar1=-li),
                   [nlam], [nlam])
                op("dve", lambda e, l=l, li=lam_init: e.tensor_scalar_mul(out=swt.ap[:, l:l + 1],
                                                                          in0=swt.ap[:, l:l + 1], scalar1=1.0 - li),
                   [swt], [swt])
            kb.barrier()

        segs = [(0, TA), (1, TB)]

        def rstd_from_ss(ss, n, eps):
            op("dve", lambda e: e.tensor_scalar(out=ss.ap, in0=ss.ap, scalar1=1.0 / n, scalar2=eps,
                                                 op0=ALU.mult, op1=ALU.add), [ss], [ss])
            op("act", lambda e: e.activation(out=ss.ap, in_=ss.ap, func=AF.Ln), [ss], [ss])
            op("act", lambda e: e.activation(out=ss.ap, in_=ss.ap, func=AF.Exp, scale=-0.5), [ss], [ss])

        for l in range(NLAYERS):
            xin = [xa, xb] if l == 0 else [x1a, x1b]
            xout = [x1a, x1b] if l == 0 else [ya, yb]
            if l == DEPTH - 1:
                xout = [ya, yb]
            mixTs = [mixTa, mixTb]
            with ExitStack() as es:
                wa = [kb.sb(es, "wa", [128, KC, D], F32) for _ in range(2)]
                badT = kb.sb(es, "badT", [128, 6, KC], F32)
                badR = kb.sb(es, "badR", [4, 6 * D], F32)
                npT = kb.sb(es, "npT", [128, 2, KC], F32)
                npR = kb.sb(es, "npR", [4, 2, D], F32)
                grow = kb.sb(es, "grow", [4, D], F32)
                pm = kb.ps(es, "pm", [128, KC, 4])
                pr_ = [kb.ps(es, "prw", [128, 512]) for _ in range(2)]
                dma("sp", badT, badT.ap, b_adaT, b_adaT.ap[:, l])
                dma("sp", badR, badR.ap, b_adaR, b_adaR.ap[:, l])
                dma("sp", npT, npT.ap, npreT, npreT.ap[:, l])
                dma("sp", npR, npR.ap, npostR, npostR.ap[:, l])
                for part in range(6):
                    w = wa[part % 2]
                    src = w_ada.ap[l, :, part * D:(part + 1) * D].rearrange("(k p) n -> p k n", p=128)
                    for k in range(KC):
                        dma("sp", w, w.ap[:, k, :], w_ada, src[:, k, :])
                    which = 0 if part < 3 else 1
                    kind = part % 3
                    if kind < 2:
                        for m in range(KC):
                            for k in range(KC):
                                op("pe", lambda e, m=m, k=k, w=w: e.matmul(
                                    pm.ap[:, m, :], lhsT=w.ap[:, k, m * 128:(m + 1) * 128], rhs=sc.ap[:, k, :],
                                    start=(k == 0), stop=(k == KC - 1)), [w, sc], [pm])
                        for s in range(3):
                            if kind == 0:
                                op("dve", lambda e, s=s, which=which, part=part: e.tensor_add(
                                    out=modS.ap[:, which, s, :], in0=pm.ap[:, :, s], in1=badT.ap[:, part, :]),
                                   [pm, badT, modS], [modS])
                            else:
                                op("dve", lambda e, s=s, which=which, part=part: e.tensor_add(
                                    out=modA.ap[:, which, s, :], in0=pm.ap[:, :, s], in1=badT.ap[:, part, :]),
                                   [pm, badT, modA], [modA])
                                op("dve", lambda e, s=s, which=which: e.tensor_scalar_add(
                                    out=modA.ap[:, which, s, :], in0=modA.ap[:, which, s, :], scalar1=1.0),
                                   [modA], [modA])
                                op("dve", lambda e, s=s, which=which: e.tensor_mul(
                                    out=modA.ap[:, which, s, :], in0=modA.ap[:, which, s, :],
                                    in1=npT.ap[:, which, :]), [modA, npT], [modA])
                    else:
                        for hf in range(2):
                            p_ = pr_[hf]
                            for k in range(KC):
                                op("pe", lambda e, k=k, hf=hf, p_=p_, w=w: e.matmul(
                                    p_.ap[0:4, :], lhsT=sc.ap[:, k, :], rhs=w.ap[:, k, hf * 512:(hf + 1) * 512],
                                    start=(k == 0), stop=(k == KC - 1)), [w, sc], [p_])
                            op("dve", lambda e, hf=hf, p_=p_, part=part: e.tensor_add(
                                out=grow.ap[:, hf * 512:(hf + 1) * 512], in0=p_.ap[0:4, :],
                                in1=badR.ap[:, part * D + hf * 512: part * D + (hf + 1) * 512]),
                               [p_, badR, grow], [grow])
                        op("dve", lambda e, which=which: e.tensor_mul(out=grow.ap, in0=grow.ap,
                                                                      in1=npR.ap[:, which, :]), [grow, npR], [grow])
                        for s in range(3):
                            for hf in range(2):
                                p_ = pr_[hf]
                                op("pe", lambda e, s=s, hf=hf, p_=p_: e.matmul(
                                    p_.ap, lhsT=sel.ap[:, s * 128:(s + 1) * 128],
                                    rhs=grow.ap[:, hf * 512:(hf + 1) * 512], start=True, stop=True),
                                   [sel, grow], [p_])
                                op("dve", lambda e, s=s, hf=hf, p_=p_, which=which: e.tensor_copy(
                                    out=Gbc.ap[:, which, s, hf * 512:(hf + 1) * 512], in_=p_.ap), [p_, Gbc], [Gbc])
                kb.barrier()
            cast_jobs = []
            for m in range(KC):
                cast_jobs.append((w_out, w_out.ap[l, :, m * 128:(m + 1) * 128].rearrange("(k p) n -> p k n", p=128),
                                  wo_s, wo_s.ap[:, :, m * 128:(m + 1) * 128]))
            for j in range(JC):
                cast_jobs.append((w_g, w_g.ap[l, :, j * 128:(j + 1) * 128].rearrange("(k p) n -> p k n", p=128),
                                  wg_s, wg_s.ap[j]))
                cast_jobs.append((w_u, w_u.ap[l, :, j * 128:(j + 1) * 128].rearrange("(k p) n -> p k n", p=128),
                                  wu_s, wu_s.ap[j]))
                cast_jobs.append((w_d, w_d.ap[l, j * 128:(j + 1) * 128, :].rearrange("p (k n) -> p k n", k=KC),
                                  wd_s, wd_s.ap[j].rearrange("p (k n) -> p k n", k=KC)))

            for (si, T) in segs:
                NT = T // 128
                NB = T // 512
                X = xin[si]
                XO = xout[si]
                MT = mixTs[si]

                def subseq(tok):
                    return (tok // TB) if si == 0 else 2

                with ExitStack() as segs_es:
                    hT = kb.sb(segs_es, "hT", [128, KC, T], BF16)
                    hT_blk = [TK(hT.ap, f"hTb{b}") for b in range(NB)]
                    with ExitStack() as es:
                        xt = [kb.sb(es, "xt", [128, D], F32) for _ in range(3)]
                        junk = kb.sb(es, "junk", [128, D], BF16)
                        xn = [kb.sb(es, "xn", [128, D], BF16) for _ in range(2)]
                        ss = [kb.sb(es, "ss", [128, 1], F32) for _ in range(2)]
                        ptr = [kb.ps(es, "ptr", [128, KC, 128], BF16) for _ in range(2)]
                        stg = [kb.sb(es, "stg", [128, KC, 128], F32) for _ in range(4)]
                        stb = [kb.sb(es, "stb", [128, KC, 128], BF16) for _ in range(4)]
                        ncast = [0]

                        def cast_some(k_):
                            for _ in range(k_):
                                if not cast_jobs:
                                    return
                                src_tk, src_ap, dst_tk, dst_ap = cast_jobs.pop(0)
                                i_ = ncast[0] % 4
                                ncast[0] += 1
                                dma("sp", stg[i_], stg[i_].ap, src_tk, src_ap)
                                op("pool", lambda e: e.tensor_copy(out=stb[i_].ap, in_=stg[i_].ap), [stg[i_]],
                                   [stb[i_]])
                                dma("pool", dst_tk, dst_ap, stb[i_], stb[i_].ap, sbuf_side=stb[i_])

                        per_tile = -(-len(cast_jobs) // NT) if si == 0 else 0
                        for it in range(NT):
                            cast_some(per_tile)
                            x_ = xt[it % 3]; s_ = ss[it % 2]; n_ = xn[it % 2]; p_ = ptr[it % 2]
                            hb = hT_blk[it // 4]
                            sq = subseq(it * 128)
                            dma("sp", x_, x_.ap, X, X.ap[it * 128:(it + 1) * 128, :])
                            op("act", lambda e: e.activation(out=junk.ap, in_=x_.ap, func=AF.Square,
                                                             accum_out=s_.ap), [x_], [junk, s_])
                            rstd_from_ss(s_, D, NORM_EPS)
                            op("dve", lambda e: e.tensor_scalar(out=n_.ap, in0=x_.ap, scalar1=s_.ap, scalar2=None,
                                                                 op0=ALU.mult), [x_, s_], [n_])
                            for k in range(KC):
                                op("pe", lambda e, k=k: e.transpose(p_.ap[:, k, :], n_.ap[:, k * 128:(k + 1) * 128],
                                                                    ident.ap), [n_, ident], [p_])
                            for k in range(KC):
                                op("act", lambda e, k=k: e.activation(
                                    out=hT.ap[:, k, it * 128:(it + 1) * 128], in_=p_.ap[:, k, :], func=AF.Identity,
                                    scale=modA.ap[:, 0, sq, k:k + 1], bias=modS.ap[:, 0, sq, k:k + 1]),
                                   [p_, modA, modS], [hb])
                        cast_some(len(cast_jobs))
                        kb.barrier()

                    def load_w(es_w, wst, wbf, col, n_i):
                        i = n_i % len(wst)
                        src = w_in.ap[l, :, col:col + 128].rearrange("(k p) n -> p k n", p=128)
                        dma("sp", wst[i], wst[i].ap, w_in, src)
                        wt = wbf[n_i % len(wbf)]
                        op("pool", lambda e: e.tensor_copy(out=wt.ap, in_=wst[i].ap), [wst[i]], [wt])
                        return wt

                    def proj_fm(pz, wt, b):
                        for k in range(KC):
                            op("pe", lambda e, k=k: e.matmul(pz.ap, lhsT=wt.ap[:, k, :],
                                                             rhs=hT.ap[:, k, b * 512:(b + 1) * 512],
                                                             start=(k == 0), stop=(k == KC - 1)),
                               [wt, hT_blk[b]], [pz])

                    def proj_tm(pz, wt, it):
                        for k in range(KC):
                            op("pe", lambda e, k=k: e.matmul(pz.ap, lhsT=hT.ap[:, k, it * 128:(it + 1) * 128],
                                                             rhs=wt.ap[:, k, :],
                                                             start=(k == 0), stop=(k == KC - 1)),
                               [wt, hT_blk[it // 4]], [pz])

                    with ExitStack() as es:
                        for h in range(4):
                            if h == 0:
                                NCH = T // 64
                                wst = [kb.sb(es, "wst", [128, KC, 128], F32) for _ in range(2)]
                                wbf = [kb.sb(es, "wbf", [128, KC, 128], BF16) for _ in range(5)]
                                qf = kb.sb(es, "qf", [128, T], F32)
                                qf_b = [TK(qf.ap, f"qfb{b}") for b in range(NB)]
                                gate = kb.sb(es, "gate", [128, T], BF16)
                                vtm = kb.sb(es, "vtm", [64, NCH, 128], BF16)
                                vtm_b = [TK(vtm.ap, f"vtb{b}") for b in range(NB)]
                                oacc = kb.sb(es, "oacc", [128, T], F32)
                                oacc_b = [TK(oacc.ap, f"oab{b}") for b in range(NB)]
                                tmp = [[kb.sb(es, "tmp", [128, 512], F32) for _ in range(4)] for _ in range(2)]
                                qt = [[kb.sb(es, "qt", [128, 512], BF16) for _ in range(2)] for _ in range(2)]
                                kt = [[kb.sb(es, "kt", [128, 512], BF16) for _ in range(2)] for _ in range(2)]
                                kh = [[kb.sb(es, "kh", [128, 512], BF16) for _ in range(2)] for _ in range(2)]
                                ktm = [[kb.sb(es, "ktm", [64, 128], BF16) for _ in range(3)] for _ in range(2)]
                                stm = [[kb.sb(es, "stm", [64, 64], BF16) for _ in range(3)] for _ in range(2)]
                                bref = [kb.sb(es, "bref", [128, 8], F32) for _ in range(2)]
                                d1 = [[kb.sb(es, "d1", [128, 8], F32) for _ in range(2)] for _ in range(2)]
                                d2 = [kb.sb(es, "d2", [128, 8], F32) for _ in range(2)]
                                d12 = [[kb.sb(es, "d12", [128, 8], F32) for _ in range(2)] for _ in range(2)]
                                S = [kb.sb(es, "S", [128, 128], F32) for _ in range(2)]
                                Sp = [[kb.sb(es, "Sp", [128, 128], BF16) for _ in range(2)] for _ in range(2)]
                                pz = [kb.ps(es, "pz", [128, 512]) for _ in range(2)]
                                pv = kb.ps(es, "pv", [128, 512])
                                kb_fin = [kb.sb(es, "ffin", [128, 512], BF16) for _ in range(2)]
                                pobank = [kb.ps(es, "pob", [128, 512]) for _ in range(2)]
                                po = [[TK(pobank[d_].ap[:, i * 64:(i + 1) * 64], f"po{d_}{i}") for i in range(8)]
                                      for d_ in range(2)]

                                def poslot(n_, dr):
                                    g_ = (n_ // 4) % 2
                                    k_ = (n_ % 4) if dr == 0 else 3 - (n_ % 4)
                                    return g_ * 4 + k_
                                msbank = [kb.ps(es, "msb", [128, 512]) for _ in range(2)]
                                pst = [[TK(msbank[d_].ap[0:64, i * 64:(i + 1) * 64], f"pst{d_}{i}") for i in range(2)]
                                       for d_ in range(2)]
                                pkv = [[TK(msbank[d_].ap[:, 128 + i * 128:256 + i * 128], f"pkv{d_}{i}")
                                        for i in range(2)] for d_ in range(2)]
                                pktb = kb.ps(es, "pktb", [128, 4, 128], BF16)
                                pkt = [[TK(pktb.ap[0:64, d_ * 2 + i, :], f"pkt{d_}{i}") for i in range(2)]
                                       for d_ in range(2)]
                            for d_ in range(2):
                                for t_ in stm[d_]:
                                    op("pool", lambda e, t_=t_: e.memset(t_.ap, 0.0), writes=[t_])
                                op("pool", lambda e, d_=d_: e.memset(S[d_].ap, 0.0), writes=[S[d_]])
                            op("pool", lambda e: e.memset(oacc.ap, 0.0), writes=[oacc] + oacc_b)
                            cols = [h * 128, 512 + h * 128, 1024 + h * 128, 1536 + h * 128, 2048 + h * 128]
                            wq = load_w(es, wst, wbf, cols[0], 0)
                            wff = load_w(es, wst, wbf, cols[1], 1)
                            wfb = load_w(es, wst, wbf, cols[2], 2)
                            wv = load_w(es, wst, wbf, cols[3], 3)
                            wg_ = load_w(es, wst, wbf, cols[4], 4)
                            cnt = [0]

                            def silu_ps(out_ap, out_tk, p_, tm):
                                a, b_, c_ = tm[0], tm[1], tm[2]
                                op("act", lambda e: e.activation(out=a.ap, in_=p_.ap, func=AF.Exp, scale=-1.0),
                                   [p_], [a])
                                op("act", lambda e: e.activation(out=b_.ap, in_=a.ap, func=AF.Ln, bias=1.0),
                                   [a], [b_])
                                op("act", lambda e: e.activation(out=c_.ap, in_=b_.ap, func=AF.Exp, scale=-1.0),
                                   [b_], [c_])
                                op("dve", lambda e: e.tensor_mul(out=out_ap, in0=p_.ap, in1=c_.ap), [p_, c_],
                                   [out_tk])

                            def gates(b, dr, wt, qi):
                                p_ = pz[cnt[0] % 2]; cnt[0] += 1
                                A, B, C, Dd = tmp[dr]
                                lb_ap = lbt.ap[:, dr, l, h:h + 1]
                                oml_ap = omlt.ap[:, dr, l, h:h + 1]
                                q_, k_, kh_ = qt[dr][qi], kt[dr][qi], kh[dr][qi]
                                D1, D12, D2, br = d1[dr][qi], d12[dr][qi], d2[dr], bref[dr]
                                proj_fm(p_, wt, b)
                                op("act", lambda e: e.activation(out=A.ap, in_=p_.ap, func=AF.Exp, scale=-1.0),
                                   [p_], [A])
                                op("act", lambda e: e.activation(out=B.ap, in_=A.ap, func=AF.Ln, scale=lb_ap,
                                                                 bias=1.0), [A, lbt], [B])
                                op("act", lambda e: e.activation(out=C.ap, in_=A.ap, func=AF.Ln, bias=1.0),
                                   [A], [C])
                                op("act", lambda e: e.activation(out=Dd.ap, in_=C.ap, func=AF.Exp, scale=-1.0),
                                   [C], [Dd])
                                yield
                                op("pool", lambda e: e.tensor_sub(out=B.ap, in0=B.ap, in1=C.ap), [B, C], [B])
                                op("dve", lambda e: e.scalar_tensor_tensor(out=A.ap, in0=A.ap, scalar=oml_ap,
                                                                           in1=Dd.ap, op0=ALU.mult, op1=ALU.mult),
                                   [A, Dd, omlt], [A])
                                if dr == 0:
                                    op("dve", lambda e: e.tensor_tensor_scan(out=C.ap, data0=rmask.ap[:, 0, :],
                                                                             data1=B.ap, initial=0.0,
                                                                             op0=ALU.mult, op1=ALU.add),
                                       [rmask, B], [C])
                                    ri, ei = 31, 63
                                else:
                                    op("dve", lambda e: e.tensor_tensor_scan(out=C.ap[:, ::-1],
                                                                             data0=rmask.ap[:, 1, ::-1],
                                                                             data1=B.ap[:, ::-1], initial=0.0,
                                                                             op0=ALU.mult, op1=ALU.add),
                                       [rmask, B], [C])
                                    ri, ei = 32, 0
                                yield
                                c3 = C.ap.rearrange("p (c t) -> p c t", t=64)
                                op("dve", lambda e: e.tensor_copy(out=br.ap, in_=c3[:, :, ri]), [C], [br])
                                op("dve", lambda e: e.tensor_sub(out=D2.ap, in0=c3[:, :, ei], in1=br.ap),
                                   [C, br], [D2])
                                op("act", lambda e: e.activation(out=D12.ap, in_=c3[:, :, ei], func=AF.Exp),
                                   [C], [D12])
                                op("act", lambda e: e.activation(out=D1.ap, in_=br.ap, func=AF.Exp), [br], [D1])
                                op("act", lambda e: e.activation(out=D2.ap, in_=D2.ap, func=AF.Exp), [D2], [D2])
                                yield
                                op("dve", lambda e: e.tensor_sub(out=c3, in0=c3,
                                                                 in1=br.ap.unsqueeze(2).to_broadcast([128, 8, 64])),
                                   [C, br], [C])
                                op("act", lambda e: e.activation(out=Dd.ap, in_=C.ap, func=AF.Exp), [C], [Dd])
                                op("act", lambda e: e.activation(out=B.ap, in_=C.ap, func=AF.Exp, scale=-1.0),
                                   [C], [B])
                                yield
                                op("dve", lambda e: e.tensor_mul(out=q_.ap, in0=qf.ap[:, b * 512:(b + 1) * 512],
                                                                 in1=Dd.ap), [qf_b[b], Dd], [q_])
                                op("pool", lambda e: e.tensor_mul(out=k_.ap, in0=A.ap, in1=B.ap), [A, B], [k_])
                                op("pool", lambda e: e.tensor_mul(
                                    out=kh_.ap.rearrange("p (c t) -> p c t", t=64),
                                    in0=k_.ap.rearrange("p (c t) -> p c t", t=64),
                                    in1=D2.ap.unsqueeze(2).to_broadcast([128, 8, 64])), [k_, D2], [kh_])

                            def cparams(i, j, dr):
                                b = i if dr == 0 else NB - 1 - i
                                ch = j if dr == 0 else 7 - j
                                n_ = i * 8 + j
                                return b, ch, n_, i % 2

                            def stageA(i, j, dr):
                                b, ch, n_, qi = cparams(i, j, dr)
                                o0 = ch * 64
                                kh_ = kh[dr][qi]
                                pk_ = pkt[dr][n_ % 2]; km = ktm[dr][n_ % 3]
                                op("pe", lambda e: e.transpose(pk_.ap, kh_.ap[:, o0:o0 + 64], ident.ap),
                                   [kh_, ident], [pk_])
                                op("act", lambda e: e.copy(out=km.ap, in_=pk_.ap), [pk_], [km])

                            def stageB(i, j, dr):
                                b, ch, n_, qi = cparams(i, j, dr)
                                gch = b * 8 + ch
                                o0 = ch * 64
                                q_, k_ = qt[dr][qi], kt[dr][qi]
                                km = ktm[dr][n_ % 3]; sm_ = stm[dr][n_ % 3]
                                ps_ = pst[dr][n_ % 2]; pkv_ = pkv[dr][n_ % 2]
                                vt_ = vtm_b[b]
                                op("pe", lambda e: e.matmul(pkv_.ap, lhsT=km.ap, rhs=vtm.ap[:, gch, :], start=True,
                                                            stop=True), [km, vt_], [pkv_])
                                if dr == 0:
                                    ra = (slice(0, 64), slice(32, 64)); rb = (slice(0, 32), slice(0, 32))
                                else:
                                    ra = (slice(0, 64), slice(0, 32)); rb = (slice(32, 64), slice(32, 64))
                                for (rs_, cs_) in (ra, rb):
                                    op("pe", lambda e, rs_=rs_, cs_=cs_: e.matmul(
                                        ps_.ap[rs_, cs_], lhsT=k_.ap[:, o0 + rs_.start:o0 + rs_.stop],
                                        rhs=q_.ap[:, o0 + cs_.start:o0 + cs_.stop], start=True, stop=True),
                                       [k_, q_], [ps_])
                                op("dve", lambda e: e.copy_predicated(out=sm_.ap, mask=tmask.ap[0:64, dr, 0:64],
                                                                      data=ps_.ap), [ps_, tmask, sm_], [sm_])

                            def stageC1(i, j, dr):
                                b, ch, n_, qi = cparams(i, j, dr)
                                gch = b * 8 + ch
                                sp_ = Sp[dr][n_ % 2]; S_ = S[dr]
                                if si == 0 and ((dr == 0 and gch == (TB // 64)) or
                                                (dr == 1 and gch == (TB // 64) - 1)):
                                    op("dve", lambda e: e.tensor_scalar(out=S_.ap, in0=S_.ap,
                                                                         scalar1=flg.ap[:, 1:2], scalar2=None,
                                                                         op0=ALU.mult), [S_, flg], [S_])
                                op("act", lambda e: e.activation(out=sp_.ap, in_=S_.ap, func=AF.Copy,
                                                                 scale=d1[dr][qi].ap[:, ch:ch + 1]),
                                   [S_, d1[dr][qi]], [sp_])

                            def stageC2(i, j, dr):
                                b, ch, n_, qi = cparams(i, j, dr)
                                gch = b * 8 + ch
                                o0 = ch * 64
                                q_ = qt[dr][qi]
                                sm_ = stm[dr][n_ % 3]; po_ = po[dr][poslot(n_, dr)]; vt_ = vtm_b[b]
                                op("pe", lambda e: e.matmul(po_.ap, lhsT=vtm.ap[:, gch, :], rhs=sm_.ap, start=True,
                                                            stop=False), [vt_, sm_], [po_])

                            def stageC3(i, j, dr):
                                b, ch, n_, qi = cparams(i, j, dr)
                                gch = b * 8 + ch
                                o0 = ch * 64
                                q_ = qt[dr][qi]
                                po_ = po[dr][poslot(n_, dr)]; sp_ = Sp[dr][n_ % 2]; S_ = S[dr]; pkv_ = pkv[dr][n_ % 2]
                                op("pe", lambda e: e.matmul(po_.ap, lhsT=sp_.ap, rhs=q_.ap[:, o0:o0 + 64],
                                                            start=False, stop=True), [sp_, q_], [po_])
                                op("dve", lambda e: e.scalar_tensor_tensor(out=S_.ap, in0=S_.ap,
                                                                           scalar=d12[dr][qi].ap[:, ch:ch + 1],
                                                                           in1=pkv_.ap, op0=ALU.mult, op1=ALU.add),
                                   [S_, d12[dr][qi], pkv_], [S_])
                                if n_ % 4 == 3:
                                    g_ = (n_ // 4) % 2
                                    t0 = (gch - 3) * 64 if dr == 0 else gch * 64
                                    ob = oacc_b[b]
                                    grp = po[dr][g_ * 4:g_ * 4 + 4]
                                    op("dve", lambda e: e.tensor_add(
                                        out=oacc.ap[:, t0:t0 + 256], in0=pobank[dr].ap[:, g_ * 256:(g_ + 1) * 256],
                                        in1=oacc.ap[:, t0:t0 + 256]), grp + [ob], [ob])

                            for b in range(NB):
                                p_ = pz[cnt[0] % 2]; cnt[0] += 1
                                proj_fm(p_, wq, b)
                                silu_ps(qf.ap[:, b * 512:(b + 1) * 512], qf_b[b], p_, tmp[0])
                                p_ = pz[cnt[0] % 2]; cnt[0] += 1
                                proj_fm(p_, wg_, b)
                                silu_ps(gate.ap[:, b * 512:(b + 1) * 512], gate, p_, tmp[1])
                                for c4 in range(2):
                                    for cq in range(4):
                                        cch = b * 8 + c4 * 4 + cq
                                        for k in range(KC):
                                            op("pe", lambda e, k=k: e.matmul(
                                                pv.ap[0:64, cq * 128:(cq + 1) * 128],
                                                lhsT=hT.ap[:, k, cch * 64:(cch + 1) * 64], rhs=wv.ap[:, k, :],
                                                start=(k == 0), stop=(k == KC - 1)), [wv, hT_blk[b]], [pv])
                                    c0_ = b * 8 + c4 * 4
                                    op("act", lambda e: e.copy(
                                        out=vtm.ap[:, c0_:c0_ + 4, :],
                                        in_=pv.ap[0:64, :].rearrange("p (c n) -> p c n", n=128)), [pv], [vtm_b[b]])
                            def fin_block(b):
                                sqt = [tmp[0][0], tmp[1][0]]
                                fin = [qt[0][0], qt[0][1]]
                                ob = oacc_b[b]
                                sq_tk = wst[0]
                                sq_ap = wst[0].ap.rearrange("p k n -> p (k n)")[:, (b % 2) * 512:(b % 2 + 1) * 512]
                                osl = oacc.ap[:, b * 512:(b + 1) * 512]
                                op("act", lambda e: e.activation(out=sq_ap, in_=osl, func=AF.Square), [ob], [sq_tk])
                                p_ = pz[cnt[0] % 2]; cnt[0] += 1
                                op("pe", lambda e: e.matmul(p_.ap, lhsT=onesf.ap, rhs=sq_ap, start=True,
                                                            stop=True), [onesf, sq_tk], [p_])
                                op("act", lambda e: e.activation(out=sq_ap, in_=p_.ap, func=AF.Ln,
                                                                 scale=1.0 / 128, bias=NORM_EPS), [p_], [sq_tk])
                                op("act", lambda e: e.activation(out=sq_ap, in_=sq_ap, func=AF.Exp,
                                                                 scale=-0.5), [sq_tk], [sq_tk])
                                op("dve", lambda e: e.scalar_tensor_tensor(out=sq_ap, in0=osl,
                                                                           scalar=gnt.ap[:, l:l + 1], in1=sq_ap,
                                                                           op0=ALU.mult, op1=ALU.mult),
                                   [ob, gnt, sq_tk], [sq_tk])
                                f_ = kb_fin[b % 2]
                                op("pool", lambda e: e.tensor_mul(out=f_.ap, in0=sq_ap,
                                                                  in1=gate.ap[:, b * 512:(b + 1) * 512]),
                                   [sq_tk, gate], [f_])
                                dma("pool", MT, MT.ap[h, :, b * 512:(b + 1) * 512], f_, f_.ap, sbuf_side=f_)

                            for d_ in range(2):
                                for t_ in pst[d_]:
                                    op("dve", lambda e, t_=t_: e.memset(t_.ap, 0.0), writes=[t_])
                            steps = [(i, j) for i in range(NB) for j in range(8)]
                            NS = len(steps)
                            for n in range(NS + 2):
                                if n == 0:
                                    for _ in gates(0, 0, wff, 0):
                                        pass
                                    for _ in gates(NB - 1, 1, wfb, 0):
                                        pass
                                    ggen = []
                                if n < NS and steps[n][1] == 1 and steps[n][0] + 1 < NB:
                                    i = steps[n][0] + 1
                                    ggen = [gates(i, 0, wff, i % 2), gates(NB - 1 - i, 1, wfb, i % 2)]
                                if n < NS and 1 <= steps[n][1] <= 5:
                                    for gg in ggen:
                                        next(gg, None)
                                if n < NS and steps[n][1] == 6:
                                    for gg in ggen:
                                        for _ in gg:
                                            pass
                                    ggen = []
                                if n - 2 >= 0:
                                    for dr in range(2):
                                        stageC1(*steps[n - 2], dr)
                                if n - 1 >= 0 and n - 1 < NS:
                                    for dr in range(2):
                                        stageB(*steps[n - 1], dr)
                                if n - 2 >= 0:
                                    for dr in range(2):
                                        stageC2(*steps[n - 2], dr)
                                if n < NS:
                                    for dr in range(2):
                                        stageA(*steps[n], dr)
                                if n - 2 >= 0:
                                    for dr in range(2):
                                        stageC3(*steps[n - 2], dr)
                                    if steps[n - 2][1] == 7:
                                        i_done = steps[n - 2][0]
                                        for bb_ in sorted({i_done, NB - 1 - i_done}):
                                            if max(bb_, NB - 1 - bb_) == i_done:
                                                fin_block(bb_)
                        kb.barrier()
                    with ExitStack() as es:
                        wst = [kb.sb(es, "wst", [128, KC, 128], F32) for _ in range(2)]
                        wbf = [kb.sb(es, "wbf", [128, KC, 128], BF16) for _ in range(5)]
                        QT0 = kb.sb(es, "QT0", [128, T], BF16)
                        QT1 = kb.sb(es, "QT1", [128, T], BF16)
                        QTs = [QT0, QT1]
                        KT = kb.sb(es, "KT", [128, T], BF16)
                        op("pool", lambda e: e.memset(QT0.ap, 0.0), writes=[QT0])
                        op("pool", lambda e: e.memset(QT1.ap, 0.0), writes=[QT1])
                        Vt = kb.sb(es, "Vt", [128, NT, 128], BF16)
                        cs = kb.sb(es, "cs", [128, T], F32)
                        sn = kb.sb(es, "sn", [128, T], F32)
                        dma("sp", cs, cs.ap, cosT, cosT.ap[:, 0:T])
                        dma("sp", sn, sn.ap, sinT, sinT.ap[:, 0:T])

                        def load_d(hh):
                            return (load_w(es, wst, wbf, 2560 + hh * 128, 0), load_w(es, wst, wbf, 4096 + hh * 128, 1),
                                    load_w(es, wst, wbf, 3072 + hh * 128, 2), load_w(es, wst, wbf, 4608 + hh * 128, 3),
                                    load_w(es, wst, wbf, 3584 + hh * 128, 4))

                        Wd = load_d(0)
                        for h in range(4):
                            wq_, wqs, wk_, wks, wv_ = Wd
                            with ExitStack() as es2:
                                pz = [kb.ps(es2, "pz", [128, 512]) for _ in range(4)]
                                pv = kb.ps(es2, "pv", [128, 128])
                                ta = [kb.sb(es2, "ta", [128, 512], F32) for _ in range(2)]
                                tb = [kb.sb(es2, "tb", [128, 512], F32) for _ in range(2)]
                                n2 = 0
                                for b in range(NB):
                                    for (wa_, ws_, dst) in ((wq_, wqs, None), (wk_, wks, KT)):
                                        pa = pz[(2 * n2) % 4]; pb = pz[(2 * n2 + 1) % 4]
                                        t1_ = ta[n2 % 2]; t2_ = tb[n2 % 2]; n2 += 1
                                        proj_fm(pa, wa_, b)
                                        proj_fm(pb, ws_, b)
                                        op("dve", lambda e: e.tensor_mul(out=t1_.ap, in0=pa.ap,
                                                                         in1=cs.ap[:, b * 512:(b + 1) * 512]),
                                           [pa, cs], [t1_])
                                        op("dve", lambda e: e.tensor_mul(out=t2_.ap, in0=pb.ap,
                                                                         in1=sn.ap[:, b * 512:(b + 1) * 512]),
                                           [pb, sn], [t2_])
                                        if dst is None:
                                            for c_ in range(2):
                                                rs_ = slice(c_ * 64, (c_ + 1) * 64)
                                                op("pool", lambda e, c_=c_, rs_=rs_: e.tensor_add(
                                                    out=QTs[c_].ap[rs_, b * 512:(b + 1) * 512],
                                                    in0=t1_.ap[rs_, :], in1=t2_.ap[rs_, :]), [t1_, t2_], [QTs[c_]])
                                        else:
                                            op("pool", lambda e: e.tensor_add(
                                                out=dst.ap[:, b * 512:(b + 1) * 512],
                                                in0=t1_.ap, in1=t2_.ap), [t1_, t2_], [dst])
                                    for it in range(b * 4, b * 4 + 4):
                                        proj_tm(pv, wv_, it)
                                        op("act", lambda e, it=it: e.copy(out=Vt.ap[:, it, :], in_=pv.ap), [pv],
                                           [Vt])
                                kb.barrier()
                            if h + 1 < 4:
                                Wd = load_d(h + 1)
                            with ExitStack() as es2:
                                pS = [kb.ps(es2, "pS", [128, 512]) for _ in range(3)]
                                pO = [kb.ps(es2, "pO", [128, 512]) for _ in range(2)]
                                pZ = [kb.ps(es2, "pZ", [128, 512]) for _ in range(2)]
                                pN = kb.ps(es2, "pN", [128, 512])
                                Pm = [kb.sb(es2, "Pm", [128, 512], BF16) for _ in range(5)]
                                r1 = kb.sb(es2, "r1", [128, 512], F32)
                                r2 = kb.sb(es2, "r2", [128, 512], F32)
                                oo = [kb.sb(es2, "oo", [128, 512], F32) for _ in range(2)]
                                sq2 = [kb.sb(es2, "sq2", [128, 512], F32) for _ in range(2)]
                                fin = [kb.sb(es2, "fin", [128, 512], BF16) for _ in range(2)]
                                LAG = 2
                                tiles = [(qb, kt_, c) for qb in range(NB) for kt_ in range(NT) for c in range(2)]

                                def stage1(n):
                                    qb, kt_, c = tiles[n]
                                    qsl = slice(qb * 512, (qb + 1) * 512)
                                    ksl = slice(kt_ * 128, (kt_ + 1) * 128)
                                    cross = (si == 0) and ((qb * 512) // TB != (kt_ * 128) // TB)
                                    bias_ap = flg.ap[:, 0:1] if cross else zero1.ap
                                    ps_ = pS[n % 3]; pm_ = Pm[n % 5]
                                    rs = slice(c * 64, (c + 1) * 64)
                                    op("pe", lambda e: e.matmul(ps_.ap, lhsT=KT.ap[:, ksl], rhs=QTs[c].ap[:, qsl],
                                                                start=True, stop=True), [KT, QTs[c]], [ps_])
                                    op("act", lambda e: e.activation(out=pm_.ap, in_=ps_.ap, func=AF.Exp,
                                                                     scale=0.125, bias=bias_ap),
                                       [ps_, flg, zero1], [pm_])

                                def stage2(n):
                                    qb, kt_, c = tiles[n]
                                    pm_ = Pm[n % 5]
                                    op("pe", lambda e: e.matmul(pO[c].ap, lhsT=Vt.ap[:, kt_, :], rhs=pm_.ap,
                                                                start=(kt_ == 0), stop=(kt_ == NT - 1)),
                                       [Vt, pm_], [pO[c]])
                                    op("pe", lambda e: e.matmul(pZ[c].ap, lhsT=onesb.ap, rhs=pm_.ap,
                                                                start=(kt_ == 0), stop=(kt_ == NT - 1)),
                                       [onesb, pm_], [pZ[c]])

                                def fin1(qb):
                                    o_ = oo[qb % 2]; s2_ = sq2[qb % 2]
                                    op("dve", lambda e: e.reciprocal(out=r1.ap, in_=pZ[0].ap), [pZ[0]], [r1])
                                    op("dve", lambda e: e.reciprocal(out=r2.ap, in_=pZ[1].ap), [pZ[1]], [r2])
                                    op("dve", lambda e: e.tensor_mul(out=r1.ap, in0=pO[0].ap, in1=r1.ap),
                                       [pO[0], r1], [r1])
                                    op("dve", lambda e: e.tensor_mul(out=r2.ap, in0=pO[1].ap, in1=r2.ap),
                                       [pO[1], r2], [r2])
                                    op("dve", lambda e: e.scalar_tensor_tensor(out=o_.ap, in0=r2.ap,
                                                                               scalar=nlam.ap[:, l:l + 1], in1=r1.ap,
                                                                               op0=ALU.mult, op1=ALU.add),
                                       [r1, r2, nlam], [o_])
                                    op("act", lambda e: e.activation(out=s2_.ap, in_=o_.ap, func=AF.Square), [o_],
                                       [s2_])

                                def fin2(qb):
                                    o_ = oo[qb % 2]; s2_ = sq2[qb % 2]
                                    qsl = slice(qb * 512, (qb + 1) * 512)
                                    op("pe", lambda e: e.matmul(pN.ap, lhsT=onesf.ap, rhs=s2_.ap, start=True,
                                                                stop=True), [onesf, s2_], [pN])
                                    op("act", lambda e: e.activation(out=s2_.ap, in_=pN.ap, func=AF.Ln,
                                                                     scale=1.0 / 128, bias=SUBLN_EPS), [pN], [s2_])
                                    op("act", lambda e: e.activation(out=s2_.ap, in_=s2_.ap, func=AF.Exp,
                                                                     scale=-0.5), [s2_], [s2_])
                                    f_ = fin[qb % 2]
                                    op("dve", lambda e: e.scalar_tensor_tensor(out=f_.ap, in0=o_.ap,
                                                                               scalar=swt.ap[:, l:l + 1],
                                                                               in1=s2_.ap, op0=ALU.mult,
                                                                               op1=ALU.mult), [o_, swt, s2_], [f_])
                                    dma("pool", MT, MT.ap[4 + h, :, qsl], f_, f_.ap, sbuf_side=f_)

                                NTI = len(tiles)
                                per_qb = NT * 2
                                pend = {}
                                for n in range(NTI + LAG):
                                    if n < NTI:
                                        stage1(n)
                                    m = n - LAG
                                    if m >= 0:
                                        stage2(m)
                                        if (m + 1) % per_qb == 0:
                                            qb_done = m // per_qb
                                            fin1(qb_done)
                                            pend[n + 4] = qb_done
                                    if n in pend:
                                        fin2(pend.pop(n))
                                for k_ in sorted(pend):
                                    fin2(pend[k_])
                                kb.barrier()
                    kb.barrier()

                with ExitStack() as es:
                    wo = kb.sb(es, "wo", [128, KC, D], BF16)
                    wd = kb.sb(es, "wd", [128, JC, D], BF16)
                    wgr = [kb.sb(es, "wgr", [128, KC, 128], BF16) for _ in range(3)]
                    wur = [kb.sb(es, "wur", [128, KC, 128], BF16) for _ in range(3)]
                    xt = [kb.sb(es, "xt", [128, D], F32) for _ in range(2)]
                    xm = [kb.sb(es, "xm", [128, D], F32) for _ in range(8)]
                    mxb = [kb.sb(es, "mxb", [128, KC, 512], BF16) for _ in range(1)]
                    h2T = [kb.sb(es, "h2T", [128, KC, 512], BF16) for _ in range(2)]
                    hid = kb.sb(es, "hid", [128, JC, 512], BF16)
                    hid_j = [TK(hid.ap, f"hid{j}") for j in range(JC)]
                    tf = kb.sb(es, "tf", [128, D], F32)
                    xn = [kb.sb(es, "xn", [128, D], BF16) for _ in range(2)]
                    ss = [kb.sb(es, "ss", [128, 1], F32) for _ in range(6)]
                    ea = [kb.sb(es, "ea", [128, 512], F32) for _ in range(2)]
                    eb = [kb.sb(es, "eb", [128, 512], F32) for _ in range(2)]
                    ptok = [kb.ps(es, "ptok", [128, D]) for _ in range(2)]
                    ptr = kb.ps(es, "ptr", [128, KC, 128], BF16)
                    pgu = [kb.ps(es, "pgu", [128, 512]) for _ in range(3)]
                    dma("sp", wo, wo.ap, wo_s, wo_s.ap)
                    st_ = {"nss": 0, "ngu": 0, "nt": 0, "nx": 0}

                    def pre_load(b):
                        mb = mxb[0]
                        for k in range(KC):
                            dma("sp", mb, mb.ap[:, k, :], MT, MT.ap[k, :, b * 512:(b + 1) * 512])

                    def pre_tile(b, tt):
                        mb = mxb[0]; h2 = h2T[b % 2]
                        sq = subseq(b * 512)
                        it = b * 4 + tt
                        x_ = xt[st_["nx"] % 2]; n_ = xn[st_["nx"] % 2]; st_["nx"] += 1
                        pt = ptok[st_["nt"] % 2]; st_["nt"] += 1
                        s1 = ss[st_["nss"] % 6]; s2 = ss[(st_["nss"] + 1) % 6]; st_["nss"] += 2
                        xm_ = xm[(b % 2) * 4 + tt]
                        dma("sp", x_, x_.ap, X, X.ap[it * 128:(it + 1) * 128, :])
                        for hf in range(2):
                            for k in range(KC):
                                op("pe", lambda e, k=k, hf=hf: e.matmul(
                                    pt.ap[:, hf * 512:(hf + 1) * 512], lhsT=mb.ap[:, k, tt * 128:(tt + 1) * 128],
                                    rhs=wo.ap[:, k, hf * 512:(hf + 1) * 512], start=(k == 0),
                                    stop=(k == KC - 1)), [mb, wo], [pt])
                        yield
                        op("act", lambda e: e.activation(out=n_.ap, in_=pt.ap, func=AF.Square,
                                                         accum_out=s1.ap), [pt], [n_, s1])
                        rstd_from_ss(s1, D, NORM_EPS)
                        yield
                        op("dve", lambda e: e.scalar_tensor_tensor(out=xm_.ap, in0=pt.ap, scalar=s1.ap,
                                                                   in1=Gbc.ap[:, 0, sq, :], op0=ALU.mult,
                                                                   op1=ALU.mult), [pt, s1, Gbc], [xm_])
                        op("pool", lambda e: e.tensor_add(out=xm_.ap, in0=xm_.ap, in1=x_.ap), [xm_, x_], [xm_])
                        yield
                        op("act", lambda e: e.activation(out=n_.ap, in_=xm_.ap, func=AF.Square,
                                                         accum_out=s2.ap), [xm_], [n_, s2])
                        rstd_from_ss(s2, D, NORM_EPS)
                        yield
                        op("dve", lambda e: e.tensor_scalar(out=n_.ap, in0=xm_.ap, scalar1=s2.ap, scalar2=None,
                                                             op0=ALU.mult), [xm_, s2], [n_])
                        yield
                        pre_tile2(b, tt, n_)

                    def pre_tile2(b, tt, n_):
                        h2 = h2T[b % 2]
                        sq = subseq(b * 512)
                        for k in range(KC):
                            op("pe", lambda e, k=k: e.transpose(ptr.ap[:, k, :], n_.ap[:, k * 128:(k + 1) * 128],
                                                                ident.ap), [n_, ident], [ptr])
                        for k in range(KC):
                            op("act", lambda e, k=k: e.activation(
                                out=h2.ap[:, k, tt * 128:(tt + 1) * 128], in_=ptr.ap[:, k, :], func=AF.Identity,
                                scale=modA.ap[:, 1, sq, k:k + 1], bias=modS.ap[:, 1, sq, k:k + 1]),
                               [ptr, modA, modS], [h2])

                    def ffn_j(b, j):
                        h2 = h2T[b % 2]
                        wg_t = wgr[j % 3]; wu_t = wur[j % 3]
                        dma("sp", wg_t, wg_t.ap, wg_s, wg_s.ap[j])
                        dma("sp", wu_t, wu_t.ap, wu_s, wu_s.ap[j])
                        pg = pgu[st_["ngu"] % 3]; pu = pgu[(st_["ngu"] + 1) % 3]; st_["ngu"] += 2
                        a_ = ea[j % 2]; b_ = eb[j % 2]
                        for k in range(KC):
                            op("pe", lambda e, k=k: e.matmul(pg.ap, lhsT=wg_t.ap[:, k, :], rhs=h2.ap[:, k, :],
                                                             start=(k == 0), stop=(k == KC - 1)), [wg_t, h2], [pg])
                        for k in range(KC):
                            op("pe", lambda e, k=k: e.matmul(pu.ap, lhsT=wu_t.ap[:, k, :], rhs=h2.ap[:, k, :],
                                                             start=(k == 0), stop=(k == KC - 1)), [wu_t, h2], [pu])
                        op("act", lambda e: e.activation(out=a_.ap, in_=pg.ap, func=AF.Exp, scale=-1.0), [pg], [a_])
                        op("act", lambda e: e.activation(out=b_.ap, in_=a_.ap, func=AF.Ln, bias=1.0), [a_], [b_])
                        op("act", lambda e: e.activation(out=a_.ap, in_=b_.ap, func=AF.Exp, scale=-1.0), [b_], [a_])
                        op("dve", lambda e: e.tensor_mul(out=b_.ap, in0=pg.ap, in1=a_.ap), [pg, a_], [b_])
                        op("dve", lambda e: e.tensor_mul(out=hid.ap[:, j, :], in0=pu.ap, in1=b_.ap),
                           [pu, b_], [hid_j[j]])

                    def down_tile(b, tt):
                        sq = subseq(b * 512)
                        it = b * 4 + tt
                        pt = ptok[st_["nt"] % 2]; st_["nt"] += 1
                        s1 = ss[st_["nss"] % 6]; st_["nss"] += 1
                        xm_ = xm[(b % 2) * 4 + tt]
                        for hf in range(2):
                            for j in range(JC):
                                op("pe", lambda e, j=j, hf=hf: e.matmul(
                                    pt.ap[:, hf * 512:(hf + 1) * 512], lhsT=hid.ap[:, j, tt * 128:(tt + 1) * 128],
                                    rhs=wd.ap[:, j, hf * 512:(hf + 1) * 512], start=(j == 0),
                                    stop=(j == JC - 1)), [hid_j[j], wd], [pt])
                        op("act", lambda e: e.activation(out=tf.ap, in_=pt.ap, func=AF.Square,
                                                         accum_out=s1.ap), [pt], [tf, s1])
                        rstd_from_ss(s1, D, NORM_EPS)
                        op("dve", lambda e: e.scalar_tensor_tensor(out=tf.ap, in0=pt.ap, scalar=s1.ap,
                                                                   in1=Gbc.ap[:, 1, sq, :], op0=ALU.mult,
                                                                   op1=ALU.mult), [pt, s1, Gbc], [tf])
                        op("pool", lambda e: e.tensor_add(out=xm_.ap, in0=tf.ap, in1=xm_.ap), [tf, xm_], [xm_])
                        dma("pool", XO, XO.ap[it * 128:(it + 1) * 128, :], xm_, xm_.ap, sbuf_side=xm_)

                    pre_load(0)
                    for tt in range(4):
                        for _ in pre_tile(0, tt):
                            pass
                    for j in range(JC):
                        dma("sp", wd, wd.ap[:, j, :], wd_s, wd_s.ap[j])
                    for b in range(NB):
                        if b + 1 < NB:
                            pre_load(b + 1)
                        gens = []
                        for j in range(JC):
                            ffn_j(b, j)
                            if b + 1 < NB and j % 5 == 0 and j // 5 < 4:
                                gens.append(pre_tile(b + 1, j // 5))
                            for gg in gens:
                                next(gg, None)
                        for gg in gens:
                            for _ in gg:
                                pass
                        for tt in range(4):
                            down_tile(b, tt)
                    kb.barrier()
        kb.barrier()
    return nc, kb


_PROG = {}


def _rot_tables(TA, restart):
    pos = np.arange(TA, dtype=np.float32)
    if restart:
        pos = np.where(np.arange(TA) >= TA // 2, pos - np.float32(TA // 2), pos).astype(np.float32)
    inv = (1.0 / (np.float32(10000.0) ** (np.arange(0, 64, 2, dtype=np.float32) / np.float32(64)))).astype(np.float32)
    ang = (pos[None, :] * inv[:, None]).astype(np.float32)
    c = np.cos(ang).astype(np.float32)
    s = np.sin(ang).astype(np.float32)
    cosT = np.zeros((128, TA), np.float32)
    sinT = np.zeros((128, TA), np.float32)
    for p in range(128):
        j = p % 64
        i = j % 32
        cosT[p] = c[i]
        sinT[p] = -s[i] if j < 32 else s[i]
    return cosT, sinT


def kernel(x_prompt, x_sample, c_prompt, c_sample, w_ada, b_ada, norm_pre_mix, norm_post_mix,
           norm_pre_ffn, norm_post_ffn, w_in, hg_lower_bounds, hg_gnorm, da_lambda_q1,
           da_lambda_k1, da_lambda_q2, da_lambda_k2, da_subln, w_out, w_ffn_gate, w_ffn_up,
           w_ffn_down):
    f32 = np.float32
    x_prompt = np.asarray(x_prompt, f32); x_sample = np.asarray(x_sample, f32)
    c_prompt = np.asarray(c_prompt, f32); c_sample = np.asarray(c_sample, f32)
    TA = x_prompt.shape[1]; TB = x_sample.shape[1]
    assert x_prompt.shape[0] == 4 and x_sample.shape[0] == 16
    key = (TA, TB)
    if key not in _PROG:
        _PROG[key] = build_program(TA, TB)
    nc = _PROG[key]
    w_in = np.asarray(w_in, f32)
    perm = np.arange(512).reshape(8, 2, 32)[:, ::-1, :].reshape(512)
    w_in_ext = np.ascontiguousarray(np.concatenate(
        [w_in, w_in[:, :, 2560 + perm], w_in[:, :, 3072 + perm]], axis=2))
    b_ada = np.asarray(b_ada, f32)
    b_adaT = np.ascontiguousarray(b_ada.reshape(DEPTH, 6, KC, 128).transpose(3, 0, 1, 2))
    b_adaR = np.ascontiguousarray(np.broadcast_to(b_ada[None], (4, DEPTH, 6 * D)))
    npre = np.stack([np.asarray(norm_pre_mix, f32), np.asarray(norm_pre_ffn, f32)], axis=1)
    npreT = np.ascontiguousarray(npre.reshape(DEPTH, 2, KC, 128).transpose(3, 0, 1, 2))
    npost = np.stack([np.asarray(norm_post_mix, f32), np.asarray(norm_post_ffn, f32)], axis=1)
    npostR = np.ascontiguousarray(np.broadcast_to(npost[None], (4, DEPTH, 2, D)))
    hlb = np.asarray(hg_lower_bounds, f32)
    hlbT = np.ascontiguousarray(hlb.reshape(2, DEPTH, 4, 128).transpose(3, 0, 1, 2))
    gnT = np.ascontiguousarray(np.asarray(hg_gnorm, f32).T)
    swT = np.ascontiguousarray(np.asarray(da_subln, f32).T)
    lam = np.stack([np.asarray(a, f32) for a in (da_lambda_q1, da_lambda_k1, da_lambda_q2, da_lambda_k2)], axis=0)
    lamB = np.ascontiguousarray(np.broadcast_to(lam[None], (128, 4, DEPTH, 64)))
    ident = np.eye(128, dtype=f32).astype(ml_dtypes.bfloat16)
    onesf = np.ones((128, 128), f32)
    onesb = np.ones((128, 128), f32).astype(ml_dtypes.bfloat16)
    sel = np.zeros((4, 384), f32)
    for s in range(3):
        sel[s, s * 128:(s + 1) * 128] = 1.0
    t = np.arange(512)
    rmask = np.ones((128, 2, 512), f32)
    rmask[:, 0, t % 64 == 0] = 0.0
    rmask[:, 1, t % 64 == 63] = 0.0
    s_ = np.arange(128)[:, None]; t_ = np.arange(128)[None, :]
    same = (s_ // 64) == (t_ // 64)
    tmask = np.zeros((128, 2, 128), np.uint32)
    tmask[:, 0, :] = (same & (s_ <= t_)).astype(np.uint32)
    tmask[:, 1, :] = (same & (s_ >= t_)).astype(np.uint32)
    tabs = [_rot_tables(TA, False), _rot_tables(TA, True)]
    common = dict(w_ada=np.asarray(w_ada, f32), b_adaT=b_adaT, b_adaR=b_adaR, npreT=npreT, npostR=npostR,
                  w_in=w_in_ext, hlbT=hlbT, gnT=gnT, swT=swT, lamB=lamB, w_out=np.asarray(w_out, f32),
                  w_g=np.asarray(w_ffn_gate, f32), w_u=np.asarray(w_ffn_up, f32), w_d=np.asarray(w_ffn_down, f32),
                  ident=ident, onesf=onesf, onesb=onesb, sel=sel, rmask=rmask, tmask=tmask)
    in_maps = []
    for c in range(8):
        if c < 4:
            xa = x_prompt[c]; xb = x_sample[c]
            cs = [c_prompt[c], c_prompt[c], c_sample[c]]
            fl = (0.0, 1.0); tb = tabs[0]
        else:
            k = 4 + 3 * (c - 4)
            xa = np.concatenate([x_sample[k], x_sample[k + 1]], axis=0); xb = x_sample[k + 2]
            cs = [c_sample[k], c_sample[k + 1], c_sample[k + 2]]
            fl = (NEG_BIG, 0.0); tb = tabs[1]
        c3 = np.zeros((4, D), f32)
        c3[:3] = np.stack(cs)
        c3T = np.ascontiguousarray(c3.reshape(4, KC, 128).transpose(2, 1, 0))
        flags = np.zeros((128, 2), f32); flags[:, 0] = fl[0]; flags[:, 1] = fl[1]
        m = dict(common)
        m.update(xa=np.ascontiguousarray(xa), xb=np.ascontiguousarray(xb), c3T=c3T, cosT=tb[0], sinT=tb[1],
                 flags=flags)
        in_maps.append(m)
    res = run_bass_kernel_spmd(nc, in_maps, core_ids=list(range(8)))
    if DEBUG:
        global _DBG
        _DBG = res.results
    y_prompt = np.zeros_like(x_prompt); y_sample = np.zeros_like(x_sample)
    for c in range(8):
        r = res.results[c]
        if c < 4:
            y_prompt[c] = r["ya"]; y_sample[c] = r["yb"]
        else:
            k = 4 + 3 * (c - 4)
            y_sample[k] = r["ya"][:TB]; y_sample[k + 1] = r["ya"][TB:]; y_sample[k + 2] = r["yb"]
    return (y_prompt, y_sample)
```

```python
import numpy as np
import ml_dtypes
from contextlib import ExitStack
import concourse.bass as bass
import concourse.mybir as mybir
from concourse.bass_utils import run_bass_kernel_spmd

F32 = mybir.dt.float32
BF16 = mybir.dt.bfloat16
U32 = mybir.dt.uint32
AF = mybir.ActivationFunctionType
ALU = mybir.AluOpType

D = 1024
KC = 8
HID = 2816
JC = 22
DEPTH = 2
NCOL = 5120
NEG_BIG = -30000.0
NORM_EPS = 1e-6
SUBLN_EPS = 1e-5
SEM_LIMIT = 30000
import os
DEBUG = bool(int(os.environ.get("MK_DEBUG", "0")))
NLAYERS = int(os.environ.get("MK_LAYERS", str(DEPTH)))


class TK:
    __slots__ = ("ap", "w", "r", "dsem", "dcnt", "name", "dsem_sw")

    def __init__(self, ap, name):
        self.ap = ap
        self.w = None
        self.r = {}
        self.dsem = None
        self.dcnt = 0
        self.name = name


class KB:
    def __init__(self, nc, needed=None):
        self.nc = nc
        self.needed = needed
        self.rec = {k: set() for k in ("pe", "act", "dve", "pool", "sp")}
        self.idx = {k: 0 for k in ("pe", "act", "dve", "pool", "sp")}
        self.tokmap = {k: {} for k in ("pe", "act", "dve", "pool", "sp")}
        self.emitted = {k: [] for k in ("pe", "act", "dve", "pool", "sp")}
        self.seen_idx = {k: {} for k in ("pe", "act", "dve", "pool", "sp")}
        self.eng = {"pe": nc.tensor, "act": nc.scalar, "dve": nc.vector, "pool": nc.gpsimd, "sp": nc.sync}
        self.sem = {}
        self.cnt = {}
        self.seen = {k: {} for k in self.eng}
        self.nsem = 0
        for k in self.eng:
            self._newsem(k)
        self.uid = 0
        self.dma_toks = {}
        self.free_dsems = []
        self.free_dsems_sw = []
        self.dval = {}
        self.dsem_tks = []

    def _newsem(self, k):
        self.sem[k] = self.nc.alloc_semaphore(name=f"s_{k}_{self.nsem}")
        self.nsem += 1
        self.cnt[k] = 0

    def name(self, base):
        self.uid += 1
        return f"{base}_{self.uid}"

    def sb(self, es, base, shape, dtype):
        t = es.enter_context(self.nc.sbuf_tensor(self.name(base), list(shape), dtype))
        return TK(t.ap(), base)

    def ps(self, es, base, shape, dtype=F32):
        t = es.enter_context(self.nc.psum_tensor(self.name(base), list(shape), dtype))
        return TK(t.ap(), base)

    def dram(self, base, shape, dtype, kind="Internal"):
        t = self.nc.dram_tensor(base, list(shape), dtype, kind=kind)
        return TK(t.ap(), base)

    def _wait(self, e, toks):
        import bisect
        need = {}
        for tok in toks:
            if tok is None:
                continue
            if tok[0] == "E":
                _, src, i = tok
                if src == e and e == "pe":
                    continue
                if self.needed is None:
                    if self.seen_idx[e].get(src, 0) >= i:
                        continue
                    self.seen_idx[e][src] = i
                    self.rec[src].add(i)
                    continue
                lst = self.emitted[src]
                p = bisect.bisect_left(lst, i)
                sem, val = self.tokmap[src][lst[p]]
            else:
                sem, val, src = tok
            key = sem.num
            if self.seen[e].get(key, 0) >= val:
                continue
            if key not in need or need[key][1] < val:
                need[key] = (sem, val)
        for key, (sem, val) in need.items():
            self.eng[e].wait_ge(sem, val)
            self.seen[e][key] = val

    def _wait_all(self, e, toks):
        self._wait(e, toks)

    def _deps(self, reads, writes):
        toks = []
        for t in reads:
            toks.append(t.w)
        for t in writes:
            toks.append(t.w)
            toks.extend(t.r.values())
        return toks

    def op(self, e, fn, reads=(), writes=()):
        self._wait(e, self._deps(reads, writes))
        ins = fn(self.eng[e])
        self.idx[e] += 1
        i = self.idx[e]
        if self.needed is not None and i in self.needed[e]:
            if self.cnt[e] >= SEM_LIMIT:
                self._newsem(e)
            self.cnt[e] += 1
            ins.then_inc(self.sem[e], 1)
            self.tokmap[e][i] = (self.sem[e], self.cnt[e])
            self.emitted[e].append(i)
        tok = ("E", e, i)
        for t in reads:
            t.r[("E", e)] = tok
        for t in writes:
            t.w = tok
            t.r = {}
        return ins

    def dma(self, q, out_tk, out_ap, in_tk, in_ap, sbuf_side=None, **kw):
        self._wait(q, self._deps([in_tk], [out_tk]))
        st = sbuf_side if sbuf_side is not None else out_tk
        if st.dsem is None:
            pool_ = self.free_dsems_sw if q == "pool" else self.free_dsems
            st.dsem_sw = (q == "pool")
            if pool_:
                st.dsem = pool_.pop()
            else:
                st.dsem = self.nc.alloc_semaphore(name=f"d_{self.nsem}")
                self.nsem += 1
                self.dval[st.dsem.num] = 0
            self.dsem_tks.append(st)
        ins = self.eng[q].dma_start(out=out_ap, in_=in_ap, **kw)
        self.dval[st.dsem.num] += 16
        ins.then_inc(st.dsem, 16)
        tok = (st.dsem, self.dval[st.dsem.num], "dma")
        in_tk.r[("D", tok[0].num)] = tok
        out_tk.w = tok
        out_tk.r = {}
        self.dma_toks[tok[0].num] = tok
        return tok

    def barrier(self):
        toks = [("E", k, self.idx[k]) for k in self.eng if self.idx[k] > 0]
        toks += list(self.dma_toks.values())
        for e in self.eng:
            self._wait_all(e, toks)
        self.dma_toks = {}
        for t in self.dsem_tks:
            if self.dval[t.dsem.num] >= SEM_LIMIT:
                t.dsem = None
                continue
            (self.free_dsems_sw if getattr(t, "dsem_sw", False) else self.free_dsems).append(t.dsem)
            t.dsem = None
        self.dsem_tks = []


def build_program(TA, TB):
    nc1, kb1 = _build(TA, TB, None)
    needed = {k: v for k, v in kb1.rec.items()}
    nc2, kb2 = _build(TA, TB, needed)
    return nc2


def _build(TA, TB, needed):
    assert TA == 2 * TB and TB % 512 == 0
    nc = bass.Bass("TRN2", target_bir_lowering=False)
    kb = KB(nc, needed)
    op, dma = kb.op, kb.dma

    def din(name, shape, dt=F32):
        return kb.dram(name, shape, dt, kind="ExternalInput")

    xa = din("xa", [TA, D]); xb = din("xb", [TB, D])
    c3T = din("c3T", [128, KC, 4])
    w_ada = din("w_ada", [DEPTH, D, 6 * D]); b_adaT = din("b_adaT", [128, DEPTH, 6, KC])
    b_adaR = din("b_adaR", [4, DEPTH, 6 * D])
    npreT = din("npreT", [128, DEPTH, 2, KC])
    npostR = din("npostR", [4, DEPTH, 2, D])
    w_in = din("w_in", [DEPTH, D, NCOL])
    hlbT = din("hlbT", [128, 2, DEPTH, 4])
    gnT = din("gnT", [128, DEPTH]); swT = din("swT", [128, DEPTH])
    lamB = din("lamB", [128, 4, DEPTH, 64])
    w_out = din("w_out", [DEPTH, D, D])
    w_g = din("w_g", [DEPTH, D, HID]); w_u = din("w_u", [DEPTH, D, HID]); w_d = din("w_d", [DEPTH, HID, D])
    cosT = din("cosT", [128, TA]); sinT = din("sinT", [128, TA])
    flags = din("flags", [128, 2])
    ident_d = din("ident", [128, 128], BF16)
    onesf_d = din("onesf", [128, 128])
    onesb_d = din("onesb", [128, 128], BF16)
    sel_d = din("sel", [4, 3 * 128])
    rmask_d = din("rmask", [128, 2, 512])
    tmask_d = din("tmask", [128, 2, 128], U32)
    ya = kb.dram("ya", [TA, D], F32, kind="ExternalOutput")
    yb = kb.dram("yb", [TB, D], F32, kind="ExternalOutput")
    dk_ = "ExternalOutput" if DEBUG else "Internal"
    x1a = kb.dram("x1a", [TA, D], F32, kind=dk_); x1b = kb.dram("x1b", [TB, D], F32, kind=dk_)
    mixTa = kb.dram("mixTa", [KC, 128, TA], BF16, kind=dk_); mixTb = kb.dram("mixTb", [KC, 128, TB], BF16, kind=dk_)
    wo_s = kb.dram("wo_s", [128, KC, D], BF16)
    wg_s = kb.dram("wg_s", [JC, 128, KC, 128], BF16); wu_s = kb.dram("wu_s", [JC, 128, KC, 128], BF16)
    wd_s = kb.dram("wd_s", [JC, 128, D], BF16)

    with ExitStack() as g:
        ident = kb.sb(g, "ident", [128, 128], BF16)
        onesf = kb.sb(g, "onesf", [128, 128], F32)
        onesb = kb.sb(g, "onesb", [128, 128], BF16)
        sel = kb.sb(g, "sel", [4, 384], F32)
        rmask = kb.sb(g, "rmask", [128, 2, 512], F32)
        tmask = kb.sb(g, "tmask", [128, 2, 128], U32)
        flg = kb.sb(g, "flg", [128, 2], F32)
        zero1 = kb.sb(g, "zero1", [128, 1], F32)
        sc = kb.sb(g, "sc", [128, KC, 4], F32)
        modA = kb.sb(g, "modA", [128, 2, 3, KC], F32)
        modS = kb.sb(g, "modS", [128, 2, 3, KC], F32)
        Gbc = kb.sb(g, "Gbc", [128, 2, 3, D], F32)
        lbt = kb.sb(g, "lbt", [128, 2, DEPTH, 4], F32)
        omlt = kb.sb(g, "omlt", [128, 2, DEPTH, 4], F32)
        gnt = kb.sb(g, "gnt", [128, DEPTH], F32)
        swt = kb.sb(g, "swt", [128, DEPTH], F32)
        nlam = kb.sb(g, "nlam", [128, DEPTH], F32)
        for (t, d_) in ((ident, ident_d), (onesf, onesf_d), (onesb, onesb_d), (sel, sel_d), (rmask, rmask_d),
                        (tmask, tmask_d), (flg, flags), (lbt, hlbT), (gnt, gnT), (swt, swT)):
            dma("sp", t, t.ap, d_, d_.ap)
        op("dve", lambda e: e.memset(zero1.ap, 0.0), writes=[zero1])

        with ExitStack() as es:
            ct = kb.sb(es, "ct", [128, KC, 4], F32)
            t1 = kb.sb(es, "t1", [128, KC, 4], F32)
            dma("sp", ct, ct.ap, c3T, c3T.ap)
            op("act", lambda e: e.activation(out=t1.ap, in_=ct.ap, func=AF.Exp, scale=-1.0), [ct], [t1])
            op("dve", lambda e: e.tensor_scalar_add(out=t1.ap, in0=t1.ap, scalar1=1.0), [t1], [t1])
            op("dve", lambda e: e.reciprocal(out=t1.ap, in_=t1.ap), [t1], [t1])
            op("dve", lambda e: e.tensor_mul(out=sc.ap, in0=ct.ap, in1=t1.ap), [ct, t1], [sc])
            e0 = kb.sb(es, "e0", [128, 2, DEPTH, 4], F32)
            tot = kb.sb(es, "tot", [128, 2, 4], F32)
            op("act", lambda e: e.activation(out=e0.ap, in_=lbt.ap, func=AF.Exp), [lbt], [e0])
            op("dve", lambda e: e.tensor_add(out=tot.ap, in0=e0.ap[:, :, 0, :], in1=e0.ap[:, :, 1, :]), [e0], [tot])
            op("dve", lambda e: e.reciprocal(out=tot.ap, in_=tot.ap), [tot], [tot])
            for l in range(DEPTH):
                op("dve", lambda e, l=l: e.tensor_mul(out=e0.ap[:, :, l, :], in0=e0.ap[:, :, l, :], in1=tot.ap),
                   [e0, tot], [e0])
            op("dve", lambda e: e.tensor_sub(out=lbt.ap[:, :, 0, :], in0=e0.ap[:, :, 0, :], in1=e0.ap[:, :, 0, :]),
               [e0], [lbt])
            op("dve", lambda e: e.tensor_add(out=tot.ap, in0=e0.ap[:, :, 0, :], in1=e0.ap[:, :, 1, :]), [e0], [tot])
            op("dve", lambda e: e.tensor_sub(out=lbt.ap[:, :, 1, :], in0=tot.ap, in1=e0.ap[:, :, 0, :]),
               [tot, e0, lbt], [lbt])
            op("dve", lambda e: e.tensor_scalar(out=omlt.ap, in0=lbt.ap, scalar1=-1.0, scalar2=1.0,
                                                 op0=ALU.mult, op1=ALU.add), [lbt], [omlt])
            lm = kb.sb(es, "lm", [128, 4, DEPTH, 64], F32)
            pr = kb.sb(es, "pr", [128, 2, DEPTH, 64], F32)
            sm = kb.sb(es, "sm", [128, 2, DEPTH], F32)
            dma("sp", lm, lm.ap, lamB, lamB.ap)
            op("dve", lambda e: e.tensor_mul(out=pr.ap[:, 0], in0=lm.ap[:, 0], in1=lm.ap[:, 1]), [lm], [pr])
            op("dve", lambda e: e.tensor_mul(out=pr.ap[:, 1], in0=lm.ap[:, 2], in1=lm.ap[:, 3]), [lm, pr], [pr])
            op("dve", lambda e: e.reduce_sum(out=sm.ap, in_=pr.ap, axis=mybir.AxisListType.X), [pr], [sm])
            op("act", lambda e: e.activation(out=sm.ap, in_=sm.ap, func=AF.Exp), [sm], [sm])
            for l in range(DEPTH):
                lam_init = 0.8 - 0.6 * float(np.exp(-0.3 * l))
                op("dve", lambda e, l=l: e.tensor_sub(out=nlam.ap[:, l:l + 1], in0=sm.ap[:, 1, l:l + 1],
                                                      in1=sm.ap[:, 0, l:l + 1]), [sm, nlam], [nlam])
                op("dve", lambda e, l=l, li=lam_init: e.tensor_scalar_add(out=nlam.ap[:, l:l + 1],
                                                                          in0=nlam.ap[:, l:l + 1], scalar1=-li),
                   [nlam], [nlam])
                op("dve", lambda e, l=l, li=lam_init: e.tensor_scalar_mul(out=swt.ap[:, l:l + 1],
                                                                          in0=swt.ap[:, l:l + 1], scalar1=1.0 - li),
                   [swt], [swt])
            kb.barrier()

        segs = [(0, TA), (1, TB)]

        def rstd_from_ss(ss, n, eps):
            op("dve", lambda e: e.tensor_scalar(out=ss.ap, in0=ss.ap, scalar1=1.0 / n, scalar2=eps,
                                                 op0=ALU.mult, op1=ALU.add), [ss], [ss])
            op("act", lambda e: e.activation(out=ss.ap, in_=ss.ap, func=AF.Ln), [ss], [ss])
            op("act", lambda e: e.activation(out=ss.ap, in_=ss.ap, func=AF.Exp, scale=-0.5), [ss], [ss])

        for l in range(NLAYERS):
            xin = [xa, xb] if l == 0 else [x1a, x1b]
            xout = [x1a, x1b] if l == 0 else [ya, yb]
            if l == DEPTH - 1:
                xout = [ya, yb]
            mixTs = [mixTa, mixTb]
            with ExitStack() as es:
                wa = [kb.sb(es, "wa", [128, KC, D], F32) for _ in range(2)]
                badT = kb.sb(es, "badT", [128, 6, KC], F32)
                badR = kb.sb(es, "badR", [4, 6 * D], F32)
                npT = kb.sb(es, "npT", [128, 2, KC], F32)
                npR = kb.sb(es, "npR", [4, 2, D], F32)
                grow = kb.sb(es, "grow", [4, D], F32)
                pm = kb.ps(es, "pm", [128, KC, 4])
                pr_ = [kb.ps(es, "prw", [128, 512]) for _ in range(2)]
                dma("sp", badT, badT.ap, b_adaT, b_adaT.ap[:, l])
                dma("sp", badR, badR.ap, b_adaR, b_adaR.ap[:, l])
                dma("sp", npT, npT.ap, npreT, npreT.ap[:, l])
                dma("sp", npR, npR.ap, npostR, npostR.ap[:, l])
                for part in range(6):
                    w = wa[part % 2]
                    src = w_ada.ap[l, :, part * D:(part + 1) * D].rearrange("(k p) n -> p k n", p=128)
                    for k in range(KC):
                        dma("sp", w, w.ap[:, k, :], w_ada, src[:, k, :])
                    which = 0 if part < 3 else 1
                    kind = part % 3
                    if kind < 2:
                        for m in range(KC):
                            for k in range(KC):
                                op("pe", lambda e, m=m, k=k, w=w: e.matmul(
                                    pm.ap[:, m, :], lhsT=w.ap[:, k, m * 128:(m + 1) * 128], rhs=sc.ap[:, k, :],
                                    start=(k == 0), stop=(k == KC - 1)), [w, sc], [pm])
                        for s in range(3):
                            if kind == 0:
                                op("dve", lambda e, s=s, which=which, part=part: e.tensor_add(
                                    out=modS.ap[:, which, s, :], in0=pm.ap[:, :, s], in1=badT.ap[:, part, :]),
                                   [pm, badT, modS], [modS])
                            else:
                                op("dve", lambda e, s=s, which=which, part=part: e.tensor_add(
                                    out=modA.ap[:, which, s, :], in0=pm.ap[:, :, s], in1=badT.ap[:, part, :]),
                                   [pm, badT, modA], [modA])
                                op("dve", lambda e, s=s, which=which: e.tensor_scalar_add(
                                    out=modA.ap[:, which, s, :], in0=modA.ap[:, which, s, :], scalar1=1.0),
                                   [modA], [modA])
                                op("dve", lambda e, s=s, which=which: e.tensor_mul(
                                    out=modA.ap[:, which, s, :], in0=modA.ap[:, which, s, :],
                                    in1=npT.ap[:, which, :]), [modA, npT], [modA])
                    else:
                        for hf in range(2):
                            p_ = pr_[hf]
                            for k in range(KC):
                                op("pe", lambda e, k=k, hf=hf, p_=p_, w=w: e.matmul(
                                    p_.ap[0:4, :], lhsT=sc.ap[:, k, :], rhs=w.ap[:, k, hf * 512:(hf + 1) * 512],
                                    start=(k == 0), stop=(k == KC - 1)), [w, sc], [p_])
                            op("dve", lambda e, hf=hf, p_=p_, part=part: e.tensor_add(
                                out=grow.ap[:, hf * 512:(hf + 1) * 512], in0=p_.ap[0:4, :],
                                in1=badR.ap[:, part * D + hf * 512: part * D + (hf + 1) * 512]),
                               [p_, badR, grow], [grow])
                        op("dve", lambda e, which=which: e.tensor_mul(out=grow.ap, in0=grow.ap,
                                                                      in1=npR.ap[:, which, :]), [grow, npR], [grow])
                        for s in range(3):
                            for hf in range(2):
                                p_ = pr_[hf]
                                op("pe", lambda e, s=s, hf=hf, p_=p_: e.matmul(
                                    p_.ap, lhsT=sel.ap[:, s * 128:(s + 1) * 128],
                                    rhs=grow.ap[:, hf * 512:(hf + 1) * 512], start=True, stop=True),
                                   [sel, grow], [p_])
                                op("dve", lambda e, s=s, hf=hf, p_=p_, which=which: e.tensor_copy(
                                    out=Gbc.ap[:, which, s, hf * 512:(hf + 1) * 512], in_=p_.ap), [p_, Gbc], [Gbc])
                kb.barrier()
            cast_jobs = []
            for m in range(KC):
                cast_jobs.append((w_out, w_out.ap[l, :, m * 128:(m + 1) * 128].rearrange("(k p) n -> p k n", p=128),
                                  wo_s, wo_s.ap[:, :, m * 128:(m + 1) * 128]))
            for j in range(JC):
                cast_jobs.append((w_g, w_g.ap[l, :, j * 128:(j + 1) * 128].rearrange("(k p) n -> p k n", p=128),
                                  wg_s, wg_s.ap[j]))
                cast_jobs.append((w_u, w_u.ap[l, :, j * 128:(j + 1) * 128].rearrange("(k p) n -> p k n", p=128),
                                  wu_s, wu_s.ap[j]))
                cast_jobs.append((w_d, w_d.ap[l, j * 128:(j + 1) * 128, :].rearrange("p (k n) -> p k n", k=KC),
                                  wd_s, wd_s.ap[j].rearrange("p (k n) -> p k n", k=KC)))

            for (si, T) in segs:
                NT = T // 128
                NB = T // 512
                X = xin[si]
                XO = xout[si]
                MT = mixTs[si]

                def subseq(tok):
                    return (tok // TB) if si == 0 else 2

                with ExitStack() as segs_es:
                    hT = kb.sb(segs_es, "hT", [128, KC, T], BF16)
                    hT_blk = [TK(hT.ap, f"hTb{b}") for b in range(NB)]
                    with ExitStack() as es:
                        xt = [kb.sb(es, "xt", [128, D], F32) for _ in range(3)]
                        junk = kb.sb(es, "junk", [128, D], BF16)
                        xn = [kb.sb(es, "xn", [128, D], BF16) for _ in range(2)]
                        ss = [kb.sb(es, "ss", [128, 1], F32) for _ in range(2)]
                        ptr = [kb.ps(es, "ptr", [128, KC, 128], BF16) for _ in range(2)]
                        stg = [kb.sb(es, "stg", [128, KC, 128], F32) for _ in range(4)]
                        stb = [kb.sb(es, "stb", [128, KC, 128], BF16) for _ in range(4)]
                        ncast = [0]

                        def cast_some(k_):
                            for _ in range(k_):
                                if not cast_jobs:
                                    return
                                src_tk, src_ap, dst_tk, dst_ap = cast_jobs.pop(0)
                                i_ = ncast[0] % 4
                                ncast[0] += 1
                                dma("sp", stg[i_], stg[i_].ap, src_tk, src_ap)
                                op("pool", lambda e: e.tensor_copy(out=stb[i_].ap, in_=stg[i_].ap), [stg[i_]],
                                   [stb[i_]])
                                dma("pool", dst_tk, dst_ap, stb[i_], stb[i_].ap, sbuf_side=stb[i_])

                        per_tile = -(-len(cast_jobs) // NT) if si == 0 else 0
                        for it in range(NT):
                            cast_some(per_tile)
                            x_ = xt[it % 3]; s_ = ss[it % 2]; n_ = xn[it % 2]; p_ = ptr[it % 2]
                            hb = hT_blk[it // 4]
                            sq = subseq(it * 128)
                            dma("sp", x_, x_.ap, X, X.ap[it * 128:(it + 1) * 128, :])
                            op("act", lambda e: e.activation(out=junk.ap, in_=x_.ap, func=AF.Square,
                                                             accum_out=s_.ap), [x_], [junk, s_])
                            rstd_from_ss(s_, D, NORM_EPS)
                            op("dve", lambda e: e.tensor_scalar(out=n_.ap, in0=x_.ap, scalar1=s_.ap, scalar2=None,
                                                                 op0=ALU.mult), [x_, s_], [n_])
                            for k in range(KC):
                                op("pe", lambda e, k=k: e.transpose(p_.ap[:, k, :], n_.ap[:, k * 128:(k + 1) * 128],
                                                                    ident.ap), [n_, ident], [p_])
                            for k in range(KC):
                                op("act", lambda e, k=k: e.activation(
                                    out=hT.ap[:, k, it * 128:(it + 1) * 128], in_=p_.ap[:, k, :], func=AF.Identity,
                                    scale=modA.ap[:, 0, sq, k:k + 1], bias=modS.ap[:, 0, sq, k:k + 1]),
                                   [p_, modA, modS], [hb])
                        cast_some(len(cast_jobs))
                        kb.barrier()

                    def load_w(es_w, wst, wbf, col, n_i):
                        i = n_i % len(wst)
                        src = w_in.ap[l, :, col:col + 128].rearrange("(k p) n -> p k n", p=128)
                        dma("sp", wst[i], wst[i].ap, w_in, src)
                        wt = wbf[n_i % len(wbf)]
                        op("pool", lambda e: e.tensor_copy(out=wt.ap, in_=wst[i].ap), [wst[i]], [wt])
                        return wt

                    def proj_fm(pz, wt, b):
                        for k in range(KC):
                            op("pe", lambda e, k=k: e.matmul(pz.ap, lhsT=wt.ap[:, k, :],
                                                             rhs=hT.ap[:, k, b * 512:(b + 1) * 512],
                                                             start=(k == 0), stop=(k == KC - 1)),
                               [wt, hT_blk[b]], [pz])

                    def proj_tm(pz, wt, it):
                        for k in range(KC):
                            op("pe", lambda e, k=k: e.matmul(pz.ap, lhsT=hT.ap[:, k, it * 128:(it + 1) * 128],
                                                             rhs=wt.ap[:, k, :],
                                                             start=(k == 0), stop=(k == KC - 1)),
                               [wt, hT_blk[it // 4]], [pz])

                    with ExitStack() as es:
                        for h in range(4):
                            if h == 0:
                                NCH = T // 64
                                wst = [kb.sb(es, "wst", [128, KC, 128], F32) for _ in range(2)]
                                wbf = [kb.sb(es, "wbf", [128, KC, 128], BF16) for _ in range(5)]
                                qf = kb.sb(es, "qf", [128, T], F32)
                                qf_b = [TK(qf.ap, f"qfb{b}") for b in range(NB)]
                                gate = kb.sb(es, "gate", [128, T], BF16)
                                vtm = kb.sb(es, "vtm", [64, NCH, 128], BF16)
                                vtm_b = [TK(vtm.ap, f"vtb{b}") for b in range(NB)]
                                oacc = kb.sb(es, "oacc", [128, T], F32)
                                oacc_b = [TK(oacc.ap, f"oab{b}") for b in range(NB)]
                                tmp = [[kb.sb(es, "tmp", [128, 512], F32) for _ in range(4)] for _ in range(2)]
                                qt = [[kb.sb(es, "qt", [128, 512], BF16) for _ in range(2)] for _ in range(2)]
                                kt = [[kb.sb(es, "kt", [128, 512], BF16) for _ in range(2)] for _ in range(2)]
                                kh = [[kb.sb(es, "kh", [128, 512], BF16) for _ in range(2)] for _ in range(2)]
                                ktm = [[kb.sb(es, "ktm", [64, 128], BF16) for _ in range(3)] for _ in range(2)]
                                stm = [[kb.sb(es, "stm", [64, 64], BF16) for _ in range(3)] for _ in range(2)]
                                bref = [kb.sb(es, "bref", [128, 8], F32) for _ in range(2)]
                                d1 = [[kb.sb(es, "d1", [128, 8], F32) for _ in range(2)] for _ in range(2)]
                                d2 = [kb.sb(es, "d2", [128, 8], F32) for _ in range(2)]
                                d12 = [[kb.sb(es, "d12", [128, 8], F32) for _ in range(2)] for _ in range(2)]
                                S = [kb.sb(es, "S", [128, 128], F32) for _ in range(2)]
                                Sp = [[kb.sb(es, "Sp", [128, 128], BF16) for _ in range(2)] for _ in range(2)]
                                pz = [kb.ps(es, "pz", [128, 512]) for _ in range(2)]
                                pv = kb.ps(es, "pv", [128, 512])
                                pobank = [kb.ps(es, "pob", [128, 512]) for _ in range(2)]
                                po = [[TK(pobank[d_].ap[:, i * 64:(i + 1) * 64], f"po{d_}{i}") for i in range(8)]
                                      for d_ in range(2)]

                                def poslot(n_, dr):
                                    g_ = (n_ // 4) % 2
                                    k_ = (n_ % 4) if dr == 0 else 3 - (n_ % 4)
                                    return g_ * 4 + k_
                                msbank = [kb.ps(es, "msb", [128, 512]) for _ in range(2)]
                                pst = [[TK(msbank[d_].ap[0:64, i * 64:(i + 1) * 64], f"pst{d_}{i}") for i in range(2)]
                                       for d_ in range(2)]
                                pkv = [[TK(msbank[d_].ap[:, 128 + i * 128:256 + i * 128], f"pkv{d_}{i}")
                                        for i in range(2)] for d_ in range(2)]
                                pktb = kb.ps(es, "pktb", [128, 4, 128], BF16)
                                pkt = [[TK(pktb.ap[0:64, d_ * 2 + i, :], f"pkt{d_}{i}") for i in range(2)]
                                       for d_ in range(2)]
                            for d_ in range(2):
                                for t_ in stm[d_]:
                                    op("pool", lambda e, t_=t_: e.memset(t_.ap, 0.0), writes=[t_])
                                op("pool", lambda e, d_=d_: e.memset(S[d_].ap, 0.0), writes=[S[d_]])
                            op("pool", lambda e: e.memset(oacc.ap, 0.0), writes=[oacc] + oacc_b)
                            cols = [h * 128, 512 + h * 128, 1024 + h * 128, 1536 + h * 128, 2048 + h * 128]
                            wq = load_w(es, wst, wbf, cols[0], 0)
                            wff = load_w(es, wst, wbf, cols[1], 1)
                            wfb = load_w(es, wst, wbf, cols[2], 2)
                            wv = load_w(es, wst, wbf, cols[3], 3)
                            wg_ = load_w(es, wst, wbf, cols[4], 4)
                            cnt = [0]

                            def silu_ps(out_ap, out_tk, p_, tm):
                                a, b_, c_ = tm[0], tm[1], tm[2]
                                op("act", lambda e: e.activation(out=a.ap, in_=p_.ap, func=AF.Exp, scale=-1.0),
                                   [p_], [a])
                                op("act", lambda e: e.activation(out=b_.ap, in_=a.ap, func=AF.Ln, bias=1.0),
                                   [a], [b_])
                                op("act", lambda e: e.activation(out=c_.ap, in_=b_.ap, func=AF.Exp, scale=-1.0),
                                   [b_], [c_])
                                op("dve", lambda e: e.tensor_mul(out=out_ap, in0=p_.ap, in1=c_.ap), [p_, c_],
                                   [out_tk])

                            def gates(b, dr, wt, qi):
                                p_ = pz[cnt[0] % 2]; cnt[0] += 1
                                A, B, C, Dd = tmp[dr]
                                lb_ap = lbt.ap[:, dr, l, h:h + 1]
                                oml_ap = omlt.ap[:, dr, l, h:h + 1]
                                q_, k_, kh_ = qt[dr][qi], kt[dr][qi], kh[dr][qi]
                                D1, D12, D2, br = d1[dr][qi], d12[dr][qi], d2[dr], bref[dr]
                                proj_fm(p_, wt, b)
                                op("act", lambda e: e.activation(out=A.ap, in_=p_.ap, func=AF.Exp, scale=-1.0),
                                   [p_], [A])
                                op("act", lambda e: e.activation(out=B.ap, in_=A.ap, func=AF.Ln, scale=lb_ap,
                                                                 bias=1.0), [A, lbt], [B])
                                op("act", lambda e: e.activation(out=C.ap, in_=A.ap, func=AF.Ln, bias=1.0),
                                   [A], [C])
                                op("act", lambda e: e.activation(out=Dd.ap, in_=C.ap, func=AF.Exp, scale=-1.0),
                                   [C], [Dd])
                                yield
                                op("pool", lambda e: e.tensor_sub(out=B.ap, in0=B.ap, in1=C.ap), [B, C], [B])
                                op("dve", lambda e: e.scalar_tensor_tensor(out=A.ap, in0=A.ap, scalar=oml_ap,
                                                                           in1=Dd.ap, op0=ALU.mult, op1=ALU.mult),
                                   [A, Dd, omlt], [A])
                                if dr == 0:
                                    op("dve", lambda e: e.tensor_tensor_scan(out=C.ap, data0=rmask.ap[:, 0, :],
                                                                             data1=B.ap, initial=0.0,
                                                                             op0=ALU.mult, op1=ALU.add),
                                       [rmask, B], [C])
                                    ri, ei = 31, 63
                                else:
                                    op("dve", lambda e: e.tensor_tensor_scan(out=C.ap[:, ::-1],
                                                                             data0=rmask.ap[:, 1, ::-1],
                                                                             data1=B.ap[:, ::-1], initial=0.0,
                                                                             op0=ALU.mult, op1=ALU.add),
                                       [rmask, B], [C])
                                    ri, ei = 32, 0
                                yield
                                c3 = C.ap.rearrange("p (c t) -> p c t", t=64)
                                op("dve", lambda e: e.tensor_copy(out=br.ap, in_=c3[:, :, ri]), [C], [br])
                                op("dve", lambda e: e.tensor_sub(out=D2.ap, in0=c3[:, :, ei], in1=br.ap),
                                   [C, br], [D2])
                                op("act", lambda e: e.activation(out=D12.ap, in_=c3[:, :, ei], func=AF.Exp),
                                   [C], [D12])
                                op("act", lambda e: e.activation(out=D1.ap, in_=br.ap, func=AF.Exp), [br], [D1])
                                op("act", lambda e: e.activation(out=D2.ap, in_=D2.ap, func=AF.Exp), [D2], [D2])
                                yield
                                op("dve", lambda e: e.tensor_sub(out=c3, in0=c3,
                                                                 in1=br.ap.unsqueeze(2).to_broadcast([128, 8, 64])),
                                   [C, br], [C])
                                op("act", lambda e: e.activation(out=Dd.ap, in_=C.ap, func=AF.Exp), [C], [Dd])
                                op("act", lambda e: e.activation(out=B.ap, in_=C.ap, func=AF.Exp, scale=-1.0),
                                   [C], [B])
                                yield
                                op("dve", lambda e: e.tensor_mul(out=q_.ap, in0=qf.ap[:, b * 512:(b + 1) * 512],
                                                                 in1=Dd.ap), [qf_b[b], Dd], [q_])
                                op("pool", lambda e: e.tensor_mul(out=k_.ap, in0=A.ap, in1=B.ap), [A, B], [k_])
                                op("pool", lambda e: e.tensor_mul(
                                    out=kh_.ap.rearrange("p (c t) -> p c t", t=64),
                                    in0=k_.ap.rearrange("p (c t) -> p c t", t=64),
                                    in1=D2.ap.unsqueeze(2).to_broadcast([128, 8, 64])), [k_, D2], [kh_])

                            def cparams(i, j, dr):
                                b = i if dr == 0 else NB - 1 - i
                                ch = j if dr == 0 else 7 - j
                                n_ = i * 8 + j
                                return b, ch, n_, i % 2

                            def stageA(i, j, dr):
                                b, ch, n_, qi = cparams(i, j, dr)
                                o0 = ch * 64
                                kh_ = kh[dr][qi]
                                pk_ = pkt[dr][n_ % 2]; km = ktm[dr][n_ % 3]
                                op("pe", lambda e: e.transpose(pk_.ap, kh_.ap[:, o0:o0 + 64], ident.ap),
                                   [kh_, ident], [pk_])
                                op("act", lambda e: e.copy(out=km.ap, in_=pk_.ap), [pk_], [km])

                            def stageB(i, j, dr):
                                b, ch, n_, qi = cparams(i, j, dr)
                                gch = b * 8 + ch
                                o0 = ch * 64
                                q_, k_ = qt[dr][qi], kt[dr][qi]
                                km = ktm[dr][n_ % 3]; sm_ = stm[dr][n_ % 3]
                                ps_ = pst[dr][n_ % 2]; pkv_ = pkv[dr][n_ % 2]
                                vt_ = vtm_b[b]
                                op("pe", lambda e: e.matmul(pkv_.ap, lhsT=km.ap, rhs=vtm.ap[:, gch, :], start=True,
                                                            stop=True), [km, vt_], [pkv_])
                                if dr == 0:
                                    ra = (slice(0, 64), slice(32, 64)); rb = (slice(0, 32), slice(0, 32))
                                else:
                                    ra = (slice(0, 64), slice(0, 32)); rb = (slice(32, 64), slice(32, 64))
                                for (rs_, cs_) in (ra, rb):
                                    op("pe", lambda e, rs_=rs_, cs_=cs_: e.matmul(
                                        ps_.ap[rs_, cs_], lhsT=k_.ap[:, o0 + rs_.start:o0 + rs_.stop],
                                        rhs=q_.ap[:, o0 + cs_.start:o0 + cs_.stop], start=True, stop=True),
                                       [k_, q_], [ps_])
                                op("dve", lambda e: e.copy_predicated(out=sm_.ap, mask=tmask.ap[0:64, dr, 0:64],
                                                                      data=ps_.ap), [ps_, tmask, sm_], [sm_])

                            def stageC1(i, j, dr):
                                b, ch, n_, qi = cparams(i, j, dr)
                                gch = b * 8 + ch
                                sp_ = Sp[dr][n_ % 2]; S_ = S[dr]
                                if si == 0 and ((dr == 0 and gch == (TB // 64)) or
                                                (dr == 1 and gch == (TB // 64) - 1)):
                                    op("dve", lambda e: e.tensor_scalar(out=S_.ap, in0=S_.ap,
                                                                         scalar1=flg.ap[:, 1:2], scalar2=None,
                                                                         op0=ALU.mult), [S_, flg], [S_])
                                op("act", lambda e: e.activation(out=sp_.ap, in_=S_.ap, func=AF.Copy,
                                                                 scale=d1[dr][qi].ap[:, ch:ch + 1]),
                                   [S_, d1[dr][qi]], [sp_])

                            def stageC2(i, j, dr):
                                b, ch, n_, qi = cparams(i, j, dr)
                                gch = b * 8 + ch
                                o0 = ch * 64
                                q_ = qt[dr][qi]
                                sm_ = stm[dr][n_ % 3]; po_ = po[dr][poslot(n_, dr)]; vt_ = vtm_b[b]
                                op("pe", lambda e: e.matmul(po_.ap, lhsT=vtm.ap[:, gch, :], rhs=sm_.ap, start=True,
                                                            stop=False), [vt_, sm_], [po_])

                            def stageC3(i, j, dr):
                                b, ch, n_, qi = cparams(i, j, dr)
                                gch = b * 8 + ch
                                o0 = ch * 64
                                q_ = qt[dr][qi]
                                po_ = po[dr][poslot(n_, dr)]; sp_ = Sp[dr][n_ % 2]; S_ = S[dr]; pkv_ = pkv[dr][n_ % 2]
                                op("pe", lambda e: e.matmul(po_.ap, lhsT=sp_.ap, rhs=q_.ap[:, o0:o0 + 64],
                                                            start=False, stop=True), [sp_, q_], [po_])
                                op("dve", lambda e: e.scalar_tensor_tensor(out=S_.ap, in0=S_.ap,
                                                                           scalar=d12[dr][qi].ap[:, ch:ch + 1],
                                                                           in1=pkv_.ap, op0=ALU.mult, op1=ALU.add),
                                   [S_, d12[dr][qi], pkv_], [S_])
                                if n_ % 4 == 3:
                                    g_ = (n_ // 4) % 2
                                    t0 = (gch - 3) * 64 if dr == 0 else gch * 64
                                    ob = oacc_b[b]
                                    grp = po[dr][g_ * 4:g_ * 4 + 4]
                                    op("dve", lambda e: e.tensor_add(
                                        out=oacc.ap[:, t0:t0 + 256], in0=pobank[dr].ap[:, g_ * 256:(g_ + 1) * 256],
                                        in1=oacc.ap[:, t0:t0 + 256]), grp + [ob], [ob])

                            for b in range(NB):
                                p_ = pz[cnt[0] % 2]; cnt[0] += 1
                                proj_fm(p_, wq, b)
                                silu_ps(qf.ap[:, b * 512:(b + 1) * 512], qf_b[b], p_, tmp[0])
                                p_ = pz[cnt[0] % 2]; cnt[0] += 1
                                proj_fm(p_, wg_, b)
                                silu_ps(gate.ap[:, b * 512:(b + 1) * 512], gate, p_, tmp[1])
                                for c4 in range(2):
                                    for cq in range(4):
                                        cch = b * 8 + c4 * 4 + cq
                                        for k in range(KC):
                                            op("pe", lambda e, k=k: e.matmul(
                                                pv.ap[0:64, cq * 128:(cq + 1) * 128],
                                                lhsT=hT.ap[:, k, cch * 64:(cch + 1) * 64], rhs=wv.ap[:, k, :],
                                                start=(k == 0), stop=(k == KC - 1)), [wv, hT_blk[b]], [pv])
                                    c0_ = b * 8 + c4 * 4
                                    op("act", lambda e: e.copy(
                                        out=vtm.ap[:, c0_:c0_ + 4, :],
                                        in_=pv.ap[0:64, :].rearrange("p (c n) -> p c n", n=128)), [pv], [vtm_b[b]])
                            for d_ in range(2):
                                for t_ in pst[d_]:
                                    op("dve", lambda e, t_=t_: e.memset(t_.ap, 0.0), writes=[t_])
                            steps = [(i, j) for i in range(NB) for j in range(8)]
                            NS = len(steps)
                            for n in range(NS + 2):
                                if n == 0:
                                    for _ in gates(0, 0, wff, 0):
                                        pass
                                    for _ in gates(NB - 1, 1, wfb, 0):
                                        pass
                                    ggen = []
                                if n < NS and steps[n][1] == 1 and steps[n][0] + 1 < NB:
                                    i = steps[n][0] + 1
                                    ggen = [gates(i, 0, wff, i % 2), gates(NB - 1 - i, 1, wfb, i % 2)]
                                if n < NS and 1 <= steps[n][1] <= 5:
                                    for gg in ggen:
                                        next(gg, None)
                                if n < NS and steps[n][1] == 6:
                                    for gg in ggen:
                                        for _ in gg:
                                            pass
                                    ggen = []
                                if n - 2 >= 0:
                                    for dr in range(2):
                                        stageC1(*steps[n - 2], dr)
                                if n - 1 >= 0 and n - 1 < NS:
                                    for dr in range(2):
                                        stageB(*steps[n - 1], dr)
                                if n - 2 >= 0:
                                    for dr in range(2):
                                        stageC2(*steps[n - 2], dr)
                                if n < NS:
                                    for dr in range(2):
                                        stageA(*steps[n], dr)
                                if n - 2 >= 0:
                                    for dr in range(2):
                                        stageC3(*steps[n - 2], dr)
                            with ExitStack() as es2:
                                sqt = [tmp[0][0], tmp[1][0]]
                                fin = [qt[0][0], qt[0][1]]
                                for b in range(NB):
                                    ob = oacc_b[b]
                                    sq_ = sqt[b % 2]
                                    osl = oacc.ap[:, b * 512:(b + 1) * 512]
                                    op("act", lambda e: e.activation(out=sq_.ap, in_=osl, func=AF.Square), [ob],
                                       [sq_])
                                    p_ = pz[cnt[0] % 2]; cnt[0] += 1
                                    op("pe", lambda e: e.matmul(p_.ap, lhsT=onesf.ap, rhs=sq_.ap, start=True,
                                                                stop=True), [onesf, sq_], [p_])
                                    op("act", lambda e: e.activation(out=sq_.ap, in_=p_.ap, func=AF.Ln,
                                                                     scale=1.0 / 128, bias=NORM_EPS), [p_], [sq_])
                                    op("act", lambda e: e.activation(out=sq_.ap, in_=sq_.ap, func=AF.Exp,
                                                                     scale=-0.5), [sq_], [sq_])
                                    op("dve", lambda e: e.scalar_tensor_tensor(out=sq_.ap, in0=osl,
                                                                               scalar=gnt.ap[:, l:l + 1], in1=sq_.ap,
                                                                               op0=ALU.mult, op1=ALU.mult),
                                       [ob, gnt, sq_], [sq_])
                                    f_ = fin[b % 2]
                                    op("pool", lambda e: e.tensor_mul(out=f_.ap, in0=sq_.ap,
                                                                      in1=gate.ap[:, b * 512:(b + 1) * 512]),
                                       [sq_, gate], [f_])
                                    dma("pool", MT, MT.ap[h, :, b * 512:(b + 1) * 512], f_, f_.ap, sbuf_side=f_)

                        kb.barrier()
                    with ExitStack() as es:
                        wst = [kb.sb(es, "wst", [128, KC, 128], F32) for _ in range(2)]
                        wbf = [kb.sb(es, "wbf", [128, KC, 128], BF16) for _ in range(5)]
                        QT0 = kb.sb(es, "QT0", [128, T], BF16)
                        QT1 = kb.sb(es, "QT1", [128, T], BF16)
                        QTs = [QT0, QT1]
                        KT = kb.sb(es, "KT", [128, T], BF16)
                        op("pool", lambda e: e.memset(QT0.ap, 0.0), writes=[QT0])
                        op("pool", lambda e: e.memset(QT1.ap, 0.0), writes=[QT1])
                        Vt = kb.sb(es, "Vt", [128, NT, 128], BF16)
                        cs = kb.sb(es, "cs", [128, T], F32)
                        sn = kb.sb(es, "sn", [128, T], F32)
                        dma("sp", cs, cs.ap, cosT, cosT.ap[:, 0:T])
                        dma("sp", sn, sn.ap, sinT, sinT.ap[:, 0:T])

                        def load_d(hh):
                            return (load_w(es, wst, wbf, 2560 + hh * 128, 0), load_w(es, wst, wbf, 4096 + hh * 128, 1),
                                    load_w(es, wst, wbf, 3072 + hh * 128, 2), load_w(es, wst, wbf, 4608 + hh * 128, 3),
                                    load_w(es, wst, wbf, 3584 + hh * 128, 4))

                        Wd = load_d(0)
                        for h in range(4):
                            wq_, wqs, wk_, wks, wv_ = Wd
                            with ExitStack() as es2:
                                pz = [kb.ps(es2, "pz", [128, 512]) for _ in range(4)]
                                pv = kb.ps(es2, "pv", [128, 128])
                                ta = [kb.sb(es2, "ta", [128, 512], F32) for _ in range(2)]
                                tb = [kb.sb(es2, "tb", [128, 512], F32) for _ in range(2)]
                                n2 = 0
                                for b in range(NB):
                                    for (wa_, ws_, dst) in ((wq_, wqs, None), (wk_, wks, KT)):
                                        pa = pz[(2 * n2) % 4]; pb = pz[(2 * n2 + 1) % 4]
                                        t1_ = ta[n2 % 2]; t2_ = tb[n2 % 2]; n2 += 1
                                        proj_fm(pa, wa_, b)
                                        proj_fm(pb, ws_, b)
                                        op("dve", lambda e: e.tensor_mul(out=t1_.ap, in0=pa.ap,
                                                                         in1=cs.ap[:, b * 512:(b + 1) * 512]),
                                           [pa, cs], [t1_])
                                        op("dve", lambda e: e.tensor_mul(out=t2_.ap, in0=pb.ap,
                                                                         in1=sn.ap[:, b * 512:(b + 1) * 512]),
                                           [pb, sn], [t2_])
                                        if dst is None:
                                            for c_ in range(2):
                                                rs_ = slice(c_ * 64, (c_ + 1) * 64)
                                                op("pool", lambda e, c_=c_, rs_=rs_: e.tensor_add(
                                                    out=QTs[c_].ap[rs_, b * 512:(b + 1) * 512],
                                                    in0=t1_.ap[rs_, :], in1=t2_.ap[rs_, :]), [t1_, t2_], [QTs[c_]])
                                        else:
                                            op("pool", lambda e: e.tensor_add(
                                                out=dst.ap[:, b * 512:(b + 1) * 512],
                                                in0=t1_.ap, in1=t2_.ap), [t1_, t2_], [dst])
                                    for it in range(b * 4, b * 4 + 4):
                                        proj_tm(pv, wv_, it)
                                        op("act", lambda e, it=it: e.copy(out=Vt.ap[:, it, :], in_=pv.ap), [pv],
                                           [Vt])
                                kb.barrier()
                            if h + 1 < 4:
                                Wd = load_d(h + 1)
                            with ExitStack() as es2:
                                pS = [kb.ps(es2, "pS", [128, 512]) for _ in range(3)]
                                pO = [kb.ps(es2, "pO", [128, 512]) for _ in range(2)]
                                pZ = [kb.ps(es2, "pZ", [128, 512]) for _ in range(2)]
                                pN = kb.ps(es2, "pN", [128, 512])
                                Pm = [kb.sb(es2, "Pm", [128, 512], BF16) for _ in range(5)]
                                r1 = kb.sb(es2, "r1", [128, 512], F32)
                                r2 = kb.sb(es2, "r2", [128, 512], F32)
                                oo = [kb.sb(es2, "oo", [128, 512], F32) for _ in range(2)]
                                sq2 = [kb.sb(es2, "sq2", [128, 512], F32) for _ in range(2)]
                                fin = [kb.sb(es2, "fin", [128, 512], BF16) for _ in range(2)]
                                LAG = 2
                                tiles = [(qb, kt_, c) for qb in range(NB) for kt_ in range(NT) for c in range(2)]

                                def stage1(n):
                                    qb, kt_, c = tiles[n]
                                    qsl = slice(qb * 512, (qb + 1) * 512)
                                    ksl = slice(kt_ * 128, (kt_ + 1) * 128)
                                    cross = (si == 0) and ((qb * 512) // TB != (kt_ * 128) // TB)
                                    bias_ap = flg.ap[:, 0:1] if cross else zero1.ap
                                    ps_ = pS[n % 3]; pm_ = Pm[n % 5]
                                    rs = slice(c * 64, (c + 1) * 64)
                                    op("pe", lambda e: e.matmul(ps_.ap, lhsT=KT.ap[:, ksl], rhs=QTs[c].ap[:, qsl],
                                                                start=True, stop=True), [KT, QTs[c]], [ps_])
                                    op("act", lambda e: e.activation(out=pm_.ap, in_=ps_.ap, func=AF.Exp,
                                                                     scale=0.125, bias=bias_ap),
                                       [ps_, flg, zero1], [pm_])

                                def stage2(n):
                                    qb, kt_, c = tiles[n]
                                    pm_ = Pm[n % 5]
                                    op("pe", lambda e: e.matmul(pO[c].ap, lhsT=Vt.ap[:, kt_, :], rhs=pm_.ap,
                                                                start=(kt_ == 0), stop=(kt_ == NT - 1)),
                                       [Vt, pm_], [pO[c]])
                                    op("pe", lambda e: e.matmul(pZ[c].ap, lhsT=onesb.ap, rhs=pm_.ap,
                                                                start=(kt_ == 0), stop=(kt_ == NT - 1)),
                                       [onesb, pm_], [pZ[c]])

                                def fin1(qb):
                                    o_ = oo[qb % 2]; s2_ = sq2[qb % 2]
                                    op("dve", lambda e: e.reciprocal(out=r1.ap, in_=pZ[0].ap), [pZ[0]], [r1])
                                    op("dve", lambda e: e.reciprocal(out=r2.ap, in_=pZ[1].ap), [pZ[1]], [r2])
                                    op("dve", lambda e: e.tensor_mul(out=r1.ap, in0=pO[0].ap, in1=r1.ap),
                                       [pO[0], r1], [r1])
                                    op("dve", lambda e: e.tensor_mul(out=r2.ap, in0=pO[1].ap, in1=r2.ap),
                                       [pO[1], r2], [r2])
                                    op("dve", lambda e: e.scalar_tensor_tensor(out=o_.ap, in0=r2.ap,
                                                                               scalar=nlam.ap[:, l:l + 1], in1=r1.ap,
                                                                               op0=ALU.mult, op1=ALU.add),
                                       [r1, r2, nlam], [o_])
                                    op("dve", lambda e: e.tensor_mul(out=s2_.ap, in0=o_.ap, in1=o_.ap), [o_], [s2_])

                                def fin2(qb):
                                    o_ = oo[qb % 2]; s2_ = sq2[qb % 2]
                                    qsl = slice(qb * 512, (qb + 1) * 512)
                                    op("pe", lambda e: e.matmul(pN.ap, lhsT=onesf.ap, rhs=s2_.ap, start=True,
                                                                stop=True), [onesf, s2_], [pN])
                                    op("act", lambda e: e.activation(out=s2_.ap, in_=pN.ap, func=AF.Ln,
                                                                     scale=1.0 / 128, bias=SUBLN_EPS), [pN], [s2_])
                                    op("act", lambda e: e.activation(out=s2_.ap, in_=s2_.ap, func=AF.Exp,
                                                                     scale=-0.5), [s2_], [s2_])
                                    f_ = fin[qb % 2]
                                    op("dve", lambda e: e.scalar_tensor_tensor(out=f_.ap, in0=o_.ap,
                                                                               scalar=swt.ap[:, l:l + 1],
                                                                               in1=s2_.ap, op0=ALU.mult,
                                                                               op1=ALU.mult), [o_, swt, s2_], [f_])
                                    dma("pool", MT, MT.ap[4 + h, :, qsl], f_, f_.ap, sbuf_side=f_)

                                NTI = len(tiles)
                                per_qb = NT * 2
                                pend = {}
                                for n in range(NTI + LAG):
                                    if n < NTI:
                                        stage1(n)
                                    m = n - LAG
                                    if m >= 0:
                                        stage2(m)
                                        if (m + 1) % per_qb == 0:
                                            qb_done = m // per_qb
                                            fin1(qb_done)
                                            pend[n + 4] = qb_done
                                    if n in pend:
                                        fin2(pend.pop(n))
                                for k_ in sorted(pend):
                                    fin2(pend[k_])
                                kb.barrier()
                    kb.barrier()

                with ExitStack() as es:
                    wo = kb.sb(es, "wo", [128, KC, D], BF16)
                    wd = kb.sb(es, "wd", [128, JC, D], BF16)
                    wgr = [kb.sb(es, "wgr", [128, KC, 128], BF16) for _ in range(3)]
                    wur = [kb.sb(es, "wur", [128, KC, 128], BF16) for _ in range(3)]
                    xt = [kb.sb(es, "xt", [128, D], F32) for _ in range(2)]
                    xm = [kb.sb(es, "xm", [128, D], F32) for _ in range(8)]
                    mxb = [kb.sb(es, "mxb", [128, KC, 512], BF16) for _ in range(1)]
                    h2T = [kb.sb(es, "h2T", [128, KC, 512], BF16) for _ in range(2)]
                    hid = kb.sb(es, "hid", [128, JC, 512], BF16)
                    hid_j = [TK(hid.ap, f"hid{j}") for j in range(JC)]
                    tf = kb.sb(es, "tf", [128, D], F32)
                    xn = [kb.sb(es, "xn", [128, D], BF16) for _ in range(2)]
                    ss = [kb.sb(es, "ss", [128, 1], F32) for _ in range(6)]
                    ea = [kb.sb(es, "ea", [128, 512], F32) for _ in range(2)]
                    eb = [kb.sb(es, "eb", [128, 512], F32) for _ in range(2)]
                    ptok = [kb.ps(es, "ptok", [128, D]) for _ in range(2)]
                    ptr = kb.ps(es, "ptr", [128, KC, 128], BF16)
                    pgu = [kb.ps(es, "pgu", [128, 512]) for _ in range(3)]
                    dma("sp", wo, wo.ap, wo_s, wo_s.ap)
                    st_ = {"nss": 0, "ngu": 0, "nt": 0, "nx": 0}

                    def pre_load(b):
                        mb = mxb[0]
                        for k in range(KC):
                            dma("sp", mb, mb.ap[:, k, :], MT, MT.ap[k, :, b * 512:(b + 1) * 512])

                    def pre_tile(b, tt):
                        mb = mxb[0]; h2 = h2T[b % 2]
                        sq = subseq(b * 512)
                        it = b * 4 + tt
                        x_ = xt[st_["nx"] % 2]; n_ = xn[st_["nx"] % 2]; st_["nx"] += 1
                        pt = ptok[st_["nt"] % 2]; st_["nt"] += 1
                        s1 = ss[st_["nss"] % 6]; s2 = ss[(st_["nss"] + 1) % 6]; st_["nss"] += 2
                        xm_ = xm[(b % 2) * 4 + tt]
                        dma("sp", x_, x_.ap, X, X.ap[it * 128:(it + 1) * 128, :])
                        for hf in range(2):
                            for k in range(KC):
                                op("pe", lambda e, k=k, hf=hf: e.matmul(
                                    pt.ap[:, hf * 512:(hf + 1) * 512], lhsT=mb.ap[:, k, tt * 128:(tt + 1) * 128],
                                    rhs=wo.ap[:, k, hf * 512:(hf + 1) * 512], start=(k == 0),
                                    stop=(k == KC - 1)), [mb, wo], [pt])
                        yield
                        op("act", lambda e: e.activation(out=n_.ap, in_=pt.ap, func=AF.Square,
                                                         accum_out=s1.ap), [pt], [n_, s1])
                        rstd_from_ss(s1, D, NORM_EPS)
                        yield
                        op("dve", lambda e: e.scalar_tensor_tensor(out=xm_.ap, in0=pt.ap, scalar=s1.ap,
                                                                   in1=Gbc.ap[:, 0, sq, :], op0=ALU.mult,
                                                                   op1=ALU.mult), [pt, s1, Gbc], [xm_])
                        op("pool", lambda e: e.tensor_add(out=xm_.ap, in0=xm_.ap, in1=x_.ap), [xm_, x_], [xm_])
                        yield
                        op("act", lambda e: e.activation(out=n_.ap, in_=xm_.ap, func=AF.Square,
                                                         accum_out=s2.ap), [xm_], [n_, s2])
                        rstd_from_ss(s2, D, NORM_EPS)
                        yield
                        op("dve", lambda e: e.tensor_scalar(out=n_.ap, in0=xm_.ap, scalar1=s2.ap, scalar2=None,
                                                             op0=ALU.mult), [xm_, s2], [n_])
                        yield
                        pre_tile2(b, tt, n_)

                    def pre_tile2(b, tt, n_):
                        h2 = h2T[b % 2]
                        sq = subseq(b * 512)
                        for k in range(KC):
                            op("pe", lambda e, k=k: e.transpose(ptr.ap[:, k, :], n_.ap[:, k * 128:(k + 1) * 128],
                                                                ident.ap), [n_, ident], [ptr])
                        for k in range(KC):
                            op("act", lambda e, k=k: e.activation(
                                out=h2.ap[:, k, tt * 128:(tt + 1) * 128], in_=ptr.ap[:, k, :], func=AF.Identity,
                                scale=modA.ap[:, 1, sq, k:k + 1], bias=modS.ap[:, 1, sq, k:k + 1]),
                               [ptr, modA, modS], [h2])

                    def ffn_j(b, j):
                        h2 = h2T[b % 2]
                        wg_t = wgr[j % 3]; wu_t = wur[j % 3]
                        dma("sp", wg_t, wg_t.ap, wg_s, wg_s.ap[j])
                        dma("sp", wu_t, wu_t.ap, wu_s, wu_s.ap[j])
                        pg = pgu[st_["ngu"] % 3]; pu = pgu[(st_["ngu"] + 1) % 3]; st_["ngu"] += 2
                        a_ = ea[j % 2]; b_ = eb[j % 2]
                        for k in range(KC):
                            op("pe", lambda e, k=k: e.matmul(pg.ap, lhsT=wg_t.ap[:, k, :], rhs=h2.ap[:, k, :],
                                                             start=(k == 0), stop=(k == KC - 1)), [wg_t, h2], [pg])
                        for k in range(KC):
                            op("pe", lambda e, k=k: e.matmul(pu.ap, lhsT=wu_t.ap[:, k, :], rhs=h2.ap[:, k, :],
                                                             start=(k == 0), stop=(k == KC - 1)), [wu_t, h2], [pu])
                        op("act", lambda e: e.activation(out=a_.ap, in_=pg.ap, func=AF.Exp, scale=-1.0), [pg], [a_])
                        op("act", lambda e: e.activation(out=b_.ap, in_=a_.ap, func=AF.Ln, bias=1.0), [a_], [b_])
                        op("act", lambda e: e.activation(out=a_.ap, in_=b_.ap, func=AF.Exp, scale=-1.0), [b_], [a_])
                        op("dve", lambda e: e.tensor_mul(out=b_.ap, in0=pg.ap, in1=a_.ap), [pg, a_], [b_])
                        op("dve", lambda e: e.tensor_mul(out=hid.ap[:, j, :], in0=pu.ap, in1=b_.ap),
                           [pu, b_], [hid_j[j]])

                    def down_tile(b, tt):
                        sq = subseq(b * 512)
                        it = b * 4 + tt
                        pt = ptok[st_["nt"] % 2]; st_["nt"] += 1
                        s1 = ss[st_["nss"] % 6]; st_["nss"] += 1
                        xm_ = xm[(b % 2) * 4 + tt]
                        for hf in range(2):
                            for j in range(JC):
                                op("pe", lambda e, j=j, hf=hf: e.matmul(
                                    pt.ap[:, hf * 512:(hf + 1) * 512], lhsT=hid.ap[:, j, tt * 128:(tt + 1) * 128],
                                    rhs=wd.ap[:, j, hf * 512:(hf + 1) * 512], start=(j == 0),
                                    stop=(j == JC - 1)), [hid_j[j], wd], [pt])
                        op("act", lambda e: e.activation(out=tf.ap, in_=pt.ap, func=AF.Square,
                                                         accum_out=s1.ap), [pt], [tf, s1])
                        rstd_from_ss(s1, D, NORM_EPS)
                        op("dve", lambda e: e.scalar_tensor_tensor(out=tf.ap, in0=pt.ap, scalar=s1.ap,
                                                                   in1=Gbc.ap[:, 1, sq, :], op0=ALU.mult,
                                                                   op1=ALU.mult), [pt, s1, Gbc], [tf])
                        op("pool", lambda e: e.tensor_add(out=xm_.ap, in0=tf.ap, in1=xm_.ap), [tf, xm_], [xm_])
                        dma("pool", XO, XO.ap[it * 128:(it + 1) * 128, :], xm_, xm_.ap, sbuf_side=xm_)

                    pre_load(0)
                    for tt in range(4):
                        for _ in pre_tile(0, tt):
                            pass
                    for j in range(JC):
                        dma("sp", wd, wd.ap[:, j, :], wd_s, wd_s.ap[j])
                    for b in range(NB):
                        if b + 1 < NB:
                            pre_load(b + 1)
                        gens = []
                        for j in range(JC):
                            ffn_j(b, j)
                            if b + 1 < NB and j % 5 == 0 and j // 5 < 4:
                                gens.append(pre_tile(b + 1, j // 5))
                            for gg in gens:
                                next(gg, None)
                        for gg in gens:
                            for _ in gg:
                                pass
                        for tt in range(4):
                            down_tile(b, tt)
                    kb.barrier()
        kb.barrier()
    return nc, kb


_PROG = {}


def _rot_tables(TA, restart):
    pos = np.arange(TA, dtype=np.float32)
    if restart:
        pos = np.where(np.arange(TA) >= TA // 2, pos - np.float32(TA // 2), pos).astype(np.float32)
    inv = (1.0 / (np.float32(10000.0) ** (np.arange(0, 64, 2, dtype=np.float32) / np.float32(64)))).astype(np.float32)
    ang = (pos[None, :] * inv[:, None]).astype(np.float32)
    c = np.cos(ang).astype(np.float32)
    s = np.sin(ang).astype(np.float32)
    cosT = np.zeros((128, TA), np.float32)
    sinT = np.zeros((128, TA), np.float32)
    for p in range(128):
        j = p % 64
        i = j % 32
        cosT[p] = c[i]
        sinT[p] = -s[i] if j < 32 else s[i]
    return cosT, sinT


def kernel(x_prompt, x_sample, c_prompt, c_sample, w_ada, b_ada, norm_pre_mix, norm_post_mix,
           norm_pre_ffn, norm_post_ffn, w_in, hg_lower_bounds, hg_gnorm, da_lambda_q1,
           da_lambda_k1, da_lambda_q2, da_lambda_k2, da_subln, w_out, w_ffn_gate, w_ffn_up,
           w_ffn_down):
    f32 = np.float32
    x_prompt = np.asarray(x_prompt, f32); x_sample = np.asarray(x_sample, f32)
    c_prompt = np.asarray(c_prompt, f32); c_sample = np.asarray(c_sample, f32)
    TA = x_prompt.shape[1]; TB = x_sample.shape[1]
    assert x_prompt.shape[0] == 4 and x_sample.shape[0] == 16
    key = (TA, TB)
    if key not in _PROG:
        _PROG[key] = build_program(TA, TB)
    nc = _PROG[key]
    w_in = np.asarray(w_in, f32)
    perm = np.arange(512).reshape(8, 2, 32)[:, ::-1, :].reshape(512)
    w_in_ext = np.ascontiguousarray(np.concatenate(
        [w_in, w_in[:, :, 2560 + perm], w_in[:, :, 3072 + perm]], axis=2))
    b_ada = np.asarray(b_ada, f32)
    b_adaT = np.ascontiguousarray(b_ada.reshape(DEPTH, 6, KC, 128).transpose(3, 0, 1, 2))
    b_adaR = np.ascontiguousarray(np.broadcast_to(b_ada[None], (4, DEPTH, 6 * D)))
    npre = np.stack([np.asarray(norm_pre_mix, f32), np.asarray(norm_pre_ffn, f32)], axis=1)
    npreT = np.ascontiguousarray(npre.reshape(DEPTH, 2, KC, 128).transpose(3, 0, 1, 2))
    npost = np.stack([np.asarray(norm_post_mix, f32), np.asarray(norm_post_ffn, f32)], axis=1)
    npostR = np.ascontiguousarray(np.broadcast_to(npost[None], (4, DEPTH, 2, D)))
    hlb = np.asarray(hg_lower_bounds, f32)
    hlbT = np.ascontiguousarray(hlb.reshape(2, DEPTH, 4, 128).transpose(3, 0, 1, 2))
    gnT = np.ascontiguousarray(np.asarray(hg_gnorm, f32).T)
    swT = np.ascontiguousarray(np.asarray(da_subln, f32).T)
    lam = np.stack([np.asarray(a, f32) for a in (da_lambda_q1, da_lambda_k1, da_lambda_q2, da_lambda_k2)], axis=0)
    lamB = np.ascontiguousarray(np.broadcast_to(lam[None], (128, 4, DEPTH, 64)))
    ident = np.eye(128, dtype=f32).astype(ml_dtypes.bfloat16)
    onesf = np.ones((128, 128), f32)
    onesb = np.ones((128, 128), f32).astype(ml_dtypes.bfloat16)
    sel = np.zeros((4, 384), f32)
    for s in range(3):
        sel[s, s * 128:(s + 1) * 128] = 1.0
    t = np.arange(512)
    rmask = np.ones((128, 2, 512), f32)
    rmask[:, 0, t % 64 == 0] = 0.0
    rmask[:, 1, t % 64 == 63] = 0.0
    s_ = np.arange(128)[:, None]; t_ = np.arange(128)[None, :]
    same = (s_ // 64) == (t_ // 64)
    tmask = np.zeros((128, 2, 128), np.uint32)
    tmask[:, 0, :] = (same & (s_ <= t_)).astype(np.uint32)
    tmask[:, 1, :] = (same & (s_ >= t_)).astype(np.uint32)
    tabs = [_rot_tables(TA, False), _rot_tables(TA, True)]
    common = dict(w_ada=np.asarray(w_ada, f32), b_adaT=b_adaT, b_adaR=b_adaR, npreT=npreT, npostR=npostR,
                  w_in=w_in_ext, hlbT=hlbT, gnT=gnT, swT=swT, lamB=lamB, w_out=np.asarray(w_out, f32),
                  w_g=np.asarray(w_ffn_gate, f32), w_u=np.asarray(w_ffn_up, f32), w_d=np.asarray(w_ffn_down, f32),
                  ident=ident, onesf=onesf, onesb=onesb, sel=sel, rmask=rmask, tmask=tmask)
    in_maps = []
    for c in range(8):
        if c < 4:
            xa = x_prompt[c]; xb = x_sample[c]
            cs = [c_prompt[c], c_prompt[c], c_sample[c]]
            fl = (0.0, 1.0); tb = tabs[0]
        else:
            k = 4 + 3 * (c - 4)
            xa = np.concatenate([x_sample[k], x_sample[k + 1]], axis=0); xb = x_sample[k + 2]
            cs = [c_sample[k], c_sample[k + 1], c_sample[k + 2]]
            fl = (NEG_BIG, 0.0); tb = tabs[1]
        c3 = np.zeros((4, D), f32)
        c3[:3] = np.stack(cs)
        c3T = np.ascontiguousarray(c3.reshape(4, KC, 128).transpose(2, 1, 0))
        flags = np.zeros((128, 2), f32); flags[:, 0] = fl[0]; flags[:, 1] = fl[1]
        m = dict(common)
        m.update(xa=np.ascontiguousarray(xa), xb=np.ascontiguousarray(xb), c3T=c3T, cosT=tb[0], sinT=tb[1],
                 flags=flags)
        in_maps.append(m)
    res = run_bass_kernel_spmd(nc, in_maps, core_ids=list(range(8)))
    if DEBUG:
        global _DBG
        _DBG = res.results
    y_prompt = np.zeros_like(x_prompt); y_sample = np.zeros_like(x_sample)
    for c in range(8):
        r = res.results[c]
        if c < 4:
            y_prompt[c] = r["ya"]; y_sample[c] = r["yb"]
        else:
            k = 4 + 3 * (c - 4)
            y_sample[k] = r["ya"][:TB]; y_sample[k + 1] = r["ya"][TB:]; y_sample[k + 2] = r["yb"]
    return (y_prompt, y_sample)
```
